# Optimizing a Trainium2 kernel written in Bass

```python
import jax, jax.numpy as jnp
from jax import lax
import numpy as np

D_MODEL = 1024
BATCH = 2
SEQ = 8192
DEPTH = 4

GLA_HEADS = 4
GLA_DK = 128
GLA_DV = 128
GLA_QK = GLA_HEADS * GLA_DK
GLA_V = GLA_HEADS * GLA_DV
GLA_GATE_RANK = 16
GLA_GATE_NORM = 16.0
GLA_CHUNK = 64
GMLP_GROUPS = 4
GMLP_GROUP_CH = 128
GMLP_WIDTH = GMLP_GROUPS * GMLP_GROUP_CH
GMLP_CHUNK = 128
MOBA_HEADS = 4
MOBA_HD = 128
MOBA_W = MOBA_HEADS * MOBA_HD
MOBA_BLOCK = 256
MOBA_TOPK = 3
MOBA_QBLOCK = 64
ROPE_THETA = 500000.0
ROPE_DIMS = MOBA_HD // 4
N_BRANCH = 3
BRANCH_W = 512
D_FF = 4 * D_MODEL
IN_COLS = 2 * GLA_QK + 2 * GLA_V + GLA_GATE_RANK + 2 * GMLP_WIDTH + 3 * MOBA_W + N_BRANCH * D_MODEL
DEEPNORM_ALPHA = (2 * DEPTH) ** 0.25
DEEPNORM_BETA = (8 * DEPTH) ** -0.25
LN_EPS = 1e-5
RMS_EPS = 1e-6

kernel_name = 'hybrid_gla_gmlp_moba_deepnorm'


def layer_norm(x, g, b):
    xf = x.astype(jnp.float32)
    mu = jnp.mean(xf, axis=-1, keepdims=True)
    xc = xf - mu
    var = jnp.mean(xc * xc, axis=-1, keepdims=True)
    return (xc * lax.rsqrt(var + LN_EPS) * g + b).astype(x.dtype)


def partial_rope(t, positions):
    half = ROPE_DIMS // 2
    inv = 1.0 / (ROPE_THETA ** (jnp.arange(half, dtype=jnp.float32) * (2.0 / ROPE_DIMS)))
    ang = positions.astype(jnp.float32)[:, :, None] * inv
    cos = jnp.cos(ang)[:, :, None, :]
    sin = jnp.sin(ang)[:, :, None, :]
    tr = t[..., :ROPE_DIMS].astype(jnp.float32)
    x1, x2 = tr[..., :half], tr[..., half:]
    rot = jnp.concatenate([x1 * cos - x2 * sin, x2 * cos + x1 * sin], axis=-1)
    return jnp.concatenate([rot.astype(t.dtype), t[..., ROPE_DIMS:]], axis=-1)


def gla_branch(q, k, v, g, lr, w_gate_up, b_gate, norm_w):
    B, S, _ = q.shape
    H, DK, DV, C = GLA_HEADS, GLA_DK, GLA_DV, GLA_CHUNK
    nC = S // C
    f32 = jnp.float32

    def heads(t, d):
        return t.astype(f32).reshape(B, nC, C, H, d).transpose(0, 3, 1, 2, 4)

    log_a = jax.nn.log_sigmoid((lr @ w_gate_up + b_gate).astype(f32)) / GLA_GATE_NORM
    bcum = jnp.cumsum(heads(log_a, DK), axis=3)
    qh = heads(q, DK) * (DK ** -0.5)
    kh = heads(k, DK)
    vh = heads(v, DV)
    q_dec = qh * jnp.exp(bcum)
    k_inv = kh * jnp.exp(-bcum)
    causal = jnp.tril(jnp.ones((C, C), dtype=bool))
    attn = jnp.where(causal, jnp.einsum('bhntd,bhnsd->bhnts', q_dec, k_inv), 0.0)
    o_intra = jnp.einsum('bhnts,bhnsv->bhntv', attn, vh)
    b_end = bcum[:, :, :, -1:, :]
    kv_chunk = jnp.einsum('bhnsd,bhnsv->bhndv', kh * jnp.exp(b_end - bcum), vh)
    decay = jnp.exp(b_end[:, :, :, 0, :])

    def step(state, inp):
        kv_n, dec_n = inp
        return dec_n[..., None] * state + kv_n, state

    _, states = lax.scan(step, jnp.zeros((B, H, DK, DV), f32),
                         (jnp.moveaxis(kv_chunk, 2, 0), jnp.moveaxis(decay, 2, 0)))
    states = jnp.moveaxis(states, 0, 2)
    o = o_intra + jnp.einsum('bhntd,bhndv->bhntv', q_dec, states)
    o = o.transpose(0, 2, 3, 1, 4).reshape(B, S, H, DV)
    o = o * lax.rsqrt(jnp.mean(o * o, axis=-1, keepdims=True) + RMS_EPS) * norm_w
    o = o.reshape(B, S, H * DV) * jax.nn.silu(g.astype(f32))
    return o.astype(q.dtype)


def gmlp_branch(z, ln_g, ln_b, w_s, b_s):
    B, S, _ = z.shape
    G, CH, C = GMLP_GROUPS, GMLP_GROUP_CH, GMLP_CHUNK
    z = jax.nn.gelu(z, approximate=False)
    u, v = jnp.split(z, 2, axis=-1)
    v = layer_norm(v, ln_g, ln_b).reshape(B, S // C, C, G, CH)
    w = w_s * jnp.tril(jnp.ones((C, C), dtype=w_s.dtype))
    vs = jnp.einsum('gts,bnsgc->bntgc', w, v) + b_s.T[None, None, :, :, None]
    return u * vs.reshape(B, S, GMLP_WIDTH)


def moba_branch(q, k, v, positions):
    B, S, _ = q.shape
    H, HD, BLK, QB = MOBA_HEADS, MOBA_HD, MOBA_BLOCK, MOBA_QBLOCK
    f32 = jnp.float32
    qh = partial_rope(q.reshape(B, S, H, HD), positions).transpose(0, 2, 1, 3)
    kh = partial_rope(k.reshape(B, S, H, HD), positions).transpose(0, 2, 1, 3)
    vh = v.reshape(B, S, H, HD).transpose(0, 2, 1, 3)
    n_blk = -(-S // BLK)
    s_pad = n_blk * BLK
    pad = ((0, 0), (0, 0), (0, s_pad - S), (0, 0))
    kh = jnp.pad(kh, pad)
    vh = jnp.pad(vh, pad)
    k_blocks = kh.reshape(B, H, n_blk, BLK, HD)
    v_blocks = vh.reshape(B, H, n_blk, BLK, HD)
    k_mean = jnp.mean(k_blocks.astype(f32), axis=3)
    own_blk = jnp.arange(S) // BLK
    fully_past = jnp.arange(n_blk)[None, :] < own_blk[:, None]
    gate = jnp.where(fully_past, jnp.einsum('bhsd,bhnd->bhsn', qh.astype(f32), k_mean), -jnp.inf)
    n_sel = min(MOBA_TOPK, n_blk)
    _, sel = lax.top_k(gate, n_sel)
    nQ = S // QB
    q_c = jnp.moveaxis(qh.reshape(B, H, nQ, QB, HD), 2, 0)
    sel_c = jnp.moveaxis(sel.reshape(B, H, nQ, QB, n_sel), 2, 0)
    starts = jnp.arange(nQ, dtype=jnp.int32) * QB
    gather = jax.vmap(jax.vmap(lambda blocks, ids: blocks[ids]))
    scale = HD ** -0.5

    def attend(args):
        qb, sb, start = args
        t = start + jnp.arange(QB)
        blk_start = (start // BLK) * BLK
        kg = gather(k_blocks, sb)
        vg = gather(v_blocks, sb)
        s_g = jnp.einsum('bhqd,bhqjkd->bhqjk', qb, kg).astype(f32) * scale
        valid = jnp.arange(n_sel)[None, :] < (t // BLK)[:, None]
        s_g = jnp.where(valid[:, :, None], s_g, -jnp.inf).reshape(B, H, QB, n_sel * BLK)
        k_own = lax.dynamic_slice_in_dim(kh, blk_start, BLK, axis=2)
        v_own = lax.dynamic_slice_in_dim(vh, blk_start, BLK, axis=2)
        s_o = jnp.einsum('bhqd,bhkd->bhqk', qb, k_own).astype(f32) * scale
        s_o = jnp.where((blk_start + jnp.arange(BLK))[None, :] <= t[:, None], s_o, -jnp.inf)
        p = jax.nn.softmax(jnp.concatenate([s_g, s_o], axis=-1), axis=-1).astype(vh.dtype)
        p_g = p[..., :n_sel * BLK].reshape(B, H, QB, n_sel, BLK)
        return (jnp.einsum('bhqjk,bhqjkd->bhqd', p_g, vg)
                + jnp.einsum('bhqk,bhkd->bhqd', p[..., n_sel * BLK:], v_own))

    out = lax.map(attend, (q_c, sel_c, starts))
    out = jnp.moveaxis(out, 0, 2).reshape(B, H, S, HD)
    return out.transpose(0, 2, 1, 3).reshape(B, S, MOBA_W)


def mixer_sublayer(x, positions, w_in, w_gate_up, b_gate, gla_norm_w, gmlp_ln_g, gmlp_ln_b,
                   gmlp_w_s, gmlp_b_s, w_branch, w_out):
    sizes = [GLA_QK, GLA_QK, GLA_V, GLA_V, GLA_GATE_RANK, 2 * GMLP_WIDTH,
             MOBA_W, MOBA_W, MOBA_W, N_BRANCH * D_MODEL]
    offsets = [int(o) for o in np.cumsum(sizes)[:-1]]
    proj = x @ w_in
    gq, gk, gv, gg, glr, gz, mq, mk, mv, gl = jnp.split(proj, offsets, axis=-1)
    a = gla_branch(gq, gk, gv, gg, glr, w_gate_up, b_gate, gla_norm_w) @ w_branch[0]
    b = gmlp_branch(gz, gmlp_ln_g, gmlp_ln_b, gmlp_w_s, gmlp_b_s) @ w_branch[1]
    c = moba_branch(mq, mk, mv, positions) @ w_branch[2]
    ga, gb, gc = jnp.split(jax.nn.sigmoid(gl), N_BRANCH, axis=-1)
    return (ga * a + gb * b + gc * c) @ w_out


def setup_inputs(seed: int = 0) -> dict:
    key = jax.random.key(seed)
    ks = jax.random.split(key, 20)
    L, D = DEPTH, D_MODEL
    nrm = lambda k, shape, s: jax.random.normal(k, shape, jnp.float32) * s
    offset = jax.random.randint(ks[1], (BATCH, 1), 0, 1024, dtype=jnp.int32)
    return {
        'x': nrm(ks[0], (BATCH, SEQ, D), 1.0),
        'positions': offset + jnp.arange(SEQ, dtype=jnp.int32)[None, :],
        'w_in': nrm(ks[2], (L, D, IN_COLS), D ** -0.5),
        'w_gate_up': nrm(ks[3], (L, GLA_GATE_RANK, GLA_QK), GLA_GATE_RANK ** -0.5),
        'b_gate': nrm(ks[4], (L, GLA_QK), 0.1),
        'gla_norm_w': 1.0 + nrm(ks[5], (L, GLA_DV), 0.02),
        'gmlp_ln_g': 1.0 + nrm(ks[6], (L, GMLP_WIDTH), 0.02),
        'gmlp_ln_b': nrm(ks[7], (L, GMLP_WIDTH), 0.02),
        'gmlp_w_s': nrm(ks[8], (L, GMLP_GROUPS, GMLP_CHUNK, GMLP_CHUNK), 0.5 * GMLP_CHUNK ** -0.5),
        'gmlp_b_s': 1.0 + nrm(ks[9], (L, GMLP_GROUPS, GMLP_CHUNK), 0.1),
        'w_branch': nrm(ks[10], (L, N_BRANCH, BRANCH_W, D), BRANCH_W ** -0.5),
        'w_out': nrm(ks[11], (L, D, D), DEEPNORM_BETA * D ** -0.5),
        'ln1_g': 1.0 + nrm(ks[12], (L, D), 0.02),
        'ln1_b': nrm(ks[13], (L, D), 0.02),
        'w_ff1': nrm(ks[14], (L, D, D_FF), D ** -0.5),
        'w_ff2': nrm(ks[15], (L, D_FF, D), DEEPNORM_BETA * D_FF ** -0.5),
        'ln2_g': 1.0 + nrm(ks[16], (L, D), 0.02),
        'ln2_b': nrm(ks[17], (L, D), 0.02),
    }


def reference(x, positions, w_in, w_gate_up, b_gate, gla_norm_w, gmlp_ln_g, gmlp_ln_b, gmlp_w_s,
              gmlp_b_s, w_branch, w_out, ln1_g, ln1_b, w_ff1, w_ff2, ln2_g, ln2_b):
    for l in range(DEPTH):
        mix = mixer_sublayer(x, positions, w_in[l], w_gate_up[l], b_gate[l], gla_norm_w[l],
                             gmlp_ln_g[l], gmlp_ln_b[l], gmlp_w_s[l], gmlp_b_s[l], w_branch[l], w_out[l])
        x = layer_norm(DEEPNORM_ALPHA * x + mix, ln1_g[l], ln1_b[l])
        ff = jnp.square(jax.nn.relu(x @ w_ff1[l])) @ w_ff2[l]
        x = layer_norm(DEEPNORM_ALPHA * x + ff, ln2_g[l], ln2_b[l])
    return x
```

```python
from contextlib import ExitStack
import math
import numpy as np
import ml_dtypes
import concourse.bass as bass
import concourse.mybir as mybir
from concourse.bass_utils import run_bass_kernel_spmd

F32 = mybir.dt.float32
BF16 = mybir.dt.bfloat16
I32 = mybir.dt.int32
AF = mybir.ActivationFunctionType
ALU = mybir.AluOpType
AX = mybir.AxisListType

PAGE = 512
SB_BASE = 16896
SBUF_BYTES = 229376 - 512
PHASE_W = 4000
NDMASEM = 12
ENGS = ("pe", "act", "dve", "pool", "sp")


class T:
    def __init__(self, ap, pages, name):
        self.ap = ap
        self.pages = pages
        self.name = name

    def __getitem__(self, idx):
        return self.ap[idx]


class Sched:
    def __init__(self, nc):
        self.nc = nc
        self.ops = {e: [] for e in ENGS}
        self.count = {e: 0 for e in ENGS}
        self.dma_n = {e: 0 for e in ENGS}
        self.dma_semcnt = {}
        self.cc_n = 0
        self.cc_cnt = {}
        self.known = {e: {} for e in ENGS}
        self.wr = {}
        self.rd = {}
        self.sb_off = SB_BASE
        self.sb_stack = []
        self.memo = {}
        self.sb_peak = 0
        self.ntens = 0
        self.psum_banks = []
        self.stack = ExitStack()

    def sb(self, shape, dtype, name=None):
        esz = {F32: 4, BF16: 2, I32: 4}[dtype]
        nbytes = int(np.prod(shape[1:])) * esz
        off = (self.sb_off + PAGE - 1) // PAGE * PAGE
        assert off + nbytes <= SBUF_BYTES, f"SBUF overflow {name} {off}+{nbytes}"
        self.sb_off = off + nbytes
        self.sb_peak = max(self.sb_peak, self.sb_off)
        mkey = (off, tuple(shape), str(dtype))
        if mkey in self.memo:
            return self.memo[mkey]
        self.ntens += 1
        nm = f"{name or 't'}_{self.ntens}"
        h = self.nc.alloc_sbuf_tensor_at(nm, list(shape), dtype, offset=off)
        pages = range(off // PAGE, (off + nbytes + PAGE - 1) // PAGE)
        t = T(h.ap(), [("sb", p) for p in pages], nm)
        self.memo[mkey] = t
        return t

    def push(self):
        self.sb_stack.append(self.sb_off)

    def pop(self):
        self.sb_off = self.sb_stack.pop()

    def alloc_psum(self):
        for i in range(8):
            h = self.stack.enter_context(self.nc.psum_tensor(f"psb{i}", [128, 512], F32))
            self.psum_banks.append(T(h.ap(), [("ps", i)], f"psb{i}"))
        self.ps_rr = 0

    def bank(self, lo=0, hi=8):
        b = lo + self.ps_rr % (hi - lo)
        self.ps_rr += 1
        return self.psum_banks[b]

    def _deps(self, reads, writes):
        deps = []
        for t in reads:
            for p in t.pages:
                w = self.wr.get(p)
                if w is not None:
                    deps.append((w, "raw"))
        for t in writes:
            for p in t.pages:
                w = self.wr.get(p)
                if w is not None:
                    deps.append((w, "waw"))
                for r in self.rd.get(p, {}).values():
                    deps.append((r, "war"))
        return deps

    def _emit_waits(self, eng, deps):
        need = {}
        for tok, kind in deps:
            if tok[0] == "E":
                _, e2, idx = tok
                if e2 == eng and eng == "pe":
                    continue
                key = ("E", e2)
                val = idx
            elif tok[0] == "C":
                _, slot, val = tok
                key = ("C", slot)
            else:
                _, e2, slot, val = tok
                key = ("D", e2, slot)
            if self.known[eng].get(key, -1) >= val:
                continue
            if need.get(key, -1) < val:
                need[key] = val
        for key, val in need.items():
            self.known[eng][key] = val
            self.ops[eng].append(("wait", key, val))

    def _mark(self, tok, key, reads, writes):
        for t in reads:
            for p in t.pages:
                self.rd.setdefault(p, {})[key] = tok
        for t in writes:
            for p in t.pages:
                self.wr[p] = tok
                self.rd[p] = {}

    def op(self, eng, fn, reads=(), writes=()):
        pr = [t for t in reads if t.pages[0][0] == "ps" and t not in writes]
        if pr:
            writes = list(writes) + pr
        deps = self._deps(reads, writes)
        self._emit_waits(eng, deps)
        idx = self.count[eng]
        self.count[eng] += 1
        self.ops[eng].append(("op", fn, idx))
        self._mark(("E", eng, idx), ("E", eng), reads, writes)

    def dma(self, eng, fn, reads=(), writes=()):
        deps = self._deps(reads, writes)
        n = self.dma_n[eng]
        self.dma_n[eng] += 1
        slot = n % NDMASEM
        prev = self.dma_semcnt.get((eng, slot), 0)
        if prev > 0:
            deps.append((("D", eng, slot, prev), "raw"))
        self._emit_waits(eng, deps)
        cnt = prev + 1
        self.dma_semcnt[(eng, slot)] = cnt
        self.ops[eng].append(("dma", fn, slot))
        tok = ("D", eng, slot, cnt)
        self._mark(tok, ("D", eng, slot, cnt), reads, writes)

    def cc(self, fn, reads=(), writes=()):
        eng = "pool"
        deps = self._deps(reads, writes)
        slot = self.cc_n % 4
        self.cc_n += 1
        prev = self.cc_cnt.get(slot, 0)
        if prev > 0:
            deps.append((("C", slot, prev), "raw"))
        self._emit_waits(eng, deps)
        cnt = prev + 1
        self.cc_cnt[slot] = cnt
        self.ops[eng].append(("cc", fn, slot))
        tok = ("C", slot, cnt)
        self._mark(tok, tok, reads, writes)

    def wait_all(self, eng):
        deps = []
        for slot, cnt in self.cc_cnt.items():
            deps.append((("C", slot, cnt), "raw"))
        for (e2, slot), cnt in self.dma_semcnt.items():
            deps.append((("D", e2, slot, cnt), "raw"))
        for e2 in ENGS:
            if e2 != eng and self.count[e2] > 0:
                deps.append((("E", e2, self.count[e2] - 1), "raw"))
        self._emit_waits(eng, deps)

    def emit(self):
        nc = self.nc
        st = self.stack
        esems = {}
        for e in ENGS:
            nph = (self.count[e] + PHASE_W - 1) // PHASE_W
            esems[e] = [st.enter_context(nc.semaphore(f"s_{e}_{k}")) for k in range(nph)]
        dsems = {}
        for (e, slot) in self.dma_semcnt:
            dsems[(e, slot)] = st.enter_context(nc.semaphore(f"d_{e}_{slot}"))
        csems = {slot: st.enter_context(nc.semaphore(f"c_{slot}")) for slot in self.cc_cnt}

        def run(e, eng):
            for rec in self.ops[e]:
                if rec[0] == "wait":
                    _, key, val = rec
                    if key[0] == "E":
                        ph, loc = divmod(val, PHASE_W)
                        eng.wait_ge(esems[key[1]][ph], loc + 1)
                    elif key[0] == "C":
                        eng.wait_ge(csems[key[1]], val)
                    else:
                        eng.wait_ge(dsems[(key[1], key[2])], 16 * val)
                elif rec[0] == "op":
                    _, fn, idx = rec
                    fn(eng).then_inc(esems[e][idx // PHASE_W], 1)
                elif rec[0] == "cc":
                    _, fn, slot = rec
                    fn(eng).then_inc(csems[slot], 1)
                else:
                    _, fn, slot = rec
                    fn(eng).then_inc(dsems[(e, slot)], 16)

        with nc.Block() as block:
            @block.tensor
            def _(eng):
                run("pe", eng)

            @block.scalar
            def _(eng):
                run("act", eng)

            @block.vector
            def _(eng):
                run("dve", eng)

            @block.gpsimd
            def _(eng):
                run("pool", eng)

            @block.sync
            def _(eng):
                run("sp", eng)
        st.close()


D = 1024
SEQ = 8192
BATCH = 2
DEPTH = 4
NCORE = 8
TOK = 2048
NG = 4
IN_COLS = 7696
C_GQ, C_GK, C_GV, C_GG, C_LR, C_GZ, C_MQ, C_MK, C_MV, C_GL = 0, 512, 1024, 1536, 2048, 2064, 3088, 3600, 4112, 4624
ALPHA = (2 * DEPTH) ** 0.25
NEG = -30000.0
TWO_PI = 2.0 * math.pi


def build(nlayers=DEPTH, dbg=None, stop=None):
    nc = bass.Bass("TRN2", target_bir_lowering=False)
    s = Sched(nc)
    s.alloc_psum()
    NL = nlayers

    def din(name, shape, dt=F32):
        return nc.dram_tensor(name, list(shape), dt, kind="ExternalInput").ap()

    def dout(name, shape, dt=F32):
        return nc.dram_tensor(name, list(shape), dt, kind="ExternalOutput").ap()

    def dscr(name, shape, dt=F32):
        return nc.dram_tensor(name, list(shape), dt).ap()

    dr_n = [0]

    def dT(ap, name):
        dr_n[0] += 1
        return T(ap, [("dr", dr_n[0])], name)

    x_in = din("x", [TOK, D])
    posi_d = din("posi", [128, 16], I32)
    invf_d = din("invf", [128, 64])
    w_in_all = din("w_in", [NL, D, IN_COLS])
    wgu_all = din("wgu", [NL, 17, 512])
    ident_d = din("ident", [128, 128])
    tri_d = din("tri", [128, 128])
    gmask_d = din("gmask", [128, 128])
    lomask_d = din("lomask", [128, 512], BF16)
    himask_d = din("himask", [128, 512], BF16)
    cmask_d = din("cmask", [128, 2, 256], BF16)
    identb_d = din("identb", [128, 128], BF16)
    valid_d = din("valid", [128, 16, 32])
    negfill_d = din("negfill", [128, 16, 32])
    normw_all = din("normw", [NL, 128, 512])
    glng_all = din("glng", [NL, 128, 512])
    glnb_all = din("glnb", [NL, 128, 512])
    wsT_all = din("wsT", [NL, 128, 4, 128])
    triu_d = din("triu", [128, 128])
    bst_all = din("bst", [NL, 128, 4])
    wbr_all = din("wbr", [NL, 3, 512, D])
    wout_all = din("wout", [NL, D, D])
    ln_all = [din(n, [NL, 128, D]) for n in ("ln1g", "ln1b", "ln2g", "ln2b")]
    wff1_all = din("wff1", [NL, D, 4 * D])
    wff2_all = din("wff2", [NL, 4 * D, D])
    cinc_d = din("cinc", [128, 4])
    bmask_d = din("bmask", [128, 4, 4])
    out_d = dout("xout", [TOK, D])
    xbuf = [dscr(f"xbuf{i}", [TOK, D]) for i in range(2)]
    CCF = 32 + 512 + 4
    ccb_in = [[dscr(f"ccb_in{i}_{j}", [128, 4096], BF16) for j in range(4)] for i in range(2)]
    ccb_out = [[dscr(f"ccb_out{i}_{j}", [512, 4096], BF16) for j in range(4)] for i in range(2)]
    ccf_in = [dscr(f"ccf_in{i}", [128, CCF]) for i in range(2)]
    ccf_out = [dscr(f"ccf_out{i}", [512, CCF]) for i in range(2)]
    ccb_in_t = [[dT(a, "ccb_in") for a in row] for row in ccb_in]
    ccb_out_t = [[dT(a, "ccb_out") for a in row] for row in ccb_out]
    ccf_in_t = [dT(a, "ccf_in") for a in ccf_in]
    ccf_out_t = [dT(a, "ccf_out") for a in ccf_out]
    glaE_d = [dscr(f"glaE{g_}", [128, 2048]) for g_ in range(NG)]
    glaK_d = [dscr(f"glaK{g_}", [128, 2048], BF16) for g_ in range(NG)]
    glaV_d = [dscr(f"glaV{g_}", [128, 2048], BF16) for g_ in range(NG)]
    glaE_t = [dT(a, "glaE") for a in glaE_d]
    glaK_t = [dT(a, "glaK") for a in glaK_d]
    glaV_t = [dT(a, "glaV") for a in glaV_d]
    xsrc_t = {}
    for i_ in range(2):
        for g_ in range(NG):
            for t_ in range(4):
                xsrc_t[(i_, g_, t_)] = dT(xbuf[i_], f"xbuf{i_}_{g_}_{t_}")
    dbg_d = {}
    if dbg:
        for nm, shp in dbg.items():
            dbg_d[nm] = dout("dbg_" + nm, shp)

    def dbg_store(nm, t, ap):
        if nm in dbg_d:
            s.dma("sp", lambda e: e.dma_start(out=dbg_d[nm], in_=ap), reads=[t])

    def load(eng, t, ap_out, ap_in):
        s.dma(eng, lambda e: e.dma_start(out=ap_out, in_=ap_in), writes=[t])

    def const(d_ap, shape, dt=F32, name=None, eng="sp"):
        t = s.sb(shape, dt, name)
        load(eng, t, t[:], d_ap)
        return t

    def mm(out_t, out_ap, l_t, l_ap, r_t, r_ap, start=True, stop=True):
        s.op("pe", lambda e: e.matmul(out_ap, l_ap, r_ap, start=start, stop=stop),
             reads=[l_t, r_t], writes=[out_t])

    def tr(out_t, out_ap, in_t, in_ap, idt):
        s.op("pe", lambda e: e.transpose(out_ap, in_ap, idt[:]), reads=[in_t, idt], writes=[out_t])

    def act(out_t, out_ap, in_t, in_ap, func, bias=0.0, scale=1.0, accum=None, extra_r=()):
        w = [out_t] + ([accum[0]] if accum else [])
        kw = {}
        if accum:
            kw["accum_out"] = accum[1]
        s.op("act", lambda e: e.activation(out_ap, in_ap, func, bias=bias, scale=scale, **kw),
             reads=[in_t] + list(extra_r), writes=w)

    def tt(eng, out_t, out_ap, a_t, a_ap, b_t, b_ap, op):
        s.op(eng, lambda e: e.tensor_tensor(out_ap, a_ap, b_ap, op), reads=[a_t, b_t], writes=[out_t])

    def ts(eng, out_t, out_ap, a_t, a_ap, s1, s2, op0, op1=None, extra_r=()):
        if op1 is None:
            s.op(eng, lambda e: e.tensor_scalar(out_ap, a_ap, s1, None, op0),
                 reads=[a_t] + list(extra_r), writes=[out_t])
        else:
            s.op(eng, lambda e: e.tensor_scalar(out_ap, a_ap, s1, s2, op0, op1),
                 reads=[a_t] + list(extra_r), writes=[out_t])

    def stt(eng, out_t, out_ap, a_t, a_ap, sc, b_t, b_ap, op0, op1, extra_r=()):
        s.op(eng, lambda e: e.scalar_tensor_tensor(out_ap, a_ap, sc, b_ap, op0, op1),
             reads=[a_t, b_t] + list(extra_r), writes=[out_t])

    def cp(eng, out_t, out_ap, in_t, in_ap):
        if eng == "act":
            s.op("act", lambda e: e.copy(out_ap, in_ap), reads=[in_t], writes=[out_t])
        else:
            s.op(eng, lambda e: e.tensor_copy(out_ap, in_ap), reads=[in_t], writes=[out_t])

    def rot(shape, dt, name, n):
        tiles = [s.sb(shape, dt, f"{name}{i}") for i in range(n)]
        k = [0]

        def nxt():
            t = tiles[k[0] % n]
            k[0] += 1
            return t
        return nxt

    def pipeline(stages, n=4, between=None):
        ns = len(stages)
        for step in range(n + ns - 1):
            for k in range(ns):
                i = step - k
                if 0 <= i < n:
                    stages[k](i)
            if between is not None:
                between(step)

    cp_rr = [0]

    def cp_any(out_t, out_ap, in_t, in_ap):
        eng = ("act", "dve")[cp_rr[0] % 2]
        cp_rr[0] += 1
        cp(eng, out_t, out_ap, in_t, in_ap)

    ident = const(ident_d, [128, 128], name="ident")
    tri = const(tri_d, [128, 128], name="tri")
    posi = const(posi_d, [128, 16], I32, name="posi")
    invf = const(invf_d, [128, 64], name="invf")
    gmask = const(gmask_d, [128, 128], name="gmask")
    lomask = const(lomask_d, [128, 512], BF16, name="lomask")
    himask = const(himask_d, [128, 512], BF16, name="himask")
    cmask = const(cmask_d, [128, 2, 256], BF16, name="cmask")
    identb = const(identb_d, [128, 128], BF16, name="identb")
    valid = const(valid_d, [128, 16, 32], name="valid")
    negfill = const(negfill_d, [128, 16, 32], name="negfill")
    triu = const(triu_d, [128, 128], name="triu")
    cinc = const(cinc_d, [128, 4], name="cinc")
    bmask = const(bmask_d, [128, 4, 4], name="bmask")
    wgu = s.sb([17, 512], F32, "wgu")
    normw = s.sb([128, 512], F32, "normw")
    glng = s.sb([128, 512], F32, "glng")
    glnb = s.sb([128, 512], F32, "glnb")
    bst = s.sb([128, 4], F32, "bst")
    lnp = [s.sb([128, D], F32, f"ln{i}") for i in range(4)]
    kmT = s.sb([128, 4, 32], BF16, "kmT")
    wsTm = s.sb([128, 4, 128], BF16, "wsTm")

    def load_layer_params(l):
        s.push()
        wsT = s.sb([128, 4, 128], F32, "wsT")
        load("sp", wgu, wgu[:], wgu_all[l])
        load("sp", normw, normw[:], normw_all[l])
        load("sp", glng, glng[:], glng_all[l])
        load("sp", glnb, glnb[:], glnb_all[l])
        load("sp", bst, bst[:], bst_all[l])
        for i in range(4):
            load("sp", lnp[i], lnp[i][:], ln_all[i][l])
        load("sp", wsT, wsT[:], wsT_all[l])
        for g in range(4):
            tt("dve", wsTm, wsTm[:, g, :], wsT, wsT[:, g, :], triu, triu[:], ALU.mult)
        s.pop()

    cosT = s.sb([128, 16, 64], F32, "cos")
    sinT = s.sb([128, 16, 64], F32, "sin")
    s.push()
    posf = s.sb([128, 16], F32, "posf")
    cp("dve", posf, posf[:], posi, posi[:])
    ang = s.sb([128, 16, 64], F32, "ang")
    for t_ in range(16):
        ts("dve", ang, ang[:, t_, :], invf, invf[:], posf[:, t_:t_ + 1], None, ALU.mult, extra_r=[posf])
    kf = s.sb([128, 16, 64], F32, "kf")
    ki = s.sb([128, 16, 64], I32, "ki")
    ts("dve", kf, kf[:], ang, ang[:], 1.0 / TWO_PI, None, ALU.mult)
    cp("dve", ki, ki[:], kf, kf[:])
    cp("dve", kf, kf[:], ki, ki[:])
    C1 = 6.28125
    C2 = TWO_PI - C1
    r0 = s.sb([128, 16, 64], F32, "r0")
    stt("dve", r0, r0[:], kf, kf[:], -C1, ang, ang[:], ALU.mult, ALU.add)
    stt("dve", r0, r0[:], kf, kf[:], -C2, r0, r0[:], ALU.mult, ALU.add)
    m1 = s.sb([128, 16, 64], F32, "m1")
    ts("dve", m1, m1[:], r0, r0[:], math.pi, None, ALU.is_gt)
    stt("dve", r0, r0[:], m1, m1[:], -TWO_PI, r0, r0[:], ALU.mult, ALU.add)
    ts("dve", m1, m1[:], r0, r0[:], -math.pi, None, ALU.is_lt)
    stt("dve", r0, r0[:], m1, m1[:], TWO_PI, r0, r0[:], ALU.mult, ALU.add)
    ts("dve", r0, r0[:], r0, r0[:], math.pi, -math.pi, ALU.min, ALU.max)
    act(sinT, sinT[:], r0, r0[:], AF.Sin)
    stt("dve", m1, m1[:], r0, r0[:], -1.0, r0, r0[:], ALU.mult, ALU.max)
    ts("dve", m1, m1[:], m1, m1[:], -1.0, math.pi / 2, ALU.mult, ALU.add)
    act(cosT, cosT[:], m1, m1[:], AF.Sin)
    s.pop()
    dbg_store("cos", cosT, cosT[:])
    dbg_store("sin", sinT, sinT[:])

    lrT = s.sb([32, 512], F32, "lrT")
    s.op("dve", lambda e: e.memset(lrT[:], 1.0), writes=[lrT])
    Sst = s.sb([128, 4, 128], F32, "S")
    Sbf = s.sb([128, 4, 128], BF16, "Sbf")
    vaug = s.sb([128, 4, 4, 129], BF16, "vaug")
    s.op("dve", lambda e: e.memset(vaug[:], 1.0), writes=[vaug])
    Bacc = s.sb([128, 4], F32, "Bacc")
    kmacc = s.sb([128, 4, 8], F32, "kmacc")

    def init_state_a():
        s.op("dve", lambda e: e.memset(Bacc[:], 0.0), writes=[Bacc])
        s.op("dve", lambda e: e.memset(Sst[:], 0.0), writes=[Sst])

    def init_state_b(cb):
        s.push()
        Sall = s.sb([128, 4, 4, 128], F32, "Sall")
        Ball = s.sb([128, 4, 4], F32, "Ball")
        Ep = s.sb([128, 4, 4], F32, "Ep")
        coef = s.sb([128, 4, 4], F32, "coef")
        f3 = ccf_out[cb].rearrange("(r p) c -> p r c", p=128)
        s.dma("sp", lambda e: e.dma_start(out=Sall[:].rearrange("p r h d -> p r (h d)"), in_=f3[:, :, 32:544]),
              reads=[ccf_out_t[cb]], writes=[Sall])
        s.dma("sp", lambda e: e.dma_start(out=Ball[:], in_=f3[:, :, 544:548]), reads=[ccf_out_t[cb]], writes=[Ball])
        for h in range(4):
            s.dma("pool", lambda e, h=h: e.dma_start(out=kmT[:, h, :].rearrange("p (r b) -> p r b", r=4),
                                                     in_=f3[:, :, h * 8:(h + 1) * 8]),
                  reads=[ccf_out_t[cb]], writes=[kmT])
        s.op("dve", lambda e: e.memset(Ep[:], 0.0), writes=[Ep])
        for p in range(4):
            for r in range(4):
                stt("dve", Ep, Ep[:, p, :], Ball, Ball[:, r, :], bmask[:, p, r:r + 1], Ep, Ep[:, p, :],
                    ALU.mult, ALU.add, extra_r=[bmask])
        act(coef, coef[:], Ep, Ep[:], AF.Exp)
        for p in range(4):
            ts("dve", coef, coef[:, p, :], coef, coef[:, p, :], cinc[:, p:p + 1], None, ALU.mult, extra_r=[cinc])
        s.op("dve", lambda e: e.memset(Sst[:], 0.0), writes=[Sst])
        for p in range(4):
            for h in range(4):
                stt("dve", Sst, Sst[:, h, :], Sall, Sall[:, p, h, :], coef[:, p, h:h + 1], Sst, Sst[:, h, :],
                    ALU.mult, ALU.add, extra_r=[coef])
        s.pop()

    NW = 5
    wpool = [s.sb([128, 8, 512], BF16, f"w{i}") for i in range(NW)]
    w_rr = [0]

    def wtile():
        t = wpool[w_rr[0] % NW]
        w_rr[0] += 1
        return t

    def load_w(src_ap, rows_kc, c0, ncols):
        t = wtile()
        ap_out = t[:, 0:rows_kc, 0:ncols]
        ap_in = src_ap.rearrange("(kc p) c -> p kc c", p=128)[:, :, c0:c0 + ncols]
        s.dma("pool", lambda e: e.dma_start(out=ap_out, in_=ap_in), writes=[t])
        return t

    xgt = [s.sb([128, D], F32, f"xg{i}") for i in range(4)]
    xT = s.sb([128, 8, 512], BF16, "xT")
    brT = [s.sb([128, 4, 512], BF16, f"brT{i}") for i in range(3)]
    ones_f = s.sb([128, 128], F32, "ones_f")
    s.op("dve", lambda e: e.memset(ones_f[:], 1.0), writes=[ones_f])

    def proj_fm(ps, ps_ap, wt, c_lo, M, src=None):
        src = src or xT
        for kc in range(8):
            mm(ps, ps_ap, wt, wt[:, kc, c_lo:c_lo + M], src, src[:, kc, :], start=(kc == 0), stop=(kc == 7))

    def proj_tm(ps, ps_ap, wt, c_lo, N, ti, src=None):
        src = src or xT
        for kc in range(8):
            mm(ps, ps_ap, src, src[:, kc, ti * 128:(ti + 1) * 128], wt, wt[:, kc, c_lo:c_lo + N],
               start=(kc == 0), stop=(kc == 7))

    def transpose_to(src_t, src_fn, dst_t, dst_fn, nblk):
        for j0 in range(0, nblk, 4):
            n = min(4, nblk - j0)
            ps = s.bank()
            for j in range(n):
                tr(ps, ps[:, j * 128:(j + 1) * 128], src_t, src_fn(j0 + j), ident)
            cp_any(dst_t, dst_fn(j0, n), ps, ps[:, 0:n * 128].rearrange("p (a b) -> p a b", a=n))

    ln_pool = [rot([128, 2, 6], F32, "st6", 2), rot([128, 2], F32, "mv", 2), rot([128, 1], F32, "rs", 2)]

    def layer_norm(yt, g_t, b_t):
        st6 = ln_pool[0]()
        mv = ln_pool[1]()
        rs = ln_pool[2]()
        for hf in range(2):
            s.op("dve", lambda e, hf=hf: e.bn_stats(st6[:, hf, :], yt[:, hf * 512:(hf + 1) * 512]),
                 reads=[yt], writes=[st6])
        s.op("dve", lambda e: e.bn_aggr(mv[:], st6[:].rearrange("p a b -> p (a b)")), reads=[st6], writes=[mv])
        act(rs, rs[:], mv, mv[:, 1:2], AF.Sqrt, bias=1e-5)
        s.op("dve", lambda e: e.reciprocal(rs[:], rs[:]), reads=[rs], writes=[rs])
        ts("dve", yt, yt[:], yt, yt[:], mv[:, 0:1], rs[:, 0:1], ALU.subtract, ALU.mult, extra_r=[mv, rs])
        tt("dve", yt, yt[:], yt, yt[:], g_t, g_t[:], ALU.mult)
        tt("dve", yt, yt[:], yt, yt[:], b_t, b_t[:], ALU.add)

    class _Stop(Exception):
        pass

    def chk(tag):
        if stop == tag:
            raise _Stop()

    def run_phase(isB, l):
      cb = l % 2
      w_in_d = w_in_all[l]
      wbr_d = wbr_all[l]
      wout_d = wout_all[l]
      wff1_d = wff1_all[l]
      wff2_d = wff2_all[l]
      x_d = x_in if l == 0 else xbuf[(l - 1) % 2]
      xo_d = out_d if l == NL - 1 else xbuf[l % 2]
      if isB:
          init_state_b(cb)
      else:
          init_state_a()
      cp("act", Sbf, Sbf[:], Sst, Sst[:])
      pend_cc = []
      for g in range(NG):
          def load_x(gq, ti):
              r0_ = gq * 512 + ti * 128
              s.dma("sp", lambda e, ti=ti, r0_=r0_: e.dma_start(out=xgt[ti][:], in_=x_d[r0_:r0_ + 128, :]),
                    reads=([] if l == 0 else [xsrc_t[((l - 1) % 2, gq, ti)]]), writes=[xgt[ti]])

          if isB or g == 0:
              for ti in range(4):
                  load_x(g, ti)
          for ti in range(4):
              transpose_to(xgt[ti], lambda j, ti=ti: xgt[ti][:, j * 128:(j + 1) * 128],
                           xT, lambda j0, n, ti=ti: xT[:, j0:j0 + n, ti * 128:(ti + 1) * 128], 8)
          if (not isB) and g + 1 < NG:
              for ti in range(4):
                  load_x(g + 1, ti)

          chk("xT")
          s.push()
          Eq = s.sb([128, 4, 512], F32, "Eq")
          kinv_tok = s.sb([128, 4, 512], BF16, "kinv_tok")
          v_tok = s.sb([128, 4, 512], BF16, "v_tok")
          if isB:
              Ek = s.sb([128, 4, 512], F32, "Ek")
              wk = load_w(w_in_d, 8, C_GK, 512)
              s.dma("sp", lambda e, g=g: e.dma_start(out=Eq[:].rearrange("p h t -> p (h t)"), in_=glaE_d[g]),
                    reads=[glaE_t[g]], writes=[Eq])
              s.dma("sp", lambda e, g=g: e.dma_start(out=kinv_tok[:].rearrange("p h t -> p (h t)"), in_=glaK_d[g]),
                    reads=[glaK_t[g]], writes=[kinv_tok])
              s.dma("sp", lambda e, g=g: e.dma_start(out=v_tok[:].rearrange("p h t -> p (h t)"), in_=glaV_d[g]),
                    reads=[glaV_t[g]], writes=[v_tok])
              s.op("dve", lambda e: e.reciprocal(Ek[:], Eq[:]), reads=[Eq], writes=[Ek])
          else:
              wlr = load_w(w_in_d, 8, C_LR, 16)
              ps = s.bank()
              proj_fm(ps, ps[0:16, :], wlr, 0, 16)
              cp("dve", lrT, lrT[0:16, :], ps, ps[0:16, :])
              chk("g1")
              wk = load_w(w_in_d, 8, C_GK, 512)
              wv = load_w(w_in_d, 8, C_GV, 512)
              while pend_cc:
                  pend_cc.pop(0)()
              r_sp = rot([128, 512], F32, "sp", 2)
              r_ekt = rot([128, 512], F32, "ekt", 2)
              for ti in range(4):
                  sp_t = r_sp()
                  ps = s.bank()
                  mm(ps, ps[:], lrT, lrT[0:17, ti * 128:(ti + 1) * 128], wgu, wgu[0:17, :])
                  act(sp_t, sp_t[:], ps, ps[:], AF.Exp, scale=-1.0)
                  act(sp_t, sp_t[:], sp_t, sp_t[:], AF.Ln, bias=1.0)
                  chk("g2")
                  psb = s.bank()
                  for h in range(4):
                      mm(psb, psb[:, h * 128:(h + 1) * 128], sp_t, sp_t[:, h * 128:(h + 1) * 128], tri, tri[:])
                  pv = psb[:].rearrange("p (a b) -> p a b", a=4)
                  chk("g2b")
                  act(Eq, Eq[:, :, ti * 128:(ti + 1) * 128], psb, pv, AF.Exp)
                  chk("g2c")
                  if isB:
                      act(Ek, Ek[:, :, ti * 128:(ti + 1) * 128], psb, pv, AF.Exp, scale=-1.0)
                  else:
                      for cc in (63, 127):
                          tt("dve", Bacc, Bacc[:], Bacc, Bacc[:], psb, pv[:, :, cc], ALU.add)
                  chk("g3")
                  pst = s.bank()
                  mm(pst, pst[:], tri, tri[:], sp_t, sp_t[:])
                  ekt = r_ekt()
                  act(ekt, ekt[:], pst, pst[:], AF.Exp, scale=-1.0)
                  psk = s.bank()
                  proj_tm(psk, psk[:], wk, 0, 512, ti)
                  tt("dve", kinv_tok, kinv_tok[:, ti, :], psk, psk[:], ekt, ekt[:], ALU.mult)
                  psv = s.bank()
                  proj_tm(psv, psv[:], wv, 0, 512, ti)
                  cp("act", v_tok, v_tok[:, ti, :], psv, psv[:])
              s.dma("sp", lambda e, g=g: e.dma_start(out=glaE_d[g], in_=Eq[:].rearrange("p h t -> p (h t)")),
                    reads=[Eq], writes=[glaE_t[g]])
              s.dma("sp", lambda e, g=g: e.dma_start(out=glaK_d[g], in_=kinv_tok[:].rearrange("p h t -> p (h t)")),
                    reads=[kinv_tok], writes=[glaK_t[g]])
              s.dma("sp", lambda e, g=g: e.dma_start(out=glaV_d[g], in_=v_tok[:].rearrange("p h t -> p (h t)")),
                    reads=[v_tok], writes=[glaV_t[g]])
          if isB:
              wq = load_w(w_in_d, 8, C_GQ, 512)
              qdec = s.sb([128, 4, 512], BF16, "qdec")
              qlo = s.sb([128, 4, 512], BF16, "qlo")
              qhi = s.sb([128, 4, 512], BF16, "qhi")
              kinvT = s.sb([128, 4, 512], BF16, "kinvT")
              for h in range(4):
                  ps = s.bank()
                  proj_fm(ps, ps[:], wq, h * 128, 128)
                  tt("dve", qdec, qdec[:, h, :], ps, ps[:], Eq, Eq[:, h, :], ALU.mult)
                  tt("dve", qlo, qlo[:, h, :], qdec, qdec[:, h, :], lomask, lomask[:], ALU.mult)
                  tt("dve", qhi, qhi[:, h, :], qdec, qdec[:, h, :], himask, himask[:], ALU.mult)
                  ps = s.bank()
                  proj_fm(ps, ps[:], wk, h * 128, 128)
                  tt("dve", kinvT, kinvT[:, h, :], ps, ps[:], Ek, Ek[:, h, :], ALU.mult)
              wgg = load_w(w_in_d, 8, C_GG, 512)
              GW = s.sb([128, 4, 512], F32, "GW")
              for ti in range(4):
                  ps = s.bank()
                  proj_tm(ps, ps[:], wgg, 0, 512, ti)
                  act(GW, GW[:, ti, :], ps, ps[:], AF.Silu)
                  tt("dve", GW, GW[:, ti, :], GW, GW[:, ti, :], normw, normw[:], ALU.mult)
          chk("g4")
          r_tmpS = rot([128, 128], F32, "tmpS", 8)
          if isB:
              r_attn = rot([128, 128], BF16, "attn", 8)
              r_junk = rot([128, 128], F32, "junk", 2)
              r_ssq = rot([128, 1], F32, "ssq", 8)
              r_gout = rot([128, 512], F32, "gout", 2)
          HS = [slice(h * 128, (h + 1) * 128) for h in range(4)]
          pso_b = [s.psum_banks[h] for h in range(4)]
          pkv_b = [s.psum_banks[4 + h] for h in range(4)]

          def state_update(ti, half):
              rs_ = slice(half * 64, half * 64 + 64)
              cc = ti * 128 + half * 64 + 63
              for h in range(4):
                  mm(pkv_b[h], pkv_b[h][:, 0:128], kinv_tok, kinv_tok[rs_, ti, HS[h]], v_tok, v_tok[rs_, ti, HS[h]])
              for h in range(4):
                  tmpS = r_tmpS()
                  tt("dve", tmpS, tmpS[:], pkv_b[h], pkv_b[h][:, 0:128], Sst, Sst[:, h, :], ALU.add)
                  ts("dve", Sst, Sst[:, h, :], tmpS, tmpS[:], Eq[:, h, cc:cc + 1], None, ALU.mult, extra_r=[Eq])
                  if isB:
                      cp("act", Sbf, Sbf[:, h, :], Sst, Sst[:, h, :])

          pend_g = []
          rec_steps = []
          for ti in range(4):
              tsl = slice(ti * 128, (ti + 1) * 128)
              if isB:
                  gout = r_gout()
                  attns = []
                  for h in range(4):
                      mm(pkv_b[h], pkv_b[h][:, 0:128], kinvT, kinvT[:, h, tsl], qdec, qdec[:, h, tsl])
                  for h in range(4):
                      attn = r_attn()
                      attns.append(attn)
                      tt("dve", attn, attn[:], pkv_b[h], pkv_b[h][:, 0:128], gmask, gmask[:], ALU.mult)
                  for h in range(4):
                      mm(pso_b[h], pso_b[h][:, 0:128], attns[h], attns[h][:], v_tok, v_tok[:, ti, HS[h]],
                         start=True, stop=False)
                      mm(pso_b[h], pso_b[h][:, 0:128], qlo, qlo[:, h, tsl], Sbf, Sbf[:, h, :], start=False, stop=False)
              if isB:
                  state_update(ti, 0)
              else:
                  rec_steps.append(lambda ti=ti: state_update(ti, 0))
              if isB:
                  for h in range(4):
                      mm(pso_b[h], pso_b[h][:, 0:128], qhi, qhi[:, h, tsl], Sbf, Sbf[:, h, :], start=False, stop=True)
              if isB:
                  state_update(ti, 1)
              else:
                  rec_steps.append(lambda ti=ti: state_update(ti, 1))
              if isB:
                  for h in range(4):
                      junk = r_junk()
                      ssq = r_ssq()
                      act(junk, junk[:], pso_b[h], pso_b[h][:, 0:128], AF.Square, accum=(ssq, ssq[:]))
                      act(ssq, ssq[:], ssq, ssq[:], AF.Sqrt, bias=1.28e-4, scale=1.0 / 128)
                      s.op("dve", lambda e, ssq=ssq: e.reciprocal(ssq[:], ssq[:]), reads=[ssq], writes=[ssq])
                      stt("dve", gout, gout[:, HS[h]], pso_b[h], pso_b[h][:, 0:128], ssq[:, 0:1], GW, GW[:, ti, HS[h]],
                          ALU.mult, ALU.mult, extra_r=[ssq])
                  if g == 0 and ti == 0:
                      dbg_store("gla", gout, gout[:])
                  if pend_g:
                      pend_g.pop()()
                  pend_g.append(lambda gout=gout, ti=ti: transpose_to(
                      gout, lambda j: gout[:, j * 128:(j + 1) * 128],
                      brT[0], lambda j0, n: brT[0][:, j0:j0 + n, ti * 128:(ti + 1) * 128], 4))
          if isB:
              pend_g.pop()()
          if isB:
              s.pop()

          chk("gla")
          if isB:
              s.push()
              wzu = load_w(w_in_d, 8, C_GZ, 512)
              wzv = load_w(w_in_d, 8, C_GZ + 512, 512)
              r_u = rot([128, 512], F32, "u", 2)
              r_vg = rot([128, 512], F32, "vg", 2)
              r_vln = rot([128, 512], BF16, "vln", 2)
              r_gm = rot([128, 512], F32, "gm", 2)
              r_st6 = rot([128, 6], F32, "gst6", 2)
              r_mv = rot([128, 2], F32, "gmv", 2)
              r_rs = rot([128, 1], F32, "grs", 2)
              gst = {}

              def gm_s0(ti):
                  u_t = r_u()
                  vg = r_vg()
                  ps = s.bank()
                  proj_tm(ps, ps[:], wzu, 0, 512, ti)
                  act(u_t, u_t[:], ps, ps[:], AF.Gelu)
                  ps = s.bank()
                  proj_tm(ps, ps[:], wzv, 0, 512, ti)
                  act(vg, vg[:], ps, ps[:], AF.Gelu)
                  st6 = r_st6()
                  mv = r_mv()
                  rs = r_rs()
                  s.op("dve", lambda e, st6=st6, vg=vg: e.bn_stats(st6[:], vg[:]), reads=[vg], writes=[st6])
                  s.op("dve", lambda e, st6=st6, mv=mv: e.bn_aggr(mv[:], st6[:]), reads=[st6], writes=[mv])
                  act(rs, rs[:], mv, mv[:, 1:2], AF.Sqrt, bias=1e-5)
                  s.op("dve", lambda e, rs=rs: e.reciprocal(rs[:], rs[:]), reads=[rs], writes=[rs])
                  ts("dve", vg, vg[:], vg, vg[:], mv[:, 0:1], rs[:, 0:1], ALU.subtract, ALU.mult, extra_r=[mv, rs])
                  tt("dve", vg, vg[:], vg, vg[:], glng, glng[:], ALU.mult)
                  vln = r_vln()
                  tt("dve", vln, vln[:], vg, vg[:], glnb, glnb[:], ALU.add)
                  gst[ti] = (u_t, vln)

              def gm_s1(ti):
                  u_t, vln = gst[ti]
                  ps = s.bank()
                  for h in range(4):
                      hs = slice(h * 128, (h + 1) * 128)
                      mm(ps, ps[:, hs], wsTm, wsTm[:, h, :], vln, vln[:, hs])
                  gm = r_gm()
                  for h in range(4):
                      hs = slice(h * 128, (h + 1) * 128)
                      stt("dve", gm, gm[:, hs], ps, ps[:, hs], bst[:, h:h + 1], u_t, u_t[:, hs], ALU.add, ALU.mult,
                          extra_r=[bst])
                  if g == 0 and ti == 0:
                      dbg_store("gmlp", gm, gm[:])
                  gst[ti] = gm

              def gm_s2(ti):
                  gm = gst[ti]
                  transpose_to(gm, lambda j, gm=gm: gm[:, j * 128:(j + 1) * 128],
                               brT[1], lambda j0, n, ti=ti: brT[1][:, j0:j0 + n, ti * 128:(ti + 1) * 128], 4)

              pipeline([gm_s0, gm_s1, gm_s2])
              s.pop()

          chk("gmlp")
          s.push()
          krT = s.sb([128, 4, 512], BF16, "krT")
          if not isB:
              wmk = load_w(w_in_d, 8, C_MK, 512)
              wmv = load_w(w_in_d, 8, C_MV, 512)
          else:
              KT_i = ccb_in[cb][g][:, 0:2048].rearrange("p (h t) -> p h t", h=4)
              V_i = ccb_in[cb][g][:, 2048:4096].rearrange("p (h t c) -> p h t c", h=4, t=4)
              s.dma("sp", lambda e, KT_i=KT_i: e.dma_start(out=krT[:], in_=KT_i),
                    reads=[ccb_in_t[cb][g]], writes=[krT])
              for ti in range(4):
                  s.dma("sp", lambda e, ti=ti, V_i=V_i: e.dma_start(
                      out=vaug[:, ti, :, 0:128], in_=V_i[:, :, ti, :]),
                      reads=[ccb_in_t[cb][g]], writes=[vaug])
          if isB:
              wmq = load_w(w_in_d, 8, C_MQ, 512)
              qrT = s.sb([128, 4, 512], BF16, "qrT")
              nbT = s.sb([128, 4, 512], BF16, "nbT")
              s.op("dve", lambda e, nbT=nbT: e.memset(nbT[:], 0.0), writes=[nbT])

          r_t1 = rot([128, 4, 16], F32, "t1", 4)
          r_t2 = rot([128, 4, 16], F32, "t2", 4)
          r_kr = rot([128, 512], F32, "kr", 2)
          if isB:
              r_qr = rot([128, 512], F32, "qr", 2)
              r_gsc = rot([128, 4, 32], F32, "gsc", 2)
              r_nb = rot([128, 4, 32], F32, "nb", 2)
              r_mx8 = rot([128, 8], F32, "mx8", 4)
              r_km2 = None
          else:
              r_km2 = rot([128, 4], F32, "km2", 2)

          def rope_tm(ps, dst, gti):
              cp("act", dst, dst[:], ps, ps[:])
              pv = ps[:].rearrange("p (h d) -> p h d", h=4)
              dv = dst[:].rearrange("p (h d) -> p h d", h=4)
              cs = cosT[:, gti, :].rearrange("p (h d) -> p h d", h=4)
              sn = sinT[:, gti, :].rearrange("p (h d) -> p h d", h=4)
              t1 = r_t1()
              t2 = r_t2()
              tt("dve", t1, t1[:], ps, pv[:, :, 0:16], cosT, cs, ALU.mult)
              tt("dve", t2, t2[:], ps, pv[:, :, 16:32], sinT, sn, ALU.mult)
              tt("dve", dst, dv[:, :, 0:16], t1, t1[:], t2, t2[:], ALU.subtract)
              t1 = r_t1()
              t2 = r_t2()
              tt("dve", t1, t1[:], ps, pv[:, :, 16:32], cosT, cs, ALU.mult)
              tt("dve", t2, t2[:], ps, pv[:, :, 0:16], sinT, sn, ALU.mult)
              tt("dve", dst, dv[:, :, 16:32], t1, t1[:], t2, t2[:], ALU.add)

          mst = {}

          def mb_s0(ti):
              gti = g * 4 + ti
              kr = None
              if not isB:
                  kr = r_kr()
                  ps = s.bank()
                  proj_tm(ps, ps[:], wmk, 0, 512, ti)
                  rope_tm(ps, kr, gti)
                  ps = s.bank()
                  proj_tm(ps, ps[:], wmv, 0, 512, ti)
                  cp("act", vaug, vaug[:, ti, :, 0:128], ps, ps[:].rearrange("p (h d) -> p h d", h=4))
              qr = None
              if isB:
                  qr = r_qr()
                  ps = s.bank()
                  proj_tm(ps, ps[:], wmq, 0, 512, ti)
                  rope_tm(ps, qr, gti)
              mst[ti] = (kr, qr)

          def mb_s1(ti):
              gti = g * 4 + ti
              tsl = slice(ti * 128, (ti + 1) * 128)
              kr, qr = mst[ti]
              if not isB:
                  pst = s.bank()
                  for h in range(4):
                      tr(pst, pst[:, h * 128:(h + 1) * 128], kr, kr[:, h * 128:(h + 1) * 128], ident)
                  pvw = pst[:].rearrange("p (a b) -> p a b", a=4)
                  cp("act", krT, krT[:, :, tsl], pst, pvw)
              if not isB:
                  blk = gti // 2
                  if gti % 2 == 0:
                      s.op("dve", lambda e, pvw=pvw, blk=blk: e.tensor_reduce(kmacc[:, :, blk], pvw, AX.X, ALU.add),
                           reads=[pst], writes=[kmacc])
                  else:
                      km2 = r_km2()
                      s.op("dve", lambda e, pvw=pvw, km2=km2: e.tensor_reduce(km2[:], pvw, AX.X, ALU.add),
                           reads=[pst], writes=[km2])
                      tt("dve", kmacc, kmacc[:, :, blk], kmacc, kmacc[:, :, blk], km2, km2[:], ALU.add)
              else:
                  pst = s.bank()
                  for h in range(4):
                      tr(pst, pst[:, h * 128:(h + 1) * 128], qr, qr[:, h * 128:(h + 1) * 128], ident)
                  cp("dve", qrT, qrT[:, :, tsl], pst, pst[:].rearrange("p (a b) -> p a b", a=4))
                  psg = s.bank()
                  for h in range(4):
                      mm(psg, psg[:, h * 32:(h + 1) * 32], qrT, qrT[:, h, tsl], kmT, kmT[:, h, :])
                  gsc = r_gsc()
                  nb = r_nb()
                  for h in range(4):
                      mx8 = r_mx8()
                      tt("dve", gsc, gsc[:, h, :], psg, psg[:, h * 32:(h + 1) * 32], valid, valid[:, gti, :], ALU.mult)
                      tt("dve", gsc, gsc[:, h, :], gsc, gsc[:, h, :], negfill, negfill[:, gti, :], ALU.add)
                      s.op("dve", lambda e, h=h, gsc=gsc, mx8=mx8: e.max(mx8[:], gsc[:, h, :]), reads=[gsc], writes=[mx8])
                      ts("dve", nb, nb[:, h, :], gsc, gsc[:, h, :], mx8[:, 2:3], None, ALU.is_ge, extra_r=[mx8])
                      tt("dve", nb, nb[:, h, :], nb, nb[:, h, :], valid, valid[:, gti, :], ALU.mult)
                  ts("dve", nb, nb[:], nb, nb[:], 1.0, -NEG, ALU.subtract, ALU.mult)
                  if g == 0 and ti == 0:
                      dbg_store("nb", nb, nb[:])
                  mst[ti] = nb

          def mb_s2(ti):
              if not isB:
                  return
              tsl = slice(ti * 128, (ti + 1) * 128)
              nb = mst[ti]
              psn = s.bank()
              for h in range(4):
                  tr(psn, psn[0:32, h * 128:(h + 1) * 128], nb, nb[:, h, :], ident)
              cp("act", nbT, nbT[0:32, :, tsl], psn, psn[0:32, :].rearrange("p (a b) -> p a b", a=4))

          def rec_between(step):
              for _ in range(2 if step < 2 else 1):
                  if rec_steps:
                      rec_steps.pop(0)()

          pipeline([mb_s0, mb_s1, mb_s2], between=(None if isB else rec_between))
          while rec_steps:
              rec_steps.pop(0)()
          if not isB:
              KT_o = ccb_in[cb][g][:, 0:2048].rearrange("p (h t) -> p h t", h=4)
              V_o = ccb_in[cb][g][:, 2048:4096].rearrange("p (h t c) -> p h t c", h=4, t=4)
              s.dma("sp", lambda e, KT_o=KT_o: e.dma_start(out=KT_o, in_=krT[:]),
                    reads=[krT], writes=[ccb_in_t[cb][g]])
              for ti in range(4):
                  s.dma("sp", lambda e, ti=ti, V_o=V_o: e.dma_start(
                      out=V_o[:, :, ti, :], in_=vaug[:, ti, :, 0:128]),
                      reads=[vaug], writes=[ccb_in_t[cb][g]])
              pend_cc.append(lambda g=g: s.cc(
                  lambda e: e.collective_compute("AllGather", ALU.bypass, replica_groups=[[0, 1, 2, 3], [4, 5, 6, 7]],
                                                 ins=[ccb_in[cb][g].opt()], outs=[ccb_out[cb][g].opt()]),
                  reads=[ccb_in_t[cb][g]], writes=[ccb_out_t[cb][g]]))
          else:
              SCALE = 128.0 ** -0.5
              p_tiles = [s.sb([128, 512], BF16, f"p_sb{i}") for i in range(5)]
              pacc = s.sb([128, 512], F32, "pacc")
              rl = s.sb([128, 512], F32, "rl")
              NSB = 6
              KTc = [s.sb([128, 2048], BF16, f"KTc{r_}") for r_ in range(4)]
              Vc = [s.sb([128, 16, 129], BF16, f"Vc{r_}") for r_ in range(4)]
              for r_ in range(4):
                  s.op("dve", lambda e, r_=r_: e.memset(Vc[r_][:, :, 128:129], 1.0), writes=[Vc[r_]])
              for h in range(4):
                  for r_ in range(4):
                      if r_ * 16 >= 2 * min(32, 26 + 2 * g):
                          continue
                      for gg in range(4):
                          src = ccb_out[cb][gg][r_ * 128:(r_ + 1) * 128, :]
                          s.dma("sp", lambda e, h=h, src=src, r_=r_, gg=gg: e.dma_start(
                              out=KTc[r_][:, gg * 512:(gg + 1) * 512], in_=src[:, h * 512:(h + 1) * 512]),
                              reads=[ccb_out_t[cb][gg]], writes=[KTc[r_]])
                          s.dma("sp", lambda e, h=h, src=src, r_=r_, gg=gg: e.dma_start(
                              out=Vc[r_][:, gg * 4:(gg + 1) * 4, 0:128],
                              in_=src[:, 2048 + h * 512:2048 + (h + 1) * 512].rearrange("p (t c) -> p t c", t=4)),
                              reads=[ccb_out_t[cb][gg]], writes=[Vc[r_]])
                  oT = s.psum_banks[7]
                  descs = [("g", kt) for kt in range(2 * min(32, 26 + 2 * g))] + [("o", lb, j) for lb in range(2) for j in range(2)]

                  def scores(idx, d):
                      pss = s.psum_banks[idx % NSB]
                      p_sb = p_tiles[idx % 5]
                      if d[0] == "g":
                          kt = d[1]
                          n = kt // 2
                          KTh = KTc[kt // 16]
                          mm(pss, pss[:], KTh, KTh[:, (kt % 16) * 128:(kt % 16 + 1) * 128], qrT, qrT[:, h, :],
                             start=True, stop=False)
                          mm(pss, pss[:], identb, identb[:, n:n + 1].to_broadcast([128, 128]), nbT, nbT[:, h, :],
                             start=False, stop=True)
                          act(p_sb, p_sb[:], pss, pss[:], AF.Exp, scale=SCALE)
                      else:
                          _, lb, j = d
                          tk = 2 * lb + j
                          mm(pss, pss[:, 0:256], krT, krT[:, h, tk * 128:(tk + 1) * 128], qrT,
                             qrT[:, h, lb * 256:(lb + 1) * 256], start=True, stop=False)
                          mm(pss, pss[:, 0:256], identb, identb[:], cmask, cmask[:, j, :], start=False, stop=True)
                          act(p_sb, p_sb[:, 0:256], pss, pss[:, 0:256], AF.Exp, scale=SCALE)
                      return p_sb

                  def pv(idx, d, p_sb):
                      if d[0] == "g":
                          kt = d[1]
                          mm(oT, oT[:], Vc[kt // 16], Vc[kt // 16][:, kt % 16, 0:128], p_sb, p_sb[:],
                             start=(kt == 0), stop=False)
                          if idx == 0:
                              cp("dve", pacc, pacc[:], p_sb, p_sb[:])
                          else:
                              tt("dve", pacc, pacc[:], pacc, pacc[:], p_sb, p_sb[:], ALU.add)
                      else:
                          _, lb, j = d
                          tk = 2 * lb + j
                          qs = slice(lb * 256, (lb + 1) * 256)
                          mm(oT, oT[:, qs], vaug, vaug[:, tk, h, 0:128], p_sb, p_sb[:, 0:256],
                             start=False, stop=(lb == 1 and j == 1))
                          tt("dve", pacc, pacc[:, qs], pacc, pacc[:, qs], p_sb, p_sb[:, 0:256], ALU.add)

                  LOOK = 2
                  pend = []
                  for idx, d in enumerate(descs):
                      p_cur = scores(idx, d)
                      pend.append((idx, d, p_cur))
                      if len(pend) > LOOK:
                          pv(*pend.pop(0))
                  while pend:
                      pv(*pend.pop(0))
                  psl = s.psum_banks[6]
                  mm(psl, psl[:], ones_f, ones_f[:], pacc, pacc[:])
                  s.op("dve", lambda e: e.reciprocal(rl[:], psl[:]), reads=[psl], writes=[rl])
                  tt("dve", brT[2], brT[2][:, h, :], oT, oT[:], rl, rl[:], ALU.mult)
          s.pop()
          if not isB:
              s.pop()
              continue

          chk("moba")
          s.push()
          macc = s.sb([128, 4, D], F32, "macc")
          r_sig = rot([128, 512], F32, "sig", 3)
          for br in range(3):
              wb0 = load_w(wbr_d[br], 4, 0, 512)
              wb1 = load_w(wbr_d[br], 4, 512, 512)
              wg0 = load_w(w_in_d, 8, C_GL + br * 1024, 512)
              wg1 = load_w(w_in_d, 8, C_GL + br * 1024 + 512, 512)
              for ti in range(4):
                  tsl = slice(ti * 128, (ti + 1) * 128)
                  for hf, (wb, wgt) in enumerate(((wb0, wg0), (wb1, wg1))):
                      cs_ = slice(hf * 512, (hf + 1) * 512)
                      psa = s.bank()
                      for kc in range(4):
                          mm(psa, psa[:], brT[br], brT[br][:, kc, tsl], wb, wb[:, kc, 0:512], start=(kc == 0), stop=(kc == 3))
                      psg = s.bank()
                      proj_tm(psg, psg[:], wgt, 0, 512, ti)
                      sig = r_sig()
                      act(sig, sig[:], psg, psg[:], AF.Sigmoid)
                      if br == 0:
                          tt("dve", macc, macc[:, ti, cs_], psa, psa[:], sig, sig[:], ALU.mult)
                      else:
                          tt("dve", sig, sig[:], psa, psa[:], sig, sig[:], ALU.mult)
                          tt("dve", macc, macc[:, ti, cs_], macc, macc[:, ti, cs_], sig, sig[:], ALU.add)
          mT = s.sb([128, 8, 512], BF16, "mT")
          for ti in range(4):
              transpose_to(macc, lambda j, ti=ti: macc[:, ti, j * 128:(j + 1) * 128],
                           mT, lambda j0, n, ti=ti: mT[:, j0:j0 + n, ti * 128:(ti + 1) * 128], 8)
          wo = [load_w(wout_d, 8, 0, 512), load_w(wout_d, 8, 512, 512)]
          for ti in range(4):
              for hf in range(2):
                  cs_ = slice(hf * 512, (hf + 1) * 512)
                  ps = s.bank()
                  proj_tm(ps, ps[:], wo[hf], 0, 512, ti, src=mT)
                  stt("dve", xgt[ti], xgt[ti][:, cs_], xgt[ti], xgt[ti][:, cs_], ALPHA, ps, ps[:], ALU.mult, ALU.add)
              layer_norm(xgt[ti], lnp[0], lnp[1])
          s.pop()
          if g == 0:
              dbg_store("x1", xgt[0], xgt[0][:])

          chk("merge")
          s.push()
          for ti in range(4):
              transpose_to(xgt[ti], lambda j, ti=ti: xgt[ti][:, j * 128:(j + 1) * 128],
                           xT, lambda j0, n, ti=ti: xT[:, j0:j0 + n, ti * 128:(ti + 1) * 128], 8)
          hT = s.sb([128, 32, 512], BF16, "hT")
          r_hsq = rot([128, 512], F32, "hsq", 3)
          for fcc in range(8):
              w1 = load_w(wff1_d, 8, fcc * 512, 512)
              for j in range(4):
                  fc = fcc * 4 + j
                  ps = s.bank()
                  proj_fm(ps, ps[:], w1, j * 128, 128)
                  hsq = r_hsq()
                  act(hsq, hsq[:], ps, ps[:], AF.Square)
                  stt("dve", hT, hT[:, fc, :], ps, ps[:], 0.0, hsq, hsq[:], ALU.is_gt, ALU.mult)
          w2c = [s.sb([128, 8, 512], BF16, f"w2c{q4}") for q4 in range(4)]
          for hf in range(2):
              cs_ = slice(hf * 512, (hf + 1) * 512)
              for q4 in range(4):
                  s.dma("pool", lambda e, hf=hf, q4=q4: e.dma_start(
                      out=w2c[q4][:],
                      in_=wff2_d.rearrange("(fc p) c -> p fc c", p=128)[:, q4 * 8:(q4 + 1) * 8, hf * 512:(hf + 1) * 512]),
                      writes=[w2c[q4]])
              facc = [s.psum_banks[4 + ti] for ti in range(4)]
              for q4 in range(4):
                  for ti in range(4):
                      for f8 in range(8):
                          fc = q4 * 8 + f8
                          mm(facc[ti], facc[ti][:], hT, hT[:, fc, ti * 128:(ti + 1) * 128], w2c[q4], w2c[q4][:, f8, :],
                             start=(fc == 0), stop=(fc == 31))
              for ti in range(4):
                  stt("dve", xgt[ti], xgt[ti][:, cs_], xgt[ti], xgt[ti][:, cs_], ALPHA, facc[ti], facc[ti][:],
                      ALU.mult, ALU.add)
          for ti in range(4):
              layer_norm(xgt[ti], lnp[2], lnp[3])
              r0_ = g * 512 + ti * 128
              s.dma("sp", lambda e, ti=ti, r0_=r0_: e.dma_start(out=xo_d[r0_:r0_ + 128, :], in_=xgt[ti][:]),
                    reads=[xgt[ti]], writes=([] if l == NL - 1 else [xsrc_t[(l % 2, g, ti)]]))
          s.pop()

      if not isB:
          ts("dve", kmacc, kmacc[:], kmacc, kmacc[:], 1.0 / 256, None, ALU.mult)
          s.dma("sp", lambda e: e.dma_start(out=ccf_in[cb][:, 0:32].rearrange("p (h b) -> p h b", h=4), in_=kmacc[:]),
                reads=[kmacc], writes=[ccf_in_t[cb]])
          s.dma("sp", lambda e: e.dma_start(out=ccf_in[cb][:, 32:544].rearrange("p (h d) -> p h d", h=4), in_=Sst[:]),
                reads=[Sst], writes=[ccf_in_t[cb]])
          s.dma("sp", lambda e: e.dma_start(out=ccf_in[cb][:, 544:548], in_=Bacc[:]), reads=[Bacc], writes=[ccf_in_t[cb]])
          GRP = [[0, 1, 2, 3], [4, 5, 6, 7]]
          while pend_cc:
              pend_cc.pop(0)()
          s.cc(lambda e: e.collective_compute("AllGather", ALU.bypass, replica_groups=GRP,
                                              ins=[ccf_in[cb].opt()], outs=[ccf_out[cb].opt()]),
               reads=[ccf_in_t[cb]], writes=[ccf_out_t[cb]])

    try:
        chk("setup")
        for l in range(NL):
            load_layer_params(l)
            run_phase(False, l)
            chk(f"A{l}")
            run_phase(True, l)
            chk(f"B{l}")
    except _Stop:
        pass
    s.wait_all("sp")
    s.emit()
    if dbg is not None:
        print("sb_peak", s.sb_peak, "counts", s.count, "dma", s.dma_n, flush=True)
    return nc


_PROG = {}


def _prog(nl):
    if nl not in _PROG:
        _PROG[nl] = build(nl)
    return _PROG[nl]


def _consts():
    c = {}
    c["ident"] = np.eye(128, dtype=np.float32)
    s_ = np.arange(128)[:, None]
    t_ = np.arange(128)[None, :]
    same = (s_ // 64) == (t_ // 64)
    c["tri"] = np.where((s_ <= t_) & same, -1.0 / 16.0, 0.0).astype(np.float32)
    c["gmask"] = np.where((s_ <= t_) & same, 1.0, 0.0).astype(np.float32)
    c["triu"] = np.where(s_ <= t_, 1.0, 0.0).astype(np.float32)
    tcol = np.arange(512)[None, :] % 128
    c["lomask"] = np.broadcast_to(np.where(tcol < 64, 1.0, 0.0), (128, 512)).astype(ml_dtypes.bfloat16)
    c["himask"] = np.broadcast_to(np.where(tcol >= 64, 1.0, 0.0), (128, 512)).astype(ml_dtypes.bfloat16)
    cm = np.zeros((128, 2, 256), np.float32)
    for j in range(2):
        ks = j * 128 + np.arange(128)[:, None]
        cm[:, j, :] = np.where(ks <= np.arange(256)[None, :], 0.0, NEG)
    c["cmask"] = cm.astype(ml_dtypes.bfloat16)
    c["identb"] = np.eye(128, dtype=np.float32).astype(ml_dtypes.bfloat16)
    half = 16
    inv = (1.0 / (np.float32(500000.0) ** (np.arange(half, dtype=np.float32) * np.float32(2.0 / 32)))).astype(np.float32)
    c["invf"] = np.broadcast_to(np.tile(inv, 4)[None, :], (128, 64)).astype(np.float32).copy()
    return c


def _core_consts(q):
    valid = np.zeros((128, 16, 32), np.float32)
    for ti in range(16):
        own = q * 8 + ti // 2
        valid[:, ti, :own] = 1.0
    negfill = ((valid - 1.0) * 1e30).astype(np.float32)
    cinc = np.zeros((128, 4), np.float32)
    cinc[:, :q] = 1.0
    bmask = np.zeros((128, 4, 4), np.float32)
    for p in range(4):
        for r in range(4):
            if p < r < q:
                bmask[:, p, r] = 1.0
    return {"valid": valid, "negfill": negfill, "cinc": cinc, "bmask": bmask}


def _bcl(v, n=128):
    v = np.asarray(v, np.float32)
    return np.ascontiguousarray(np.broadcast_to(v[:, None, :], (v.shape[0], n, v.shape[1])))


def shard_inputs(x, positions):
    x = np.ascontiguousarray(np.asarray(x, dtype=np.float32))
    positions = np.asarray(positions).astype(np.int32)
    xs = [np.ascontiguousarray(x[c // 4, (c % 4) * TOK:(c % 4 + 1) * TOK, :]) for c in range(NCORE)]
    posi = [np.ascontiguousarray(positions[c // 4, (c % 4) * TOK:(c % 4 + 1) * TOK].reshape(16, 128).T)
            for c in range(NCORE)]
    return xs, posi


def make_in_maps(x, positions, P, nl=DEPTH):
    f = lambda a: np.ascontiguousarray(np.asarray(a, dtype=np.float32)[:nl])
    cst = _consts()
    xs, posi = shard_inputs(x, positions)
    shared = dict(cst)
    shared.update({
        "w_in": f(P["w_in"]),
        "wgu": np.ascontiguousarray(np.concatenate([f(P["w_gate_up"]), f(P["b_gate"])[:, None, :]], axis=1)),
        "normw": _bcl(np.tile(f(P["gla_norm_w"]), (1, 4))),
        "glng": _bcl(f(P["gmlp_ln_g"])), "glnb": _bcl(f(P["gmlp_ln_b"])),
        "wsT": np.ascontiguousarray(f(P["gmlp_w_s"]).transpose(0, 3, 1, 2)),
        "bst": np.ascontiguousarray(f(P["gmlp_b_s"]).transpose(0, 2, 1)),
        "wbr": f(P["w_branch"]), "wout": f(P["w_out"]),
        "ln1g": _bcl(f(P["ln1_g"])), "ln1b": _bcl(f(P["ln1_b"])),
        "ln2g": _bcl(f(P["ln2_g"])), "ln2b": _bcl(f(P["ln2_b"])),
        "wff1": f(P["w_ff1"]), "wff2": f(P["w_ff2"]),
    })
    in_maps = []
    for c in range(NCORE):
        m = dict(shared, x=xs[c], posi=posi[c])
        m.update(_core_consts(c % 4))
        in_maps.append(m)
    return in_maps


def kernel(x, positions, w_in, w_gate_up, b_gate, gla_norm_w, gmlp_ln_g, gmlp_ln_b, gmlp_w_s, gmlp_b_s,
           w_branch, w_out, ln1_g, ln1_b, w_ff1, w_ff2, ln2_g, ln2_b):
    P = dict(w_in=w_in, w_gate_up=w_gate_up, b_gate=b_gate, gla_norm_w=gla_norm_w, gmlp_ln_g=gmlp_ln_g,
             gmlp_ln_b=gmlp_ln_b, gmlp_w_s=gmlp_w_s, gmlp_b_s=gmlp_b_s, w_branch=w_branch, w_out=w_out,
             ln1_g=ln1_g, ln1_b=ln1_b, w_ff1=w_ff1, w_ff2=w_ff2, ln2_g=ln2_g, ln2_b=ln2_b)
    in_maps = make_in_maps(x, positions, P, DEPTH)
    res = run_bass_kernel_spmd(_prog(DEPTH), in_maps, core_ids=list(range(NCORE))).results
    out = np.zeros((BATCH, SEQ, D), np.float32)
    for c in range(NCORE):
        out[c // 4, (c % 4) * TOK:(c % 4 + 1) * TOK, :] = np.asarray(res[c]["xout"], np.float32)
    return out
```

```python
from contextlib import ExitStack
import math
import numpy as np
import ml_dtypes
import concourse.bass as bass
import concourse.mybir as mybir
from concourse.bass_utils import run_bass_kernel_spmd

F32 = mybir.dt.float32
BF16 = mybir.dt.bfloat16
I32 = mybir.dt.int32
AF = mybir.ActivationFunctionType
ALU = mybir.AluOpType
AX = mybir.AxisListType

PAGE = 512
SB_BASE = 16896
SBUF_BYTES = 229376 - 512
PHASE_W = 4000
NDMASEM = 12
ENGS = ("pe", "act", "dve", "pool", "sp")


class T:
    def __init__(self, ap, pages, name):
        self.ap = ap
        self.pages = pages
        self.name = name

    def __getitem__(self, idx):
        return self.ap[idx]


class Sched:
    def __init__(self, nc):
        self.nc = nc
        self.ops = {e: [] for e in ENGS}
        self.count = {e: 0 for e in ENGS}
        self.dma_n = {e: 0 for e in ENGS}
        self.dma_semcnt = {}
        self.cc_n = 0
        self.cc_cnt = {}
        self.known = {e: {} for e in ENGS}
        self.wr = {}
        self.rd = {}
        self.sb_off = SB_BASE
        self.sb_stack = []
        self.memo = {}
        self.sb_peak = 0
        self.ntens = 0
        self.psum_banks = []
        self.stack = ExitStack()

    def sb(self, shape, dtype, name=None):
        esz = {F32: 4, BF16: 2, I32: 4}[dtype]
        nbytes = int(np.prod(shape[1:])) * esz
        off = (self.sb_off + PAGE - 1) // PAGE * PAGE
        assert off + nbytes <= SBUF_BYTES, f"SBUF overflow {name} {off}+{nbytes}"
        self.sb_off = off + nbytes
        self.sb_peak = max(self.sb_peak, self.sb_off)
        mkey = (off, tuple(shape), str(dtype))
        if mkey in self.memo:
            return self.memo[mkey]
        self.ntens += 1
        nm = f"{name or 't'}_{self.ntens}"
        h = self.nc.alloc_sbuf_tensor_at(nm, list(shape), dtype, offset=off)
        pages = range(off // PAGE, (off + nbytes + PAGE - 1) // PAGE)
        t = T(h.ap(), [("sb", p) for p in pages], nm)
        self.memo[mkey] = t
        return t

    def push(self):
        self.sb_stack.append(self.sb_off)

    def pop(self):
        self.sb_off = self.sb_stack.pop()

    def alloc_psum(self):
        for i in range(8):
            h = self.stack.enter_context(self.nc.psum_tensor(f"psb{i}", [128, 512], F32))
            self.psum_banks.append(T(h.ap(), [("ps", i)], f"psb{i}"))
        self.ps_rr = 0

    def bank(self, lo=0, hi=8):
        b = lo + self.ps_rr % (hi - lo)
        self.ps_rr += 1
        return self.psum_banks[b]

    def _deps(self, reads, writes):
        deps = []
        for t in reads:
            for p in t.pages:
                w = self.wr.get(p)
                if w is not None:
                    deps.append((w, "raw"))
        for t in writes:
            for p in t.pages:
                w = self.wr.get(p)
                if w is not None:
                    deps.append((w, "waw"))
                for r in self.rd.get(p, {}).values():
                    deps.append((r, "war"))
        return deps

    def _emit_waits(self, eng, deps):
        need = {}
        for tok, kind in deps:
            if tok[0] == "E":
                _, e2, idx = tok
                if e2 == eng and eng == "pe":
                    continue
                key = ("E", e2)
                val = idx
            elif tok[0] == "C":
                _, slot, val = tok
                key = ("C", slot)
            else:
                _, e2, slot, val = tok
                key = ("D", e2, slot)
            if self.known[eng].get(key, -1) >= val:
                continue
            if need.get(key, -1) < val:
                need[key] = val
        for key, val in need.items():
            self.known[eng][key] = val
            self.ops[eng].append(("wait", key, val))

    def _mark(self, tok, key, reads, writes):
        for t in reads:
            for p in t.pages:
                self.rd.setdefault(p, {})[key] = tok
        for t in writes:
            for p in t.pages:
                self.wr[p] = tok
                self.rd[p] = {}

    def op(self, eng, fn, reads=(), writes=()):
        pr = [t for t in reads if t.pages[0][0] == "ps" and t not in writes]
        if pr:
            writes = list(writes) + pr
        deps = self._deps(reads, writes)
        self._emit_waits(eng, deps)
        idx = self.count[eng]
        self.count[eng] += 1
        self.ops[eng].append(("op", fn, idx))
        self._mark(("E", eng, idx), ("E", eng), reads, writes)

    def dma(self, eng, fn, reads=(), writes=()):
        deps = self._deps(reads, writes)
        n = self.dma_n[eng]
        self.dma_n[eng] += 1
        slot = n % NDMASEM
        prev = self.dma_semcnt.get((eng, slot), 0)
        if prev > 0:
            deps.append((("D", eng, slot, prev), "raw"))
        self._emit_waits(eng, deps)
        cnt = prev + 1
        self.dma_semcnt[(eng, slot)] = cnt
        self.ops[eng].append(("dma", fn, slot))
        tok = ("D", eng, slot, cnt)
        self._mark(tok, ("D", eng, slot, cnt), reads, writes)

    def cc(self, fn, reads=(), writes=()):
        eng = "pool"
        deps = self._deps(reads, writes)
        slot = self.cc_n % 4
        self.cc_n += 1
        prev = self.cc_cnt.get(slot, 0)
        if prev > 0:
            deps.append((("C", slot, prev), "raw"))
        self._emit_waits(eng, deps)
        cnt = prev + 1
        self.cc_cnt[slot] = cnt
        self.ops[eng].append(("cc", fn, slot))
        tok = ("C", slot, cnt)
        self._mark(tok, tok, reads, writes)

    def wait_all(self, eng):
        deps = []
        for slot, cnt in self.cc_cnt.items():
            deps.append((("C", slot, cnt), "raw"))
        for (e2, slot), cnt in self.dma_semcnt.items():
            deps.append((("D", e2, slot, cnt), "raw"))
        for e2 in ENGS:
            if e2 != eng and self.count[e2] > 0:
                deps.append((("E", e2, self.count[e2] - 1), "raw"))
        self._emit_waits(eng, deps)

    def emit(self):
        nc = self.nc
        st = self.stack
        esems = {}
        for e in ENGS:
            nph = (self.count[e] + PHASE_W - 1) // PHASE_W
            esems[e] = [st.enter_context(nc.semaphore(f"s_{e}_{k}")) for k in range(nph)]
        dsems = {}
        for (e, slot) in self.dma_semcnt:
            dsems[(e, slot)] = st.enter_context(nc.semaphore(f"d_{e}_{slot}"))
        csems = {slot: st.enter_context(nc.semaphore(f"c_{slot}")) for slot in self.cc_cnt}

        def run(e, eng):
            for rec in self.ops[e]:
                if rec[0] == "wait":
                    _, key, val = rec
                    if key[0] == "E":
                        ph, loc = divmod(val, PHASE_W)
                        eng.wait_ge(esems[key[1]][ph], loc + 1)
                    elif key[0] == "C":
                        eng.wait_ge(csems[key[1]], val)
                    else:
                        eng.wait_ge(dsems[(key[1], key[2])], 16 * val)
                elif rec[0] == "op":
                    _, fn, idx = rec
                    fn(eng).then_inc(esems[e][idx // PHASE_W], 1)
                elif rec[0] == "cc":
                    _, fn, slot = rec
                    fn(eng).then_inc(csems[slot], 1)
                else:
                    _, fn, slot = rec
                    fn(eng).then_inc(dsems[(e, slot)], 16)

        with nc.Block() as block:
            @block.tensor
            def _(eng):
                run("pe", eng)

            @block.scalar
            def _(eng):
                run("act", eng)

            @block.vector
            def _(eng):
                run("dve", eng)

            @block.gpsimd
            def _(eng):
                run("pool", eng)

            @block.sync
            def _(eng):
                run("sp", eng)
        st.close()


D = 1024
SEQ = 8192
BATCH = 2
DEPTH = 4
NCORE = 8
TOK = 2048
NG = 4
IN_COLS = 7696
C_GQ, C_GK, C_GV, C_GG, C_LR, C_GZ, C_MQ, C_MK, C_MV, C_GL = 0, 512, 1024, 1536, 2048, 2064, 3088, 3600, 4112, 4624
ALPHA = (2 * DEPTH) ** 0.25
NEG = -30000.0
TWO_PI = 2.0 * math.pi


def build(nlayers=DEPTH, dbg=None, stop=None):
    nc = bass.Bass("TRN2", target_bir_lowering=False)
    s = Sched(nc)
    s.alloc_psum()
    NL = nlayers

    def din(name, shape, dt=F32):
        return nc.dram_tensor(name, list(shape), dt, kind="ExternalInput").ap()

    def dout(name, shape, dt=F32):
        return nc.dram_tensor(name, list(shape), dt, kind="ExternalOutput").ap()

    def dscr(name, shape, dt=F32):
        return nc.dram_tensor(name, list(shape), dt).ap()

    dr_n = [0]

    def dT(ap, name):
        dr_n[0] += 1
        return T(ap, [("dr", dr_n[0])], name)

    x_in = din("x", [TOK, D])
    posi_d = din("posi", [128, 16], I32)
    invf_d = din("invf", [128, 64])
    w_in_all = din("w_in", [NL, D, IN_COLS])
    wgu_all = din("wgu", [NL, 17, 512])
    ident_d = din("ident", [128, 128])
    tri_d = din("tri", [128, 128])
    gmask_d = din("gmask", [128, 128])
    lomask_d = din("lomask", [128, 512], BF16)
    himask_d = din("himask", [128, 512], BF16)
    cmask_d = din("cmask", [128, 2, 256], BF16)
    identb_d = din("identb", [128, 128], BF16)
    valid_d = din("valid", [128, 16, 32])
    negfill_d = din("negfill", [128, 16, 32])
    normw_all = din("normw", [NL, 128, 512])
    glng_all = din("glng", [NL, 128, 512])
    glnb_all = din("glnb", [NL, 128, 512])
    wsT_all = din("wsT", [NL, 128, 4, 128])
    triu_d = din("triu", [128, 128])
    bst_all = din("bst", [NL, 128, 4])
    wbr_all = din("wbr", [NL, 3, 512, D])
    wout_all = din("wout", [NL, D, D])
    ln_all = [din(n, [NL, 128, D]) for n in ("ln1g", "ln1b", "ln2g", "ln2b")]
    wff1_all = din("wff1", [NL, D, 4 * D])
    wff2_all = din("wff2", [NL, 4 * D, D])
    cinc_d = din("cinc", [128, 4])
    bmask_d = din("bmask", [128, 4, 4])
    out_d = dout("xout", [TOK, D])
    xbuf = [dscr(f"xbuf{i}", [TOK, D]) for i in range(2)]
    CCF = 32 + 512 + 4
    ccb_in = [[dscr(f"ccb_in{i}_{j}", [128, 4096], BF16) for j in range(4)] for i in range(2)]
    ccb_out = [[dscr(f"ccb_out{i}_{j}", [512, 4096], BF16) for j in range(4)] for i in range(2)]
    ccf_in = [dscr(f"ccf_in{i}", [128, CCF]) for i in range(2)]
    ccf_out = [dscr(f"ccf_out{i}", [512, CCF]) for i in range(2)]
    ccb_in_t = [[dT(a, "ccb_in") for a in row] for row in ccb_in]
    ccb_out_t = [[dT(a, "ccb_out") for a in row] for row in ccb_out]
    ccf_in_t = [dT(a, "ccf_in") for a in ccf_in]
    ccf_out_t = [dT(a, "ccf_out") for a in ccf_out]
    glaE_d = [dscr(f"glaE{g_}", [128, 2048]) for g_ in range(NG)]
    glaK_d = [dscr(f"glaK{g_}", [128, 2048], BF16) for g_ in range(NG)]
    glaV_d = [dscr(f"glaV{g_}", [128, 2048], BF16) for g_ in range(NG)]
    glaE_t = [dT(a, "glaE") for a in glaE_d]
    glaK_t = [dT(a, "glaK") for a in glaK_d]
    glaV_t = [dT(a, "glaV") for a in glaV_d]
    xsrc_t = {}
    for i_ in range(2):
        for g_ in range(NG):
            for t_ in range(4):
                xsrc_t[(i_, g_, t_)] = dT(xbuf[i_], f"xbuf{i_}_{g_}_{t_}")
    dbg_d = {}
    if dbg:
        for nm, shp in dbg.items():
            dbg_d[nm] = dout("dbg_" + nm, shp)

    def dbg_store(nm, t, ap):
        if nm in dbg_d:
            s.dma("sp", lambda e: e.dma_start(out=dbg_d[nm], in_=ap), reads=[t])

    def load(eng, t, ap_out, ap_in):
        s.dma(eng, lambda e: e.dma_start(out=ap_out, in_=ap_in), writes=[t])

    def const(d_ap, shape, dt=F32, name=None, eng="sp"):
        t = s.sb(shape, dt, name)
        load(eng, t, t[:], d_ap)
        return t

    def mm(out_t, out_ap, l_t, l_ap, r_t, r_ap, start=True, stop=True):
        s.op("pe", lambda e: e.matmul(out_ap, l_ap, r_ap, start=start, stop=stop),
             reads=[l_t, r_t], writes=[out_t])

    def tr(out_t, out_ap, in_t, in_ap, idt):
        s.op("pe", lambda e: e.transpose(out_ap, in_ap, idt[:]), reads=[in_t, idt], writes=[out_t])

    def act(out_t, out_ap, in_t, in_ap, func, bias=0.0, scale=1.0, accum=None, extra_r=()):
        w = [out_t] + ([accum[0]] if accum else [])
        kw = {}
        if accum:
            kw["accum_out"] = accum[1]
        s.op("act", lambda e: e.activation(out_ap, in_ap, func, bias=bias, scale=scale, **kw),
             reads=[in_t] + list(extra_r), writes=w)

    def tt(eng, out_t, out_ap, a_t, a_ap, b_t, b_ap, op):
        s.op(eng, lambda e: e.tensor_tensor(out_ap, a_ap, b_ap, op), reads=[a_t, b_t], writes=[out_t])

    def ts(eng, out_t, out_ap, a_t, a_ap, s1, s2, op0, op1=None, extra_r=()):
        if op1 is None:
            s.op(eng, lambda e: e.tensor_scalar(out_ap, a_ap, s1, None, op0),
                 reads=[a_t] + list(extra_r), writes=[out_t])
        else:
            s.op(eng, lambda e: e.tensor_scalar(out_ap, a_ap, s1, s2, op0, op1),
                 reads=[a_t] + list(extra_r), writes=[out_t])

    def stt(eng, out_t, out_ap, a_t, a_ap, sc, b_t, b_ap, op0, op1, extra_r=()):
        s.op(eng, lambda e: e.scalar_tensor_tensor(out_ap, a_ap, sc, b_ap, op0, op1),
             reads=[a_t, b_t] + list(extra_r), writes=[out_t])

    def cp(eng, out_t, out_ap, in_t, in_ap):
        if eng == "act":
            s.op("act", lambda e: e.copy(out_ap, in_ap), reads=[in_t], writes=[out_t])
        else:
            s.op(eng, lambda e: e.tensor_copy(out_ap, in_ap), reads=[in_t], writes=[out_t])

    def rot(shape, dt, name, n):
        tiles = [s.sb(shape, dt, f"{name}{i}") for i in range(n)]
        k = [0]

        def nxt():
            t = tiles[k[0] % n]
            k[0] += 1
            return t
        return nxt

    def pipeline(stages, n=4, between=None):
        ns = len(stages)
        for step in range(n + ns - 1):
            for k in range(ns):
                i = step - k
                if 0 <= i < n:
                    stages[k](i)
            if between is not None:
                between(step)

    cp_rr = [0]

    def cp_any(out_t, out_ap, in_t, in_ap):
        eng = ("act", "act", "dve")[cp_rr[0] % 3]
        cp_rr[0] += 1
        cp(eng, out_t, out_ap, in_t, in_ap)

    ident = const(ident_d, [128, 128], name="ident")
    tri = const(tri_d, [128, 128], name="tri")
    posi = const(posi_d, [128, 16], I32, name="posi")
    invf = const(invf_d, [128, 64], name="invf")
    gmask = const(gmask_d, [128, 128], name="gmask")
    lomask = const(lomask_d, [128, 512], BF16, name="lomask")
    himask = const(himask_d, [128, 512], BF16, name="himask")
    cmask = const(cmask_d, [128, 2, 256], BF16, name="cmask")
    identb = const(identb_d, [128, 128], BF16, name="identb")
    valid = const(valid_d, [128, 16, 32], name="valid")
    negfill = const(negfill_d, [128, 16, 32], name="negfill")
    triu = const(triu_d, [128, 128], name="triu")
    cinc = const(cinc_d, [128, 4], name="cinc")
    bmask = const(bmask_d, [128, 4, 4], name="bmask")
    wgu = s.sb([17, 512], F32, "wgu")
    normw = s.sb([128, 512], F32, "normw")
    glng = s.sb([128, 512], F32, "glng")
    glnb = s.sb([128, 512], F32, "glnb")
    bst = s.sb([128, 4], F32, "bst")
    lnp = [s.sb([128, D], F32, f"ln{i}") for i in range(4)]
    kmT = s.sb([128, 4, 32], BF16, "kmT")
    wsTm = s.sb([128, 4, 128], BF16, "wsTm")

    def load_layer_params(l):
        s.push()
        wsT = s.sb([128, 4, 128], F32, "wsT")
        load("sp", wgu, wgu[:], wgu_all[l])
        load("sp", normw, normw[:], normw_all[l])
        load("sp", glng, glng[:], glng_all[l])
        load("sp", glnb, glnb[:], glnb_all[l])
        load("sp", bst, bst[:], bst_all[l])
        for i in range(4):
            load("sp", lnp[i], lnp[i][:], ln_all[i][l])
        load("sp", wsT, wsT[:], wsT_all[l])
        for g in range(4):
            tt("dve", wsTm, wsTm[:, g, :], wsT, wsT[:, g, :], triu, triu[:], ALU.mult)
        s.pop()

    cosT = s.sb([128, 16, 64], F32, "cos")
    sinT = s.sb([128, 16, 64], F32, "sin")
    s.push()
    posf = s.sb([128, 16], F32, "posf")
    cp("dve", posf, posf[:], posi, posi[:])
    ang = s.sb([128, 16, 64], F32, "ang")
    for t_ in range(16):
        ts("dve", ang, ang[:, t_, :], invf, invf[:], posf[:, t_:t_ + 1], None, ALU.mult, extra_r=[posf])
    kf = s.sb([128, 16, 64], F32, "kf")
    ki = s.sb([128, 16, 64], I32, "ki")
    ts("dve", kf, kf[:], ang, ang[:], 1.0 / TWO_PI, None, ALU.mult)
    cp("dve", ki, ki[:], kf, kf[:])
    cp("dve", kf, kf[:], ki, ki[:])
    C1 = 6.28125
    C2 = TWO_PI - C1
    r0 = s.sb([128, 16, 64], F32, "r0")
    stt("dve", r0, r0[:], kf, kf[:], -C1, ang, ang[:], ALU.mult, ALU.add)
    stt("dve", r0, r0[:], kf, kf[:], -C2, r0, r0[:], ALU.mult, ALU.add)
    m1 = s.sb([128, 16, 64], F32, "m1")
    ts("dve", m1, m1[:], r0, r0[:], math.pi, None, ALU.is_gt)
    stt("dve", r0, r0[:], m1, m1[:], -TWO_PI, r0, r0[:], ALU.mult, ALU.add)
    ts("dve", m1, m1[:], r0, r0[:], -math.pi, None, ALU.is_lt)
    stt("dve", r0, r0[:], m1, m1[:], TWO_PI, r0, r0[:], ALU.mult, ALU.add)
    ts("dve", r0, r0[:], r0, r0[:], math.pi, -math.pi, ALU.min, ALU.max)
    act(sinT, sinT[:], r0, r0[:], AF.Sin)
    stt("dve", m1, m1[:], r0, r0[:], -1.0, r0, r0[:], ALU.mult, ALU.max)
    ts("dve", m1, m1[:], m1, m1[:], -1.0, math.pi / 2, ALU.mult, ALU.add)
    act(cosT, cosT[:], m1, m1[:], AF.Sin)
    s.pop()
    dbg_store("cos", cosT, cosT[:])
    dbg_store("sin", sinT, sinT[:])

    lrT = s.sb([32, 512], F32, "lrT")
    s.op("dve", lambda e: e.memset(lrT[:], 1.0), writes=[lrT])
    Sst = s.sb([128, 4, 128], F32, "S")
    Sbf = s.sb([128, 4, 128], BF16, "Sbf")
    vaug = s.sb([128, 4, 4, 129], BF16, "vaug")
    s.op("dve", lambda e: e.memset(vaug[:], 1.0), writes=[vaug])
    Bacc = s.sb([128, 4], F32, "Bacc")
    kmacc = s.sb([128, 4, 8], F32, "kmacc")

    def init_state_a():
        s.op("dve", lambda e: e.memset(Bacc[:], 0.0), writes=[Bacc])
        s.op("dve", lambda e: e.memset(Sst[:], 0.0), writes=[Sst])

    def init_state_b(cb):
        s.push()
        Sall = s.sb([128, 4, 4, 128], F32, "Sall")
        Ball = s.sb([128, 4, 4], F32, "Ball")
        Ep = s.sb([128, 4, 4], F32, "Ep")
        coef = s.sb([128, 4, 4], F32, "coef")
        f3 = ccf_out[cb].rearrange("(r p) c -> p r c", p=128)
        s.dma("sp", lambda e: e.dma_start(out=Sall[:].rearrange("p r h d -> p r (h d)"), in_=f3[:, :, 32:544]),
              reads=[ccf_out_t[cb]], writes=[Sall])
        s.dma("sp", lambda e: e.dma_start(out=Ball[:], in_=f3[:, :, 544:548]), reads=[ccf_out_t[cb]], writes=[Ball])
        for h in range(4):
            s.dma("pool", lambda e, h=h: e.dma_start(out=kmT[:, h, :].rearrange("p (r b) -> p r b", r=4),
                                                     in_=f3[:, :, h * 8:(h + 1) * 8]),
                  reads=[ccf_out_t[cb]], writes=[kmT])
        s.op("dve", lambda e: e.memset(Ep[:], 0.0), writes=[Ep])
        for p in range(4):
            for r in range(4):
                stt("dve", Ep, Ep[:, p, :], Ball, Ball[:, r, :], bmask[:, p, r:r + 1], Ep, Ep[:, p, :],
                    ALU.mult, ALU.add, extra_r=[bmask])
        act(coef, coef[:], Ep, Ep[:], AF.Exp)
        for p in range(4):
            ts("dve", coef, coef[:, p, :], coef, coef[:, p, :], cinc[:, p:p + 1], None, ALU.mult, extra_r=[cinc])
        s.op("dve", lambda e: e.memset(Sst[:], 0.0), writes=[Sst])
        for p in range(4):
            for h in range(4):
                stt("dve", Sst, Sst[:, h, :], Sall, Sall[:, p, h, :], coef[:, p, h:h + 1], Sst, Sst[:, h, :],
                    ALU.mult, ALU.add, extra_r=[coef])
        s.pop()

    NW = 5
    wpool = [s.sb([128, 8, 512], BF16, f"w{i}") for i in range(NW)]
    w_rr = [0]

    def wtile():
        t = wpool[w_rr[0] % NW]
        w_rr[0] += 1
        return t

    def load_w(src_ap, rows_kc, c0, ncols):
        t = wtile()
        ap_out = t[:, 0:rows_kc, 0:ncols]
        ap_in = src_ap.rearrange("(kc p) c -> p kc c", p=128)[:, :, c0:c0 + ncols]
        s.dma("pool", lambda e: e.dma_start(out=ap_out, in_=ap_in), writes=[t])
        return t

    xgt = [s.sb([128, D], F32, f"xg{i}") for i in range(4)]
    xT = s.sb([128, 8, 512], BF16, "xT")
    brT = [s.sb([128, 4, 512], BF16, f"brT{i}") for i in range(3)]
    ones_f = s.sb([128, 128], F32, "ones_f")
    s.op("dve", lambda e: e.memset(ones_f[:], 1.0), writes=[ones_f])

    def proj_fm(ps, ps_ap, wt, c_lo, M, src=None):
        src = src or xT
        for kc in range(8):
            mm(ps, ps_ap, wt, wt[:, kc, c_lo:c_lo + M], src, src[:, kc, :], start=(kc == 0), stop=(kc == 7))

    def proj_tm(ps, ps_ap, wt, c_lo, N, ti, src=None):
        src = src or xT
        for kc in range(8):
            mm(ps, ps_ap, src, src[:, kc, ti * 128:(ti + 1) * 128], wt, wt[:, kc, c_lo:c_lo + N],
               start=(kc == 0), stop=(kc == 7))

    def transpose_to(src_t, src_fn, dst_t, dst_fn, nblk):
        for j0 in range(0, nblk, 4):
            n = min(4, nblk - j0)
            ps = s.bank()
            for j in range(n):
                tr(ps, ps[:, j * 128:(j + 1) * 128], src_t, src_fn(j0 + j), ident)
            cp_any(dst_t, dst_fn(j0, n), ps, ps[:, 0:n * 128].rearrange("p (a b) -> p a b", a=n))

    ln_pool = [rot([128, 2, 6], F32, "st6", 2), rot([128, 2], F32, "mv", 2), rot([128, 1], F32, "rs", 2)]

    def layer_norm(yt, g_t, b_t):
        st6 = ln_pool[0]()
        mv = ln_pool[1]()
        rs = ln_pool[2]()
        for hf in range(2):
            s.op("dve", lambda e, hf=hf: e.bn_stats(st6[:, hf, :], yt[:, hf * 512:(hf + 1) * 512]),
                 reads=[yt], writes=[st6])
        s.op("dve", lambda e: e.bn_aggr(mv[:], st6[:].rearrange("p a b -> p (a b)")), reads=[st6], writes=[mv])
        act(rs, rs[:], mv, mv[:, 1:2], AF.Sqrt, bias=1e-5)
        s.op("dve", lambda e: e.reciprocal(rs[:], rs[:]), reads=[rs], writes=[rs])
        ts("dve", yt, yt[:], yt, yt[:], mv[:, 0:1], rs[:, 0:1], ALU.subtract, ALU.mult, extra_r=[mv, rs])
        tt("dve", yt, yt[:], yt, yt[:], g_t, g_t[:], ALU.mult)
        tt("dve", yt, yt[:], yt, yt[:], b_t, b_t[:], ALU.add)

    class _Stop(Exception):
        pass

    def chk(tag):
        if stop == tag:
            raise _Stop()

    def run_phase(isB, l):
      cb = l % 2
      w_in_d = w_in_all[l]
      wbr_d = wbr_all[l]
      wout_d = wout_all[l]
      wff1_d = wff1_all[l]
      wff2_d = wff2_all[l]
      x_d = x_in if l == 0 else xbuf[(l - 1) % 2]
      xo_d = out_d if l == NL - 1 else xbuf[l % 2]
      if isB:
          init_state_b(cb)
      else:
          init_state_a()
      cp("act", Sbf, Sbf[:], Sst, Sst[:])
      pend_cc = []
      for g in range(NG):
          for ti in range(4):
              r0_ = g * 512 + ti * 128
              s.dma("sp", lambda e, ti=ti, r0_=r0_: e.dma_start(out=xgt[ti][:], in_=x_d[r0_:r0_ + 128, :]),
                    reads=([] if l == 0 else [xsrc_t[((l - 1) % 2, g, ti)]]), writes=[xgt[ti]])
          for ti in range(4):
              transpose_to(xgt[ti], lambda j, ti=ti: xgt[ti][:, j * 128:(j + 1) * 128],
                           xT, lambda j0, n, ti=ti: xT[:, j0:j0 + n, ti * 128:(ti + 1) * 128], 8)

          chk("xT")
          s.push()
          Eq = s.sb([128, 4, 512], F32, "Eq")
          kinv_tok = s.sb([128, 4, 512], BF16, "kinv_tok")
          v_tok = s.sb([128, 4, 512], BF16, "v_tok")
          if isB:
              Ek = s.sb([128, 4, 512], F32, "Ek")
              wk = load_w(w_in_d, 8, C_GK, 512)
              s.dma("sp", lambda e, g=g: e.dma_start(out=Eq[:].rearrange("p h t -> p (h t)"), in_=glaE_d[g]),
                    reads=[glaE_t[g]], writes=[Eq])
              s.dma("sp", lambda e, g=g: e.dma_start(out=kinv_tok[:].rearrange("p h t -> p (h t)"), in_=glaK_d[g]),
                    reads=[glaK_t[g]], writes=[kinv_tok])
              s.dma("sp", lambda e, g=g: e.dma_start(out=v_tok[:].rearrange("p h t -> p (h t)"), in_=glaV_d[g]),
                    reads=[glaV_t[g]], writes=[v_tok])
              s.op("dve", lambda e: e.reciprocal(Ek[:], Eq[:]), reads=[Eq], writes=[Ek])
          else:
              wlr = load_w(w_in_d, 8, C_LR, 16)
              ps = s.bank()
              proj_fm(ps, ps[0:16, :], wlr, 0, 16)
              cp("dve", lrT, lrT[0:16, :], ps, ps[0:16, :])
              chk("g1")
              wk = load_w(w_in_d, 8, C_GK, 512)
              wv = load_w(w_in_d, 8, C_GV, 512)
              while pend_cc:
                  pend_cc.pop(0)()
              r_sp = rot([128, 512], F32, "sp", 2)
              r_ekt = rot([128, 512], F32, "ekt", 2)
              for ti in range(4):
                  sp_t = r_sp()
                  ps = s.bank()
                  mm(ps, ps[:], lrT, lrT[0:17, ti * 128:(ti + 1) * 128], wgu, wgu[0:17, :])
                  act(sp_t, sp_t[:], ps, ps[:], AF.Exp, scale=-1.0)
                  act(sp_t, sp_t[:], sp_t, sp_t[:], AF.Ln, bias=1.0)
                  chk("g2")
                  psb = s.bank()
                  for h in range(4):
                      mm(psb, psb[:, h * 128:(h + 1) * 128], sp_t, sp_t[:, h * 128:(h + 1) * 128], tri, tri[:])
                  pv = psb[:].rearrange("p (a b) -> p a b", a=4)
                  chk("g2b")
                  act(Eq, Eq[:, :, ti * 128:(ti + 1) * 128], psb, pv, AF.Exp)
                  chk("g2c")
                  if isB:
                      act(Ek, Ek[:, :, ti * 128:(ti + 1) * 128], psb, pv, AF.Exp, scale=-1.0)
                  else:
                      for cc in (63, 127):
                          tt("dve", Bacc, Bacc[:], Bacc, Bacc[:], psb, pv[:, :, cc], ALU.add)
                  chk("g3")
                  pst = s.bank()
                  mm(pst, pst[:], tri, tri[:], sp_t, sp_t[:])
                  ekt = r_ekt()
                  act(ekt, ekt[:], pst, pst[:], AF.Exp, scale=-1.0)
                  psk = s.bank()
                  proj_tm(psk, psk[:], wk, 0, 512, ti)
                  tt("dve", kinv_tok, kinv_tok[:, ti, :], psk, psk[:], ekt, ekt[:], ALU.mult)
                  psv = s.bank()
                  proj_tm(psv, psv[:], wv, 0, 512, ti)
                  cp("act", v_tok, v_tok[:, ti, :], psv, psv[:])
              s.dma("sp", lambda e, g=g: e.dma_start(out=glaE_d[g], in_=Eq[:].rearrange("p h t -> p (h t)")),
                    reads=[Eq], writes=[glaE_t[g]])
              s.dma("sp", lambda e, g=g: e.dma_start(out=glaK_d[g], in_=kinv_tok[:].rearrange("p h t -> p (h t)")),
                    reads=[kinv_tok], writes=[glaK_t[g]])
              s.dma("sp", lambda e, g=g: e.dma_start(out=glaV_d[g], in_=v_tok[:].rearrange("p h t -> p (h t)")),
                    reads=[v_tok], writes=[glaV_t[g]])
          if isB:
              wq = load_w(w_in_d, 8, C_GQ, 512)
              qdec = s.sb([128, 4, 512], BF16, "qdec")
              qlo = s.sb([128, 4, 512], BF16, "qlo")
              qhi = s.sb([128, 4, 512], BF16, "qhi")
              kinvT = s.sb([128, 4, 512], BF16, "kinvT")
              for h in range(4):
                  ps = s.bank()
                  proj_fm(ps, ps[:], wq, h * 128, 128)
                  tt("dve", qdec, qdec[:, h, :], ps, ps[:], Eq, Eq[:, h, :], ALU.mult)
                  tt("dve", qlo, qlo[:, h, :], qdec, qdec[:, h, :], lomask, lomask[:], ALU.mult)
                  tt("dve", qhi, qhi[:, h, :], qdec, qdec[:, h, :], himask, himask[:], ALU.mult)
                  ps = s.bank()
                  proj_fm(ps, ps[:], wk, h * 128, 128)
                  tt("dve", kinvT, kinvT[:, h, :], ps, ps[:], Ek, Ek[:, h, :], ALU.mult)
              wgg = load_w(w_in_d, 8, C_GG, 512)
              GW = s.sb([128, 4, 512], F32, "GW")
              for ti in range(4):
                  ps = s.bank()
                  proj_tm(ps, ps[:], wgg, 0, 512, ti)
                  act(GW, GW[:, ti, :], ps, ps[:], AF.Silu)
                  tt("dve", GW, GW[:, ti, :], GW, GW[:, ti, :], normw, normw[:], ALU.mult)
          chk("g4")
          r_tmpS = rot([128, 128], F32, "tmpS", 8)
          if isB:
              r_attn = rot([128, 128], BF16, "attn", 8)
              r_junk = rot([128, 128], F32, "junk", 2)
              r_ssq = rot([128, 1], F32, "ssq", 8)
              r_gout = rot([128, 512], F32, "gout", 2)
          HS = [slice(h * 128, (h + 1) * 128) for h in range(4)]
          pso_b = [s.psum_banks[h] for h in range(4)]
          pkv_b = [s.psum_banks[4 + h] for h in range(4)]

          def state_update(ti, half):
              rs_ = slice(half * 64, half * 64 + 64)
              cc = ti * 128 + half * 64 + 63
              for h in range(4):
                  mm(pkv_b[h], pkv_b[h][:, 0:128], kinv_tok, kinv_tok[rs_, ti, HS[h]], v_tok, v_tok[rs_, ti, HS[h]])
              for h in range(4):
                  tmpS = r_tmpS()
                  tt("dve", tmpS, tmpS[:], pkv_b[h], pkv_b[h][:, 0:128], Sst, Sst[:, h, :], ALU.add)
                  ts("dve", Sst, Sst[:, h, :], tmpS, tmpS[:], Eq[:, h, cc:cc + 1], None, ALU.mult, extra_r=[Eq])
                  if isB:
                      cp("act", Sbf, Sbf[:, h, :], Sst, Sst[:, h, :])

          pend_g = []
          rec_steps = []
          for ti in range(4):
              tsl = slice(ti * 128, (ti + 1) * 128)
              if isB:
                  gout = r_gout()
                  attns = []
                  for h in range(4):
                      mm(pkv_b[h], pkv_b[h][:, 0:128], kinvT, kinvT[:, h, tsl], qdec, qdec[:, h, tsl])
                  for h in range(4):
                      attn = r_attn()
                      attns.append(attn)
                      tt("dve", attn, attn[:], pkv_b[h], pkv_b[h][:, 0:128], gmask, gmask[:], ALU.mult)
                  for h in range(4):
                      mm(pso_b[h], pso_b[h][:, 0:128], attns[h], attns[h][:], v_tok, v_tok[:, ti, HS[h]],
                         start=True, stop=False)
                      mm(pso_b[h], pso_b[h][:, 0:128], qlo, qlo[:, h, tsl], Sbf, Sbf[:, h, :], start=False, stop=False)
              if isB:
                  state_update(ti, 0)
              else:
                  rec_steps.append(lambda ti=ti: state_update(ti, 0))
              if isB:
                  for h in range(4):
                      mm(pso_b[h], pso_b[h][:, 0:128], qhi, qhi[:, h, tsl], Sbf, Sbf[:, h, :], start=False, stop=True)
              if isB:
                  state_update(ti, 1)
              else:
                  rec_steps.append(lambda ti=ti: state_update(ti, 1))
              if isB:
                  for h in range(4):
                      junk = r_junk()
                      ssq = r_ssq()
                      act(junk, junk[:], pso_b[h], pso_b[h][:, 0:128], AF.Square, accum=(ssq, ssq[:]))
                      act(ssq, ssq[:], ssq, ssq[:], AF.Sqrt, bias=1.28e-4, scale=1.0 / 128)
                      s.op("dve", lambda e, ssq=ssq: e.reciprocal(ssq[:], ssq[:]), reads=[ssq], writes=[ssq])
                      stt("dve", gout, gout[:, HS[h]], pso_b[h], pso_b[h][:, 0:128], ssq[:, 0:1], GW, GW[:, ti, HS[h]],
                          ALU.mult, ALU.mult, extra_r=[ssq])
                  if g == 0 and ti == 0:
                      dbg_store("gla", gout, gout[:])
                  if pend_g:
                      pend_g.pop()()
                  pend_g.append(lambda gout=gout, ti=ti: transpose_to(
                      gout, lambda j: gout[:, j * 128:(j + 1) * 128],
                      brT[0], lambda j0, n: brT[0][:, j0:j0 + n, ti * 128:(ti + 1) * 128], 4))
          if isB:
              pend_g.pop()()
          if isB:
              s.pop()

          chk("gla")
          if isB:
              s.push()
              wzu = load_w(w_in_d, 8, C_GZ, 512)
              wzv = load_w(w_in_d, 8, C_GZ + 512, 512)
              r_u = rot([128, 512], F32, "u", 2)
              r_vg = rot([128, 512], F32, "vg", 2)
              r_vln = rot([128, 512], BF16, "vln", 2)
              r_gm = rot([128, 512], F32, "gm", 2)
              r_st6 = rot([128, 6], F32, "gst6", 2)
              r_mv = rot([128, 2], F32, "gmv", 2)
              r_rs = rot([128, 1], F32, "grs", 2)
              gst = {}

              def gm_s0(ti):
                  u_t = r_u()
                  vg = r_vg()
                  ps = s.bank()
                  proj_tm(ps, ps[:], wzu, 0, 512, ti)
                  act(u_t, u_t[:], ps, ps[:], AF.Gelu)
                  ps = s.bank()
                  proj_tm(ps, ps[:], wzv, 0, 512, ti)
                  act(vg, vg[:], ps, ps[:], AF.Gelu)
                  st6 = r_st6()
                  mv = r_mv()
                  rs = r_rs()
                  s.op("dve", lambda e, st6=st6, vg=vg: e.bn_stats(st6[:], vg[:]), reads=[vg], writes=[st6])
                  s.op("dve", lambda e, st6=st6, mv=mv: e.bn_aggr(mv[:], st6[:]), reads=[st6], writes=[mv])
                  act(rs, rs[:], mv, mv[:, 1:2], AF.Sqrt, bias=1e-5)
                  s.op("dve", lambda e, rs=rs: e.reciprocal(rs[:], rs[:]), reads=[rs], writes=[rs])
                  ts("dve", vg, vg[:], vg, vg[:], mv[:, 0:1], rs[:, 0:1], ALU.subtract, ALU.mult, extra_r=[mv, rs])
                  tt("dve", vg, vg[:], vg, vg[:], glng, glng[:], ALU.mult)
                  vln = r_vln()
                  tt("dve", vln, vln[:], vg, vg[:], glnb, glnb[:], ALU.add)
                  gst[ti] = (u_t, vln)

              def gm_s1(ti):
                  u_t, vln = gst[ti]
                  ps = s.bank()
                  for h in range(4):
                      hs = slice(h * 128, (h + 1) * 128)
                      mm(ps, ps[:, hs], wsTm, wsTm[:, h, :], vln, vln[:, hs])
                  gm = r_gm()
                  for h in range(4):
                      hs = slice(h * 128, (h + 1) * 128)
                      stt("dve", gm, gm[:, hs], ps, ps[:, hs], bst[:, h:h + 1], u_t, u_t[:, hs], ALU.add, ALU.mult,
                          extra_r=[bst])
                  if g == 0 and ti == 0:
                      dbg_store("gmlp", gm, gm[:])
                  gst[ti] = gm

              def gm_s2(ti):
                  gm = gst[ti]
                  transpose_to(gm, lambda j, gm=gm: gm[:, j * 128:(j + 1) * 128],
                               brT[1], lambda j0, n, ti=ti: brT[1][:, j0:j0 + n, ti * 128:(ti + 1) * 128], 4)

              pipeline([gm_s0, gm_s1, gm_s2])
              s.pop()

          chk("gmlp")
          s.push()
          krT = s.sb([128, 4, 512], BF16, "krT")
          if not isB:
              wmk = load_w(w_in_d, 8, C_MK, 512)
              wmv = load_w(w_in_d, 8, C_MV, 512)
          else:
              KT_i = ccb_in[cb][g][:, 0:2048].rearrange("p (h t) -> p h t", h=4)
              V_i = ccb_in[cb][g][:, 2048:4096].rearrange("p (h t c) -> p h t c", h=4, t=4)
              s.dma("sp", lambda e, KT_i=KT_i: e.dma_start(out=krT[:], in_=KT_i),
                    reads=[ccb_in_t[cb][g]], writes=[krT])
              for ti in range(4):
                  s.dma("sp", lambda e, ti=ti, V_i=V_i: e.dma_start(
                      out=vaug[:, ti, :, 0:128], in_=V_i[:, :, ti, :]),
                      reads=[ccb_in_t[cb][g]], writes=[vaug])
          if isB:
              wmq = load_w(w_in_d, 8, C_MQ, 512)
              qrT = s.sb([128, 4, 512], BF16, "qrT")
              nbT = s.sb([128, 4, 512], BF16, "nbT")
              s.op("dve", lambda e, nbT=nbT: e.memset(nbT[:], 0.0), writes=[nbT])

          r_t1 = rot([128, 4, 16], F32, "t1", 4)
          r_t2 = rot([128, 4, 16], F32, "t2", 4)
          r_kr = rot([128, 512], F32, "kr", 2)
          if isB:
              r_qr = rot([128, 512], F32, "qr", 2)
              r_gsc = rot([128, 4, 32], F32, "gsc", 2)
              r_nb = rot([128, 4, 32], F32, "nb", 2)
              r_mx8 = rot([128, 8], F32, "mx8", 4)
              r_km2 = None
          else:
              r_km2 = rot([128, 4], F32, "km2", 2)

          def rope_tm(ps, dst, gti):
              cp("act", dst, dst[:], ps, ps[:])
              pv = ps[:].rearrange("p (h d) -> p h d", h=4)
              dv = dst[:].rearrange("p (h d) -> p h d", h=4)
              cs = cosT[:, gti, :].rearrange("p (h d) -> p h d", h=4)
              sn = sinT[:, gti, :].rearrange("p (h d) -> p h d", h=4)
              t1 = r_t1()
              t2 = r_t2()
              tt("dve", t1, t1[:], ps, pv[:, :, 0:16], cosT, cs, ALU.mult)
              tt("dve", t2, t2[:], ps, pv[:, :, 16:32], sinT, sn, ALU.mult)
              tt("dve", dst, dv[:, :, 0:16], t1, t1[:], t2, t2[:], ALU.subtract)
              t1 = r_t1()
              t2 = r_t2()
              tt("dve", t1, t1[:], ps, pv[:, :, 16:32], cosT, cs, ALU.mult)
              tt("dve", t2, t2[:], ps, pv[:, :, 0:16], sinT, sn, ALU.mult)
              tt("dve", dst, dv[:, :, 16:32], t1, t1[:], t2, t2[:], ALU.add)

          mst = {}

          def mb_s0(ti):
              gti = g * 4 + ti
              kr = None
              if not isB:
                  kr = r_kr()
                  ps = s.bank()
                  proj_tm(ps, ps[:], wmk, 0, 512, ti)
                  rope_tm(ps, kr, gti)
                  ps = s.bank()
                  proj_tm(ps, ps[:], wmv, 0, 512, ti)
                  cp("act", vaug, vaug[:, ti, :, 0:128], ps, ps[:].rearrange("p (h d) -> p h d", h=4))
              qr = None
              if isB:
                  qr = r_qr()
                  ps = s.bank()
                  proj_tm(ps, ps[:], wmq, 0, 512, ti)
                  rope_tm(ps, qr, gti)
              mst[ti] = (kr, qr)

          def mb_s1(ti):
              gti = g * 4 + ti
              tsl = slice(ti * 128, (ti + 1) * 128)
              kr, qr = mst[ti]
              if not isB:
                  pst = s.bank()
                  for h in range(4):
                      tr(pst, pst[:, h * 128:(h + 1) * 128], kr, kr[:, h * 128:(h + 1) * 128], ident)
                  pvw = pst[:].rearrange("p (a b) -> p a b", a=4)
                  cp("act", krT, krT[:, :, tsl], pst, pvw)
              if not isB:
                  blk = gti // 2
                  if gti % 2 == 0:
                      s.op("dve", lambda e, pvw=pvw, blk=blk: e.tensor_reduce(kmacc[:, :, blk], pvw, AX.X, ALU.add),
                           reads=[pst], writes=[kmacc])
                  else:
                      km2 = r_km2()
                      s.op("dve", lambda e, pvw=pvw, km2=km2: e.tensor_reduce(km2[:], pvw, AX.X, ALU.add),
                           reads=[pst], writes=[km2])
                      tt("dve", kmacc, kmacc[:, :, blk], kmacc, kmacc[:, :, blk], km2, km2[:], ALU.add)
              else:
                  pst = s.bank()
                  for h in range(4):
                      tr(pst, pst[:, h * 128:(h + 1) * 128], qr, qr[:, h * 128:(h + 1) * 128], ident)
                  cp("dve", qrT, qrT[:, :, tsl], pst, pst[:].rearrange("p (a b) -> p a b", a=4))
                  psg = s.bank()
                  for h in range(4):
                      mm(psg, psg[:, h * 32:(h + 1) * 32], qrT, qrT[:, h, tsl], kmT, kmT[:, h, :])
                  gsc = r_gsc()
                  nb = r_nb()
                  for h in range(4):
                      mx8 = r_mx8()
                      tt("dve", gsc, gsc[:, h, :], psg, psg[:, h * 32:(h + 1) * 32], valid, valid[:, gti, :], ALU.mult)
                      tt("dve", gsc, gsc[:, h, :], gsc, gsc[:, h, :], negfill, negfill[:, gti, :], ALU.add)
                      s.op("dve", lambda e, h=h, gsc=gsc, mx8=mx8: e.max(mx8[:], gsc[:, h, :]), reads=[gsc], writes=[mx8])
                      ts("dve", nb, nb[:, h, :], gsc, gsc[:, h, :], mx8[:, 2:3], None, ALU.is_ge, extra_r=[mx8])
                      tt("dve", nb, nb[:, h, :], nb, nb[:, h, :], valid, valid[:, gti, :], ALU.mult)
                  ts("dve", nb, nb[:], nb, nb[:], 1.0, -NEG, ALU.subtract, ALU.mult)
                  if g == 0 and ti == 0:
                      dbg_store("nb", nb, nb[:])
                  mst[ti] = nb

          def mb_s2(ti):
              if not isB:
                  return
              tsl = slice(ti * 128, (ti + 1) * 128)
              nb = mst[ti]
              psn = s.bank()
              for h in range(4):
                  tr(psn, psn[0:32, h * 128:(h + 1) * 128], nb, nb[:, h, :], ident)
              cp("act", nbT, nbT[0:32, :, tsl], psn, psn[0:32, :].rearrange("p (a b) -> p a b", a=4))

          def rec_between(step):
              for _ in range(2 if step < 2 else 1):
                  if rec_steps:
                      rec_steps.pop(0)()

          pipeline([mb_s0, mb_s1, mb_s2], between=(None if isB else rec_between))
          while rec_steps:
              rec_steps.pop(0)()
          if not isB:
              KT_o = ccb_in[cb][g][:, 0:2048].rearrange("p (h t) -> p h t", h=4)
              V_o = ccb_in[cb][g][:, 2048:4096].rearrange("p (h t c) -> p h t c", h=4, t=4)
              s.dma("sp", lambda e, KT_o=KT_o: e.dma_start(out=KT_o, in_=krT[:]),
                    reads=[krT], writes=[ccb_in_t[cb][g]])
              for ti in range(4):
                  s.dma("sp", lambda e, ti=ti, V_o=V_o: e.dma_start(
                      out=V_o[:, :, ti, :], in_=vaug[:, ti, :, 0:128]),
                      reads=[vaug], writes=[ccb_in_t[cb][g]])
              pend_cc.append(lambda g=g: s.cc(
                  lambda e: e.collective_compute("AllGather", ALU.bypass, replica_groups=[[0, 1, 2, 3], [4, 5, 6, 7]],
                                                 ins=[ccb_in[cb][g].opt()], outs=[ccb_out[cb][g].opt()]),
                  reads=[ccb_in_t[cb][g]], writes=[ccb_out_t[cb][g]]))
          else:
              SCALE = 128.0 ** -0.5
              p_tiles = [s.sb([128, 512], BF16, f"p_sb{i}") for i in range(5)]
              pacc = s.sb([128, 512], F32, "pacc")
              rl = s.sb([128, 512], F32, "rl")
              NSB = 6
              KTc = [s.sb([128, 2048], BF16, f"KTc{r_}") for r_ in range(4)]
              Vc = [s.sb([128, 16, 129], BF16, f"Vc{r_}") for r_ in range(4)]
              for r_ in range(4):
                  s.op("dve", lambda e, r_=r_: e.memset(Vc[r_][:, :, 128:129], 1.0), writes=[Vc[r_]])
              for h in range(4):
                  for r_ in range(4):
                      if r_ * 16 >= 2 * min(32, 26 + 2 * g):
                          continue
                      for gg in range(4):
                          src = ccb_out[cb][gg][r_ * 128:(r_ + 1) * 128, :]
                          s.dma("sp", lambda e, h=h, src=src, r_=r_, gg=gg: e.dma_start(
                              out=KTc[r_][:, gg * 512:(gg + 1) * 512], in_=src[:, h * 512:(h + 1) * 512]),
                              reads=[ccb_out_t[cb][gg]], writes=[KTc[r_]])
                          s.dma("sp", lambda e, h=h, src=src, r_=r_, gg=gg: e.dma_start(
                              out=Vc[r_][:, gg * 4:(gg + 1) * 4, 0:128],
                              in_=src[:, 2048 + h * 512:2048 + (h + 1) * 512].rearrange("p (t c) -> p t c", t=4)),
                              reads=[ccb_out_t[cb][gg]], writes=[Vc[r_]])
                  oT = s.psum_banks[7]
                  descs = [("g", kt) for kt in range(2 * min(32, 26 + 2 * g))] + [("o", lb, j) for lb in range(2) for j in range(2)]

                  def scores(idx, d):
                      pss = s.psum_banks[idx % NSB]
                      p_sb = p_tiles[idx % 5]
                      if d[0] == "g":
                          kt = d[1]
                          n = kt // 2
                          KTh = KTc[kt // 16]
                          mm(pss, pss[:], KTh, KTh[:, (kt % 16) * 128:(kt % 16 + 1) * 128], qrT, qrT[:, h, :],
                             start=True, stop=False)
                          mm(pss, pss[:], identb, identb[:, n:n + 1].to_broadcast([128, 128]), nbT, nbT[:, h, :],
                             start=False, stop=True)
                          act(p_sb, p_sb[:], pss, pss[:], AF.Exp, scale=SCALE)
                      else:
                          _, lb, j = d
                          tk = 2 * lb + j
                          mm(pss, pss[:, 0:256], krT, krT[:, h, tk * 128:(tk + 1) * 128], qrT,
                             qrT[:, h, lb * 256:(lb + 1) * 256], start=True, stop=False)
                          mm(pss, pss[:, 0:256], identb, identb[:], cmask, cmask[:, j, :], start=False, stop=True)
                          act(p_sb, p_sb[:, 0:256], pss, pss[:, 0:256], AF.Exp, scale=SCALE)
                      return p_sb

                  def pv(idx, d, p_sb):
                      if d[0] == "g":
                          kt = d[1]
                          mm(oT, oT[:], Vc[kt // 16], Vc[kt // 16][:, kt % 16, 0:128], p_sb, p_sb[:],
                             start=(kt == 0), stop=False)
                          if idx == 0:
                              cp("dve", pacc, pacc[:], p_sb, p_sb[:])
                          else:
                              tt("dve", pacc, pacc[:], pacc, pacc[:], p_sb, p_sb[:], ALU.add)
                      else:
                          _, lb, j = d
                          tk = 2 * lb + j
                          qs = slice(lb * 256, (lb + 1) * 256)
                          mm(oT, oT[:, qs], vaug, vaug[:, tk, h, 0:128], p_sb, p_sb[:, 0:256],
                             start=False, stop=(lb == 1 and j == 1))
                          tt("dve", pacc, pacc[:, qs], pacc, pacc[:, qs], p_sb, p_sb[:, 0:256], ALU.add)

                  LOOK = 2
                  pend = []
                  for idx, d in enumerate(descs):
                      p_cur = scores(idx, d)
                      pend.append((idx, d, p_cur))
                      if len(pend) > LOOK:
                          pv(*pend.pop(0))
                  while pend:
                      pv(*pend.pop(0))
                  psl = s.psum_banks[6]
                  mm(psl, psl[:], ones_f, ones_f[:], pacc, pacc[:])
                  s.op("dve", lambda e: e.reciprocal(rl[:], psl[:]), reads=[psl], writes=[rl])
                  tt("dve", brT[2], brT[2][:, h, :], oT, oT[:], rl, rl[:], ALU.mult)
          s.pop()
          if not isB:
              s.pop()
              continue

          chk("moba")
          s.push()
          macc = s.sb([128, 4, D], F32, "macc")
          r_sig = rot([128, 512], F32, "sig", 3)
          for br in range(3):
              wb0 = load_w(wbr_d[br], 4, 0, 512)
              wb1 = load_w(wbr_d[br], 4, 512, 512)
              wg0 = load_w(w_in_d, 8, C_GL + br * 1024, 512)
              wg1 = load_w(w_in_d, 8, C_GL + br * 1024 + 512, 512)
              for ti in range(4):
                  tsl = slice(ti * 128, (ti + 1) * 128)
                  for hf, (wb, wgt) in enumerate(((wb0, wg0), (wb1, wg1))):
                      cs_ = slice(hf * 512, (hf + 1) * 512)
                      psa = s.bank()
                      for kc in range(4):
                          mm(psa, psa[:], brT[br], brT[br][:, kc, tsl], wb, wb[:, kc, 0:512], start=(kc == 0), stop=(kc == 3))
                      psg = s.bank()
                      proj_tm(psg, psg[:], wgt, 0, 512, ti)
                      sig = r_sig()
                      act(sig, sig[:], psg, psg[:], AF.Sigmoid)
                      if br == 0:
                          tt("dve", macc, macc[:, ti, cs_], psa, psa[:], sig, sig[:], ALU.mult)
                      else:
                          tt("dve", sig, sig[:], psa, psa[:], sig, sig[:], ALU.mult)
                          tt("dve", macc, macc[:, ti, cs_], macc, macc[:, ti, cs_], sig, sig[:], ALU.add)
          mT = s.sb([128, 8, 512], BF16, "mT")
          for ti in range(4):
              transpose_to(macc, lambda j, ti=ti: macc[:, ti, j * 128:(j + 1) * 128],
                           mT, lambda j0, n, ti=ti: mT[:, j0:j0 + n, ti * 128:(ti + 1) * 128], 8)
          wo = [load_w(wout_d, 8, 0, 512), load_w(wout_d, 8, 512, 512)]
          for ti in range(4):
              for hf in range(2):
                  cs_ = slice(hf * 512, (hf + 1) * 512)
                  ps = s.bank()
                  proj_tm(ps, ps[:], wo[hf], 0, 512, ti, src=mT)
                  stt("dve", xgt[ti], xgt[ti][:, cs_], xgt[ti], xgt[ti][:, cs_], ALPHA, ps, ps[:], ALU.mult, ALU.add)
              layer_norm(xgt[ti], lnp[0], lnp[1])
          s.pop()
          if g == 0:
              dbg_store("x1", xgt[0], xgt[0][:])

          chk("merge")
          s.push()
          for ti in range(4):
              transpose_to(xgt[ti], lambda j, ti=ti: xgt[ti][:, j * 128:(j + 1) * 128],
                           xT, lambda j0, n, ti=ti: xT[:, j0:j0 + n, ti * 128:(ti + 1) * 128], 8)
          hT = s.sb([128, 32, 512], BF16, "hT")
          r_hsq = rot([128, 512], F32, "hsq", 3)
          for fcc in range(8):
              w1 = load_w(wff1_d, 8, fcc * 512, 512)
              for j in range(4):
                  fc = fcc * 4 + j
                  ps = s.bank()
                  proj_fm(ps, ps[:], w1, j * 128, 128)
                  hsq = r_hsq()
                  act(hsq, hsq[:], ps, ps[:], AF.Square)
                  stt("dve", hT, hT[:, fc, :], ps, ps[:], 0.0, hsq, hsq[:], ALU.is_gt, ALU.mult)
          w2c = [s.sb([128, 8, 512], BF16, f"w2c{q4}") for q4 in range(4)]
          for hf in range(2):
              cs_ = slice(hf * 512, (hf + 1) * 512)
              for q4 in range(4):
                  s.dma("pool", lambda e, hf=hf, q4=q4: e.dma_start(
                      out=w2c[q4][:],
                      in_=wff2_d.rearrange("(fc p) c -> p fc c", p=128)[:, q4 * 8:(q4 + 1) * 8, hf * 512:(hf + 1) * 512]),
                      writes=[w2c[q4]])
              facc = [s.psum_banks[4 + ti] for ti in range(4)]
              for q4 in range(4):
                  for ti in range(4):
                      for f8 in range(8):
                          fc = q4 * 8 + f8
                          mm(facc[ti], facc[ti][:], hT, hT[:, fc, ti * 128:(ti + 1) * 128], w2c[q4], w2c[q4][:, f8, :],
                             start=(fc == 0), stop=(fc == 31))
              for ti in range(4):
                  stt("dve", xgt[ti], xgt[ti][:, cs_], xgt[ti], xgt[ti][:, cs_], ALPHA, facc[ti], facc[ti][:],
                      ALU.mult, ALU.add)
          for ti in range(4):
              layer_norm(xgt[ti], lnp[2], lnp[3])
              r0_ = g * 512 + ti * 128
              s.dma("sp", lambda e, ti=ti, r0_=r0_: e.dma_start(out=xo_d[r0_:r0_ + 128, :], in_=xgt[ti][:]),
                    reads=[xgt[ti]], writes=([] if l == NL - 1 else [xsrc_t[(l % 2, g, ti)]]))
          s.pop()

      if not isB:
          ts("dve", kmacc, kmacc[:], kmacc, kmacc[:], 1.0 / 256, None, ALU.mult)
          s.dma("sp", lambda e: e.dma_start(out=ccf_in[cb][:, 0:32].rearrange("p (h b) -> p h b", h=4), in_=kmacc[:]),
                reads=[kmacc], writes=[ccf_in_t[cb]])
          s.dma("sp", lambda e: e.dma_start(out=ccf_in[cb][:, 32:544].rearrange("p (h d) -> p h d", h=4), in_=Sst[:]),
                reads=[Sst], writes=[ccf_in_t[cb]])
          s.dma("sp", lambda e: e.dma_start(out=ccf_in[cb][:, 544:548], in_=Bacc[:]), reads=[Bacc], writes=[ccf_in_t[cb]])
          GRP = [[0, 1, 2, 3], [4, 5, 6, 7]]
          while pend_cc:
              pend_cc.pop(0)()
          s.cc(lambda e: e.collective_compute("AllGather", ALU.bypass, replica_groups=GRP,
                                              ins=[ccf_in[cb].opt()], outs=[ccf_out[cb].opt()]),
               reads=[ccf_in_t[cb]], writes=[ccf_out_t[cb]])

    try:
        chk("setup")
        for l in range(NL):
            load_layer_params(l)
            run_phase(False, l)
            chk(f"A{l}")
            run_phase(True, l)
            chk(f"B{l}")
    except _Stop:
        pass
    s.wait_all("sp")
    s.emit()
    if dbg is not None:
        print("sb_peak", s.sb_peak, "counts", s.count, "dma", s.dma_n, flush=True)
    return nc


_PROG = {}


def _prog(nl):
    if nl not in _PROG:
        _PROG[nl] = build(nl)
    return _PROG[nl]


def _consts():
    c = {}
    c["ident"] = np.eye(128, dtype=np.float32)
    s_ = np.arange(128)[:, None]
    t_ = np.arange(128)[None, :]
    same = (s_ // 64) == (t_ // 64)
    c["tri"] = np.where((s_ <= t_) & same, -1.0 / 16.0, 0.0).astype(np.float32)
    c["gmask"] = np.where((s_ <= t_) & same, 1.0, 0.0).astype(np.float32)
    c["triu"] = np.where(s_ <= t_, 1.0, 0.0).astype(np.float32)
    tcol = np.arange(512)[None, :] % 128
    c["lomask"] = np.broadcast_to(np.where(tcol < 64, 1.0, 0.0), (128, 512)).astype(ml_dtypes.bfloat16)
    c["himask"] = np.broadcast_to(np.where(tcol >= 64, 1.0, 0.0), (128, 512)).astype(ml_dtypes.bfloat16)
    cm = np.zeros((128, 2, 256), np.float32)
    for j in range(2):
        ks = j * 128 + np.arange(128)[:, None]
        cm[:, j, :] = np.where(ks <= np.arange(256)[None, :], 0.0, NEG)
    c["cmask"] = cm.astype(ml_dtypes.bfloat16)
    c["identb"] = np.eye(128, dtype=np.float32).astype(ml_dtypes.bfloat16)
    half = 16
    inv = (1.0 / (np.float32(500000.0) ** (np.arange(half, dtype=np.float32) * np.float32(2.0 / 32)))).astype(np.float32)
    c["invf"] = np.broadcast_to(np.tile(inv, 4)[None, :], (128, 64)).astype(np.float32).copy()
    return c


def _core_consts(q):
    valid = np.zeros((128, 16, 32), np.float32)
    for ti in range(16):
        own = q * 8 + ti // 2
        valid[:, ti, :own] = 1.0
    negfill = ((valid - 1.0) * 1e30).astype(np.float32)
    cinc = np.zeros((128, 4), np.float32)
    cinc[:, :q] = 1.0
    bmask = np.zeros((128, 4, 4), np.float32)
    for p in range(4):
        for r in range(4):
            if p < r < q:
                bmask[:, p, r] = 1.0
    return {"valid": valid, "negfill": negfill, "cinc": cinc, "bmask": bmask}


def _bcl(v, n=128):
    v = np.asarray(v, np.float32)
    return np.ascontiguousarray(np.broadcast_to(v[:, None, :], (v.shape[0], n, v.shape[1])))


def shard_inputs(x, positions):
    x = np.ascontiguousarray(np.asarray(x, dtype=np.float32))
    positions = np.asarray(positions).astype(np.int32)
    xs = [np.ascontiguousarray(x[c // 4, (c % 4) * TOK:(c % 4 + 1) * TOK, :]) for c in range(NCORE)]
    posi = [np.ascontiguousarray(positions[c // 4, (c % 4) * TOK:(c % 4 + 1) * TOK].reshape(16, 128).T)
            for c in range(NCORE)]
    return xs, posi


def make_in_maps(x, positions, P, nl=DEPTH):
    f = lambda a: np.ascontiguousarray(np.asarray(a, dtype=np.float32)[:nl])
    cst = _consts()
    xs, posi = shard_inputs(x, positions)
    shared = dict(cst)
    shared.update({
        "w_in": f(P["w_in"]),
        "wgu": np.ascontiguousarray(np.concatenate([f(P["w_gate_up"]), f(P["b_gate"])[:, None, :]], axis=1)),
        "normw": _bcl(np.tile(f(P["gla_norm_w"]), (1, 4))),
        "glng": _bcl(f(P["gmlp_ln_g"])), "glnb": _bcl(f(P["gmlp_ln_b"])),
        "wsT": np.ascontiguousarray(f(P["gmlp_w_s"]).transpose(0, 3, 1, 2)),
        "bst": np.ascontiguousarray(f(P["gmlp_b_s"]).transpose(0, 2, 1)),
        "wbr": f(P["w_branch"]), "wout": f(P["w_out"]),
        "ln1g": _bcl(f(P["ln1_g"])), "ln1b": _bcl(f(P["ln1_b"])),
        "ln2g": _bcl(f(P["ln2_g"])), "ln2b": _bcl(f(P["ln2_b"])),
        "wff1": f(P["w_ff1"]), "wff2": f(P["w_ff2"]),
    })
    in_maps = []
    for c in range(NCORE):
        m = dict(shared, x=xs[c], posi=posi[c])
        m.update(_core_consts(c % 4))
        in_maps.append(m)
    return in_maps


def kernel(x, positions, w_in, w_gate_up, b_gate, gla_norm_w, gmlp_ln_g, gmlp_ln_b, gmlp_w_s, gmlp_b_s,
           w_branch, w_out, ln1_g, ln1_b, w_ff1, w_ff2, ln2_g, ln2_b):
    P = dict(w_in=w_in, w_gate_up=w_gate_up, b_gate=b_gate, gla_norm_w=gla_norm_w, gmlp_ln_g=gmlp_ln_g,
             gmlp_ln_b=gmlp_ln_b, gmlp_w_s=gmlp_w_s, gmlp_b_s=gmlp_b_s, w_branch=w_branch, w_out=w_out,
             ln1_g=ln1_g, ln1_b=ln1_b, w_ff1=w_ff1, w_ff2=w_ff2, ln2_g=ln2_g, ln2_b=ln2_b)
    in_maps = make_in_maps(x, positions, P, DEPTH)
    res = run_bass_kernel_spmd(_prog(DEPTH), in_maps, core_ids=list(range(NCORE))).results
    out = np.zeros((BATCH, SEQ, D), np.float32)
    for c in range(NCORE):
        out[c // 4, (c % 4) * TOK:(c % 4 + 1) * TOK, :] = np.asarray(res[c]["xout"], np.float32)
    return out
```

```python
from contextlib import ExitStack
import math
import numpy as np
import ml_dtypes
import concourse.bass as bass
import concourse.mybir as mybir
from concourse.bass_utils import run_bass_kernel_spmd

F32 = mybir.dt.float32
BF16 = mybir.dt.bfloat16
I32 = mybir.dt.int32
AF = mybir.ActivationFunctionType
ALU = mybir.AluOpType
AX = mybir.AxisListType

PAGE = 512
SB_BASE = 16896
SBUF_BYTES = 229376 - 512
PHASE_W = 4000
NDMASEM = 12
ENGS = ("pe", "act", "dve", "pool", "sp")


class T:
    def __init__(self, ap, pages, name):
        self.ap = ap
        self.pages = pages
        self.name = name

    def __getitem__(self, idx):
        return self.ap[idx]


class Sched:
    def __init__(self, nc):
        self.nc = nc
        self.ops = {e: [] for e in ENGS}
        self.count = {e: 0 for e in ENGS}
        self.dma_n = {e: 0 for e in ENGS}
        self.dma_semcnt = {}
        self.cc_n = 0
        self.cc_cnt = {}
        self.known = {e: {} for e in ENGS}
        self.wr = {}
        self.rd = {}
        self.sb_off = SB_BASE
        self.sb_stack = []
        self.memo = {}
        self.sb_peak = 0
        self.ntens = 0
        self.psum_banks = []
        self.stack = ExitStack()

    def sb(self, shape, dtype, name=None):
        esz = {F32: 4, BF16: 2, I32: 4}[dtype]
        nbytes = int(np.prod(shape[1:])) * esz
        off = (self.sb_off + PAGE - 1) // PAGE * PAGE
        assert off + nbytes <= SBUF_BYTES, f"SBUF overflow {name} {off}+{nbytes}"
        self.sb_off = off + nbytes
        self.sb_peak = max(self.sb_peak, self.sb_off)
        mkey = (off, tuple(shape), str(dtype))
        if mkey in self.memo:
            return self.memo[mkey]
        self.ntens += 1
        nm = f"{name or 't'}_{self.ntens}"
        h = self.nc.alloc_sbuf_tensor_at(nm, list(shape), dtype, offset=off)
        pages = range(off // PAGE, (off + nbytes + PAGE - 1) // PAGE)
        t = T(h.ap(), [("sb", p) for p in pages], nm)
        self.memo[mkey] = t
        return t

    def push(self):
        self.sb_stack.append(self.sb_off)

    def pop(self):
        self.sb_off = self.sb_stack.pop()

    def alloc_psum(self):
        for i in range(8):
            h = self.stack.enter_context(self.nc.psum_tensor(f"psb{i}", [128, 512], F32))
            self.psum_banks.append(T(h.ap(), [("ps", i)], f"psb{i}"))
        self.ps_rr = 0

    def bank(self, lo=0, hi=8):
        b = lo + self.ps_rr % (hi - lo)
        self.ps_rr += 1
        return self.psum_banks[b]

    def _deps(self, reads, writes):
        deps = []
        for t in reads:
            for p in t.pages:
                w = self.wr.get(p)
                if w is not None:
                    deps.append((w, "raw"))
        for t in writes:
            for p in t.pages:
                w = self.wr.get(p)
                if w is not None:
                    deps.append((w, "waw"))
                for r in self.rd.get(p, {}).values():
                    deps.append((r, "war"))
        return deps

    def _emit_waits(self, eng, deps):
        need = {}
        for tok, kind in deps:
            if tok[0] == "E":
                _, e2, idx = tok
                if e2 == eng and eng == "pe":
                    continue
                key = ("E", e2)
                val = idx
            elif tok[0] == "C":
                _, slot, val = tok
                key = ("C", slot)
            else:
                _, e2, slot, val = tok
                key = ("D", e2, slot)
            if self.known[eng].get(key, -1) >= val:
                continue
            if need.get(key, -1) < val:
                need[key] = val
        for key, val in need.items():
            self.known[eng][key] = val
            self.ops[eng].append(("wait", key, val))

    def _mark(self, tok, key, reads, writes):
        for t in reads:
            for p in t.pages:
                self.rd.setdefault(p, {})[key] = tok
        for t in writes:
            for p in t.pages:
                self.wr[p] = tok
                self.rd[p] = {}

    def op(self, eng, fn, reads=(), writes=()):
        pr = [t for t in reads if t.pages[0][0] == "ps" and t not in writes]
        if pr:
            writes = list(writes) + pr
        deps = self._deps(reads, writes)
        self._emit_waits(eng, deps)
        idx = self.count[eng]
        self.count[eng] += 1
        self.ops[eng].append(("op", fn, idx))
        self._mark(("E", eng, idx), ("E", eng), reads, writes)

    def dma(self, eng, fn, reads=(), writes=()):
        deps = self._deps(reads, writes)
        n = self.dma_n[eng]
        self.dma_n[eng] += 1
        slot = n % NDMASEM
        prev = self.dma_semcnt.get((eng, slot), 0)
        if prev > 0:
            deps.append((("D", eng, slot, prev), "raw"))
        self._emit_waits(eng, deps)
        cnt = prev + 1
        self.dma_semcnt[(eng, slot)] = cnt
        self.ops[eng].append(("dma", fn, slot))
        tok = ("D", eng, slot, cnt)
        self._mark(tok, ("D", eng, slot, cnt), reads, writes)

    def cc(self, fn, reads=(), writes=()):
        eng = "pool"
        deps = self._deps(reads, writes)
        slot = self.cc_n % 4
        self.cc_n += 1
        prev = self.cc_cnt.get(slot, 0)
        if prev > 0:
            deps.append((("C", slot, prev), "raw"))
        self._emit_waits(eng, deps)
        cnt = prev + 1
        self.cc_cnt[slot] = cnt
        self.ops[eng].append(("cc", fn, slot))
        tok = ("C", slot, cnt)
        self._mark(tok, tok, reads, writes)

    def wait_all(self, eng):
        deps = []
        for slot, cnt in self.cc_cnt.items():
            deps.append((("C", slot, cnt), "raw"))
        for (e2, slot), cnt in self.dma_semcnt.items():
            deps.append((("D", e2, slot, cnt), "raw"))
        for e2 in ENGS:
            if e2 != eng and self.count[e2] > 0:
                deps.append((("E", e2, self.count[e2] - 1), "raw"))
        self._emit_waits(eng, deps)

    def emit(self):
        nc = self.nc
        st = self.stack
        esems = {}
        for e in ENGS:
            nph = (self.count[e] + PHASE_W - 1) // PHASE_W
            esems[e] = [st.enter_context(nc.semaphore(f"s_{e}_{k}")) for k in range(nph)]
        dsems = {}
        for (e, slot) in self.dma_semcnt:
            dsems[(e, slot)] = st.enter_context(nc.semaphore(f"d_{e}_{slot}"))
        csems = {slot: st.enter_context(nc.semaphore(f"c_{slot}")) for slot in self.cc_cnt}

        def run(e, eng):
            for rec in self.ops[e]:
                if rec[0] == "wait":
                    _, key, val = rec
                    if key[0] == "E":
                        ph, loc = divmod(val, PHASE_W)
                        eng.wait_ge(esems[key[1]][ph], loc + 1)
                    elif key[0] == "C":
                        eng.wait_ge(csems[key[1]], val)
                    else:
                        eng.wait_ge(dsems[(key[1], key[2])], 16 * val)
                elif rec[0] == "op":
                    _, fn, idx = rec
                    fn(eng).then_inc(esems[e][idx // PHASE_W], 1)
                elif rec[0] == "cc":
                    _, fn, slot = rec
                    fn(eng).then_inc(csems[slot], 1)
                else:
                    _, fn, slot = rec
                    fn(eng).then_inc(dsems[(e, slot)], 16)

        with nc.Block() as block:
            @block.tensor
            def _(eng):
                run("pe", eng)

            @block.scalar
            def _(eng):
                run("act", eng)

            @block.vector
            def _(eng):
                run("dve", eng)

            @block.gpsimd
            def _(eng):
                run("pool", eng)

            @block.sync
            def _(eng):
                run("sp", eng)
        st.close()


D = 1024
SEQ = 8192
BATCH = 2
DEPTH = 4
NCORE = 8
TOK = 2048
NG = 4
IN_COLS = 7696
C_GQ, C_GK, C_GV, C_GG, C_LR, C_GZ, C_MQ, C_MK, C_MV, C_GL = 0, 512, 1024, 1536, 2048, 2064, 3088, 3600, 4112, 4624
ALPHA = (2 * DEPTH) ** 0.25
NEG = -30000.0
TWO_PI = 2.0 * math.pi


def build(nlayers=DEPTH, dbg=None, stop=None):
    nc = bass.Bass("TRN2", target_bir_lowering=False)
    s = Sched(nc)
    s.alloc_psum()
    NL = nlayers

    def din(name, shape, dt=F32):
        return nc.dram_tensor(name, list(shape), dt, kind="ExternalInput").ap()

    def dout(name, shape, dt=F32):
        return nc.dram_tensor(name, list(shape), dt, kind="ExternalOutput").ap()

    def dscr(name, shape, dt=F32):
        return nc.dram_tensor(name, list(shape), dt).ap()

    dr_n = [0]

    def dT(ap, name):
        dr_n[0] += 1
        return T(ap, [("dr", dr_n[0])], name)

    x_in = din("x", [TOK, D])
    posi_d = din("posi", [128, 16], I32)
    invf_d = din("invf", [128, 64])
    w_in_all = din("w_in", [NL, D, IN_COLS])
    wgu_all = din("wgu", [NL, 17, 512])
    ident_d = din("ident", [128, 128])
    tri_d = din("tri", [128, 128])
    gmask_d = din("gmask", [128, 128])
    lomask_d = din("lomask", [128, 512], BF16)
    himask_d = din("himask", [128, 512], BF16)
    cmask_d = din("cmask", [128, 2, 256], BF16)
    identb_d = din("identb", [128, 128], BF16)
    valid_d = din("valid", [128, 16, 32])
    negfill_d = din("negfill", [128, 16, 32])
    normw_all = din("normw", [NL, 128, 512])
    glng_all = din("glng", [NL, 128, 512])
    glnb_all = din("glnb", [NL, 128, 512])
    wsT_all = din("wsT", [NL, 128, 4, 128])
    triu_d = din("triu", [128, 128])
    bst_all = din("bst", [NL, 128, 4])
    wbr_all = din("wbr", [NL, 3, 512, D])
    wout_all = din("wout", [NL, D, D])
    ln_all = [din(n, [NL, 128, D]) for n in ("ln1g", "ln1b", "ln2g", "ln2b")]
    wff1_all = din("wff1", [NL, D, 4 * D])
    wff2_all = din("wff2", [NL, 4 * D, D])
    cinc_d = din("cinc", [128, 4])
    bmask_d = din("bmask", [128, 4, 4])
    out_d = dout("xout", [TOK, D])
    xbuf = [dscr(f"xbuf{i}", [TOK, D]) for i in range(2)]
    CCF = 32 + 512 + 4
    ccb_in = [[dscr(f"ccb_in{i}_{j}", [128, 4096], BF16) for j in range(4)] for i in range(2)]
    ccb_out = [[dscr(f"ccb_out{i}_{j}", [512, 4096], BF16) for j in range(4)] for i in range(2)]
    ccf_in = [dscr(f"ccf_in{i}", [128, CCF]) for i in range(2)]
    ccf_out = [dscr(f"ccf_out{i}", [512, CCF]) for i in range(2)]
    ccb_in_t = [[dT(a, "ccb_in") for a in row] for row in ccb_in]
    ccb_out_t = [[dT(a, "ccb_out") for a in row] for row in ccb_out]
    ccf_in_t = [dT(a, "ccf_in") for a in ccf_in]
    ccf_out_t = [dT(a, "ccf_out") for a in ccf_out]
    glaE_d = [dscr(f"glaE{g_}", [128, 2048]) for g_ in range(NG)]
    glaK_d = [dscr(f"glaK{g_}", [128, 2048], BF16) for g_ in range(NG)]
    glaV_d = [dscr(f"glaV{g_}", [128, 2048], BF16) for g_ in range(NG)]
    glaE_t = [dT(a, "glaE") for a in glaE_d]
    glaK_t = [dT(a, "glaK") for a in glaK_d]
    glaV_t = [dT(a, "glaV") for a in glaV_d]
    xsrc_t = {}
    for i_ in range(2):
        for g_ in range(NG):
            for t_ in range(4):
                xsrc_t[(i_, g_, t_)] = dT(xbuf[i_], f"xbuf{i_}_{g_}_{t_}")
    dbg_d = {}
    if dbg:
        for nm, shp in dbg.items():
            dbg_d[nm] = dout("dbg_" + nm, shp)

    def dbg_store(nm, t, ap):
        if nm in dbg_d:
            s.dma("sp", lambda e: e.dma_start(out=dbg_d[nm], in_=ap), reads=[t])

    def load(eng, t, ap_out, ap_in):
        s.dma(eng, lambda e: e.dma_start(out=ap_out, in_=ap_in), writes=[t])

    def const(d_ap, shape, dt=F32, name=None, eng="sp"):
        t = s.sb(shape, dt, name)
        load(eng, t, t[:], d_ap)
        return t

    def mm(out_t, out_ap, l_t, l_ap, r_t, r_ap, start=True, stop=True):
        s.op("pe", lambda e: e.matmul(out_ap, l_ap, r_ap, start=start, stop=stop),
             reads=[l_t, r_t], writes=[out_t])

    def tr(out_t, out_ap, in_t, in_ap, idt):
        s.op("pe", lambda e: e.transpose(out_ap, in_ap, idt[:]), reads=[in_t, idt], writes=[out_t])

    def act(out_t, out_ap, in_t, in_ap, func, bias=0.0, scale=1.0, accum=None, extra_r=()):
        w = [out_t] + ([accum[0]] if accum else [])
        kw = {}
        if accum:
            kw["accum_out"] = accum[1]
        s.op("act", lambda e: e.activation(out_ap, in_ap, func, bias=bias, scale=scale, **kw),
             reads=[in_t] + list(extra_r), writes=w)

    def tt(eng, out_t, out_ap, a_t, a_ap, b_t, b_ap, op):
        s.op(eng, lambda e: e.tensor_tensor(out_ap, a_ap, b_ap, op), reads=[a_t, b_t], writes=[out_t])

    def ts(eng, out_t, out_ap, a_t, a_ap, s1, s2, op0, op1=None, extra_r=()):
        if op1 is None:
            s.op(eng, lambda e: e.tensor_scalar(out_ap, a_ap, s1, None, op0),
                 reads=[a_t] + list(extra_r), writes=[out_t])
        else:
            s.op(eng, lambda e: e.tensor_scalar(out_ap, a_ap, s1, s2, op0, op1),
                 reads=[a_t] + list(extra_r), writes=[out_t])

    def stt(eng, out_t, out_ap, a_t, a_ap, sc, b_t, b_ap, op0, op1, extra_r=()):
        s.op(eng, lambda e: e.scalar_tensor_tensor(out_ap, a_ap, sc, b_ap, op0, op1),
             reads=[a_t, b_t] + list(extra_r), writes=[out_t])

    def cp(eng, out_t, out_ap, in_t, in_ap):
        if eng == "act":
            s.op("act", lambda e: e.copy(out_ap, in_ap), reads=[in_t], writes=[out_t])
        else:
            s.op(eng, lambda e: e.tensor_copy(out_ap, in_ap), reads=[in_t], writes=[out_t])

    def rot(shape, dt, name, n):
        tiles = [s.sb(shape, dt, f"{name}{i}") for i in range(n)]
        k = [0]

        def nxt():
            t = tiles[k[0] % n]
            k[0] += 1
            return t
        return nxt

    def pipeline(stages, n=4, between=None):
        ns = len(stages)
        for step in range(n + ns - 1):
            for k in range(ns):
                i = step - k
                if 0 <= i < n:
                    stages[k](i)
            if between is not None:
                between(step)

    cp_rr = [0]

    def cp_any(out_t, out_ap, in_t, in_ap):
        eng = "act"
        cp_rr[0] += 1
        cp(eng, out_t, out_ap, in_t, in_ap)

    ident = const(ident_d, [128, 128], name="ident")
    tri = const(tri_d, [128, 128], name="tri")
    posi = const(posi_d, [128, 16], I32, name="posi")
    invf = const(invf_d, [128, 64], name="invf")
    gmask = const(gmask_d, [128, 128], name="gmask")
    lomask = const(lomask_d, [128, 512], BF16, name="lomask")
    himask = const(himask_d, [128, 512], BF16, name="himask")
    cmask = const(cmask_d, [128, 2, 256], BF16, name="cmask")
    identb = const(identb_d, [128, 128], BF16, name="identb")
    valid = const(valid_d, [128, 16, 32], name="valid")
    negfill = const(negfill_d, [128, 16, 32], name="negfill")
    triu = const(triu_d, [128, 128], name="triu")
    cinc = const(cinc_d, [128, 4], name="cinc")
    bmask = const(bmask_d, [128, 4, 4], name="bmask")
    wgu = s.sb([17, 512], F32, "wgu")
    normw = s.sb([128, 512], F32, "normw")
    glng = s.sb([128, 512], F32, "glng")
    glnb = s.sb([128, 512], F32, "glnb")
    bst = s.sb([128, 4], F32, "bst")
    lnp = [s.sb([128, D], F32, f"ln{i}") for i in range(4)]
    kmT = s.sb([128, 4, 32], BF16, "kmT")
    wsTm = s.sb([128, 4, 128], BF16, "wsTm")

    def load_layer_params(l):
        s.push()
        wsT = s.sb([128, 4, 128], F32, "wsT")
        load("sp", wgu, wgu[:], wgu_all[l])
        load("sp", normw, normw[:], normw_all[l])
        load("sp", glng, glng[:], glng_all[l])
        load("sp", glnb, glnb[:], glnb_all[l])
        load("sp", bst, bst[:], bst_all[l])
        for i in range(4):
            load("sp", lnp[i], lnp[i][:], ln_all[i][l])
        load("sp", wsT, wsT[:], wsT_all[l])
        for g in range(4):
            tt("dve", wsTm, wsTm[:, g, :], wsT, wsT[:, g, :], triu, triu[:], ALU.mult)
        s.pop()

    cosT = s.sb([128, 16, 64], F32, "cos")
    sinT = s.sb([128, 16, 64], F32, "sin")
    s.push()
    posf = s.sb([128, 16], F32, "posf")
    cp("dve", posf, posf[:], posi, posi[:])
    ang = s.sb([128, 16, 64], F32, "ang")
    for t_ in range(16):
        ts("dve", ang, ang[:, t_, :], invf, invf[:], posf[:, t_:t_ + 1], None, ALU.mult, extra_r=[posf])
    kf = s.sb([128, 16, 64], F32, "kf")
    ki = s.sb([128, 16, 64], I32, "ki")
    ts("dve", kf, kf[:], ang, ang[:], 1.0 / TWO_PI, None, ALU.mult)
    cp("dve", ki, ki[:], kf, kf[:])
    cp("dve", kf, kf[:], ki, ki[:])
    C1 = 6.28125
    C2 = TWO_PI - C1
    r0 = s.sb([128, 16, 64], F32, "r0")
    stt("dve", r0, r0[:], kf, kf[:], -C1, ang, ang[:], ALU.mult, ALU.add)
    stt("dve", r0, r0[:], kf, kf[:], -C2, r0, r0[:], ALU.mult, ALU.add)
    m1 = s.sb([128, 16, 64], F32, "m1")
    ts("dve", m1, m1[:], r0, r0[:], math.pi, None, ALU.is_gt)
    stt("dve", r0, r0[:], m1, m1[:], -TWO_PI, r0, r0[:], ALU.mult, ALU.add)
    ts("dve", m1, m1[:], r0, r0[:], -math.pi, None, ALU.is_lt)
    stt("dve", r0, r0[:], m1, m1[:], TWO_PI, r0, r0[:], ALU.mult, ALU.add)
    ts("dve", r0, r0[:], r0, r0[:], math.pi, -math.pi, ALU.min, ALU.max)
    act(sinT, sinT[:], r0, r0[:], AF.Sin)
    stt("dve", m1, m1[:], r0, r0[:], -1.0, r0, r0[:], ALU.mult, ALU.max)
    ts("dve", m1, m1[:], m1, m1[:], -1.0, math.pi / 2, ALU.mult, ALU.add)
    act(cosT, cosT[:], m1, m1[:], AF.Sin)
    s.pop()
    dbg_store("cos", cosT, cosT[:])
    dbg_store("sin", sinT, sinT[:])

    lrT = s.sb([32, 512], F32, "lrT")
    s.op("dve", lambda e: e.memset(lrT[:], 1.0), writes=[lrT])
    Sst = s.sb([128, 4, 128], F32, "S")
    Sbf = s.sb([128, 4, 128], BF16, "Sbf")
    vaug = s.sb([128, 4, 4, 129], BF16, "vaug")
    s.op("dve", lambda e: e.memset(vaug[:], 1.0), writes=[vaug])
    Bacc = s.sb([128, 4], F32, "Bacc")
    kmacc = s.sb([128, 4, 8], F32, "kmacc")

    def init_state_a():
        s.op("dve", lambda e: e.memset(Bacc[:], 0.0), writes=[Bacc])
        s.op("dve", lambda e: e.memset(Sst[:], 0.0), writes=[Sst])

    def init_state_b(cb):
        s.push()
        Sall = s.sb([128, 4, 4, 128], F32, "Sall")
        Ball = s.sb([128, 4, 4], F32, "Ball")
        Ep = s.sb([128, 4, 4], F32, "Ep")
        coef = s.sb([128, 4, 4], F32, "coef")
        f3 = ccf_out[cb].rearrange("(r p) c -> p r c", p=128)
        s.dma("sp", lambda e: e.dma_start(out=Sall[:].rearrange("p r h d -> p r (h d)"), in_=f3[:, :, 32:544]),
              reads=[ccf_out_t[cb]], writes=[Sall])
        s.dma("sp", lambda e: e.dma_start(out=Ball[:], in_=f3[:, :, 544:548]), reads=[ccf_out_t[cb]], writes=[Ball])
        for h in range(4):
            s.dma("pool", lambda e, h=h: e.dma_start(out=kmT[:, h, :].rearrange("p (r b) -> p r b", r=4),
                                                     in_=f3[:, :, h * 8:(h + 1) * 8]),
                  reads=[ccf_out_t[cb]], writes=[kmT])
        s.op("dve", lambda e: e.memset(Ep[:], 0.0), writes=[Ep])
        for p in range(4):
            for r in range(4):
                stt("dve", Ep, Ep[:, p, :], Ball, Ball[:, r, :], bmask[:, p, r:r + 1], Ep, Ep[:, p, :],
                    ALU.mult, ALU.add, extra_r=[bmask])
        act(coef, coef[:], Ep, Ep[:], AF.Exp)
        for p in range(4):
            ts("dve", coef, coef[:, p, :], coef, coef[:, p, :], cinc[:, p:p + 1], None, ALU.mult, extra_r=[cinc])
        s.op("dve", lambda e: e.memset(Sst[:], 0.0), writes=[Sst])
        for p in range(4):
            for h in range(4):
                stt("dve", Sst, Sst[:, h, :], Sall, Sall[:, p, h, :], coef[:, p, h:h + 1], Sst, Sst[:, h, :],
                    ALU.mult, ALU.add, extra_r=[coef])
        s.pop()

    NW = 5
    wpool = [s.sb([128, 8, 512], BF16, f"w{i}") for i in range(NW)]
    w_rr = [0]

    def wtile():
        t = wpool[w_rr[0] % NW]
        w_rr[0] += 1
        return t

    def load_w(src_ap, rows_kc, c0, ncols):
        t = wtile()
        ap_out = t[:, 0:rows_kc, 0:ncols]
        ap_in = src_ap.rearrange("(kc p) c -> p kc c", p=128)[:, :, c0:c0 + ncols]
        s.dma("pool", lambda e: e.dma_start(out=ap_out, in_=ap_in), writes=[t])
        return t

    xgt = [s.sb([128, D], F32, f"xg{i}") for i in range(4)]
    xT = s.sb([128, 8, 512], BF16, "xT")
    brT = [s.sb([128, 4, 512], BF16, f"brT{i}") for i in range(3)]
    ones_f = s.sb([128, 128], F32, "ones_f")
    s.op("dve", lambda e: e.memset(ones_f[:], 1.0), writes=[ones_f])

    def proj_fm(ps, ps_ap, wt, c_lo, M, src=None):
        src = src or xT
        for kc in range(8):
            mm(ps, ps_ap, wt, wt[:, kc, c_lo:c_lo + M], src, src[:, kc, :], start=(kc == 0), stop=(kc == 7))

    def proj_tm(ps, ps_ap, wt, c_lo, N, ti, src=None):
        src = src or xT
        for kc in range(8):
            mm(ps, ps_ap, src, src[:, kc, ti * 128:(ti + 1) * 128], wt, wt[:, kc, c_lo:c_lo + N],
               start=(kc == 0), stop=(kc == 7))

    def transpose_to(src_t, src_fn, dst_t, dst_fn, nblk):
        for j0 in range(0, nblk, 4):
            n = min(4, nblk - j0)
            ps = s.bank()
            for j in range(n):
                tr(ps, ps[:, j * 128:(j + 1) * 128], src_t, src_fn(j0 + j), ident)
            cp_any(dst_t, dst_fn(j0, n), ps, ps[:, 0:n * 128].rearrange("p (a b) -> p a b", a=n))

    ln_pool = [rot([128, 2, 6], F32, "st6", 2), rot([128, 2], F32, "mv", 2), rot([128, 1], F32, "rs", 2)]

    def layer_norm(yt, g_t, b_t):
        st6 = ln_pool[0]()
        mv = ln_pool[1]()
        rs = ln_pool[2]()
        for hf in range(2):
            s.op("dve", lambda e, hf=hf: e.bn_stats(st6[:, hf, :], yt[:, hf * 512:(hf + 1) * 512]),
                 reads=[yt], writes=[st6])
        s.op("dve", lambda e: e.bn_aggr(mv[:], st6[:].rearrange("p a b -> p (a b)")), reads=[st6], writes=[mv])
        act(rs, rs[:], mv, mv[:, 1:2], AF.Sqrt, bias=1e-5)
        s.op("dve", lambda e: e.reciprocal(rs[:], rs[:]), reads=[rs], writes=[rs])
        ts("dve", yt, yt[:], yt, yt[:], mv[:, 0:1], rs[:, 0:1], ALU.subtract, ALU.mult, extra_r=[mv, rs])
        tt("dve", yt, yt[:], yt, yt[:], g_t, g_t[:], ALU.mult)
        tt("dve", yt, yt[:], yt, yt[:], b_t, b_t[:], ALU.add)

    class _Stop(Exception):
        pass

    def chk(tag):
        if stop == tag:
            raise _Stop()

    def run_phase(isB, l):
      cb = l % 2
      w_in_d = w_in_all[l]
      wbr_d = wbr_all[l]
      wout_d = wout_all[l]
      wff1_d = wff1_all[l]
      wff2_d = wff2_all[l]
      x_d = x_in if l == 0 else xbuf[(l - 1) % 2]
      xo_d = out_d if l == NL - 1 else xbuf[l % 2]
      if isB:
          init_state_b(cb)
      else:
          init_state_a()
      cp("act", Sbf, Sbf[:], Sst, Sst[:])
      pend_cc = []
      for g in range(NG):
          for ti in range(4):
              r0_ = g * 512 + ti * 128
              s.dma("sp", lambda e, ti=ti, r0_=r0_: e.dma_start(out=xgt[ti][:], in_=x_d[r0_:r0_ + 128, :]),
                    reads=([] if l == 0 else [xsrc_t[((l - 1) % 2, g, ti)]]), writes=[xgt[ti]])
          for ti in range(4):
              transpose_to(xgt[ti], lambda j, ti=ti: xgt[ti][:, j * 128:(j + 1) * 128],
                           xT, lambda j0, n, ti=ti: xT[:, j0:j0 + n, ti * 128:(ti + 1) * 128], 8)

          chk("xT")
          s.push()
          Eq = s.sb([128, 4, 512], F32, "Eq")
          kinv_tok = s.sb([128, 4, 512], BF16, "kinv_tok")
          v_tok = s.sb([128, 4, 512], BF16, "v_tok")
          if isB:
              Ek = s.sb([128, 4, 512], F32, "Ek")
              wk = load_w(w_in_d, 8, C_GK, 512)
              s.dma("sp", lambda e, g=g: e.dma_start(out=Eq[:].rearrange("p h t -> p (h t)"), in_=glaE_d[g]),
                    reads=[glaE_t[g]], writes=[Eq])
              s.dma("sp", lambda e, g=g: e.dma_start(out=kinv_tok[:].rearrange("p h t -> p (h t)"), in_=glaK_d[g]),
                    reads=[glaK_t[g]], writes=[kinv_tok])
              s.dma("sp", lambda e, g=g: e.dma_start(out=v_tok[:].rearrange("p h t -> p (h t)"), in_=glaV_d[g]),
                    reads=[glaV_t[g]], writes=[v_tok])
              s.op("dve", lambda e: e.reciprocal(Ek[:], Eq[:]), reads=[Eq], writes=[Ek])
          else:
              wlr = load_w(w_in_d, 8, C_LR, 16)
              ps = s.bank()
              proj_fm(ps, ps[0:16, :], wlr, 0, 16)
              cp("dve", lrT, lrT[0:16, :], ps, ps[0:16, :])
              chk("g1")
              wk = load_w(w_in_d, 8, C_GK, 512)
              wv = load_w(w_in_d, 8, C_GV, 512)
              while pend_cc:
                  pend_cc.pop(0)()
              r_sp = rot([128, 512], F32, "sp", 2)
              r_ekt = rot([128, 512], F32, "ekt", 2)
              for ti in range(4):
                  sp_t = r_sp()
                  ps = s.bank()
                  mm(ps, ps[:], lrT, lrT[0:17, ti * 128:(ti + 1) * 128], wgu, wgu[0:17, :])
                  act(sp_t, sp_t[:], ps, ps[:], AF.Exp, scale=-1.0)
                  act(sp_t, sp_t[:], sp_t, sp_t[:], AF.Ln, bias=1.0)
                  chk("g2")
                  psb = s.bank()
                  for h in range(4):
                      mm(psb, psb[:, h * 128:(h + 1) * 128], sp_t, sp_t[:, h * 128:(h + 1) * 128], tri, tri[:])
                  pv = psb[:].rearrange("p (a b) -> p a b", a=4)
                  chk("g2b")
                  act(Eq, Eq[:, :, ti * 128:(ti + 1) * 128], psb, pv, AF.Exp)
                  chk("g2c")
                  if isB:
                      act(Ek, Ek[:, :, ti * 128:(ti + 1) * 128], psb, pv, AF.Exp, scale=-1.0)
                  else:
                      for cc in (63, 127):
                          tt("dve", Bacc, Bacc[:], Bacc, Bacc[:], psb, pv[:, :, cc], ALU.add)
                  chk("g3")
                  pst = s.bank()
                  mm(pst, pst[:], tri, tri[:], sp_t, sp_t[:])
                  ekt = r_ekt()
                  act(ekt, ekt[:], pst, pst[:], AF.Exp, scale=-1.0)
                  psk = s.bank()
                  proj_tm(psk, psk[:], wk, 0, 512, ti)
                  tt("dve", kinv_tok, kinv_tok[:, ti, :], psk, psk[:], ekt, ekt[:], ALU.mult)
                  psv = s.bank()
                  proj_tm(psv, psv[:], wv, 0, 512, ti)
                  cp("act", v_tok, v_tok[:, ti, :], psv, psv[:])
              s.dma("sp", lambda e, g=g: e.dma_start(out=glaE_d[g], in_=Eq[:].rearrange("p h t -> p (h t)")),
                    reads=[Eq], writes=[glaE_t[g]])
              s.dma("sp", lambda e, g=g: e.dma_start(out=glaK_d[g], in_=kinv_tok[:].rearrange("p h t -> p (h t)")),
                    reads=[kinv_tok], writes=[glaK_t[g]])
              s.dma("sp", lambda e, g=g: e.dma_start(out=glaV_d[g], in_=v_tok[:].rearrange("p h t -> p (h t)")),
                    reads=[v_tok], writes=[glaV_t[g]])
          if isB:
              wq = load_w(w_in_d, 8, C_GQ, 512)
              qdec = s.sb([128, 4, 512], BF16, "qdec")
              qlo = s.sb([128, 4, 512], BF16, "qlo")
              qhi = s.sb([128, 4, 512], BF16, "qhi")
              kinvT = s.sb([128, 4, 512], BF16, "kinvT")
              for h in range(4):
                  ps = s.bank()
                  proj_fm(ps, ps[:], wq, h * 128, 128)
                  tt("dve", qdec, qdec[:, h, :], ps, ps[:], Eq, Eq[:, h, :], ALU.mult)
                  tt("dve", qlo, qlo[:, h, :], qdec, qdec[:, h, :], lomask, lomask[:], ALU.mult)
                  tt("dve", qhi, qhi[:, h, :], qdec, qdec[:, h, :], himask, himask[:], ALU.mult)
                  ps = s.bank()
                  proj_fm(ps, ps[:], wk, h * 128, 128)
                  tt("dve", kinvT, kinvT[:, h, :], ps, ps[:], Ek, Ek[:, h, :], ALU.mult)
              wgg = load_w(w_in_d, 8, C_GG, 512)
              GW = s.sb([128, 4, 512], F32, "GW")
              for ti in range(4):
                  ps = s.bank()
                  proj_tm(ps, ps[:], wgg, 0, 512, ti)
                  act(GW, GW[:, ti, :], ps, ps[:], AF.Silu)
                  tt("dve", GW, GW[:, ti, :], GW, GW[:, ti, :], normw, normw[:], ALU.mult)
          chk("g4")
          r_tmpS = rot([128, 128], F32, "tmpS", 8)
          if isB:
              r_attn = rot([128, 128], BF16, "attn", 8)
              r_junk = rot([128, 128], F32, "junk", 2)
              r_ssq = rot([128, 1], F32, "ssq", 8)
              r_gout = rot([128, 512], F32, "gout", 2)
          HS = [slice(h * 128, (h + 1) * 128) for h in range(4)]
          pso_b = [s.psum_banks[h] for h in range(4)]
          pkv_b = [s.psum_banks[4 + h] for h in range(4)]

          def state_update(ti, half):
              rs_ = slice(half * 64, half * 64 + 64)
              cc = ti * 128 + half * 64 + 63
              for h in range(4):
                  mm(pkv_b[h], pkv_b[h][:, 0:128], kinv_tok, kinv_tok[rs_, ti, HS[h]], v_tok, v_tok[rs_, ti, HS[h]])
              for h in range(4):
                  tmpS = r_tmpS()
                  tt("dve", tmpS, tmpS[:], pkv_b[h], pkv_b[h][:, 0:128], Sst, Sst[:, h, :], ALU.add)
                  ts("dve", Sst, Sst[:, h, :], tmpS, tmpS[:], Eq[:, h, cc:cc + 1], None, ALU.mult, extra_r=[Eq])
                  if isB:
                      cp("act", Sbf, Sbf[:, h, :], Sst, Sst[:, h, :])

          pend_g = []
          rec_steps = []
          for ti in range(4):
              tsl = slice(ti * 128, (ti + 1) * 128)
              if isB:
                  gout = r_gout()
                  attns = []
                  for h in range(4):
                      mm(pkv_b[h], pkv_b[h][:, 0:128], kinvT, kinvT[:, h, tsl], qdec, qdec[:, h, tsl])
                  for h in range(4):
                      attn = r_attn()
                      attns.append(attn)
                      tt("dve", attn, attn[:], pkv_b[h], pkv_b[h][:, 0:128], gmask, gmask[:], ALU.mult)
                  for h in range(4):
                      mm(pso_b[h], pso_b[h][:, 0:128], attns[h], attns[h][:], v_tok, v_tok[:, ti, HS[h]],
                         start=True, stop=False)
                      mm(pso_b[h], pso_b[h][:, 0:128], qlo, qlo[:, h, tsl], Sbf, Sbf[:, h, :], start=False, stop=False)
              if isB:
                  state_update(ti, 0)
              else:
                  rec_steps.append(lambda ti=ti: state_update(ti, 0))
              if isB:
                  for h in range(4):
                      mm(pso_b[h], pso_b[h][:, 0:128], qhi, qhi[:, h, tsl], Sbf, Sbf[:, h, :], start=False, stop=True)
              if isB:
                  state_update(ti, 1)
              else:
                  rec_steps.append(lambda ti=ti: state_update(ti, 1))
              if isB:
                  for h in range(4):
                      junk = r_junk()
                      ssq = r_ssq()
                      act(junk, junk[:], pso_b[h], pso_b[h][:, 0:128], AF.Square, accum=(ssq, ssq[:]))
                      act(ssq, ssq[:], ssq, ssq[:], AF.Sqrt, bias=1.28e-4, scale=1.0 / 128)
                      s.op("dve", lambda e, ssq=ssq: e.reciprocal(ssq[:], ssq[:]), reads=[ssq], writes=[ssq])
                      stt("dve", gout, gout[:, HS[h]], pso_b[h], pso_b[h][:, 0:128], ssq[:, 0:1], GW, GW[:, ti, HS[h]],
                          ALU.mult, ALU.mult, extra_r=[ssq])
                  if g == 0 and ti == 0:
                      dbg_store("gla", gout, gout[:])
                  if pend_g:
                      pend_g.pop()()
                  pend_g.append(lambda gout=gout, ti=ti: transpose_to(
                      gout, lambda j: gout[:, j * 128:(j + 1) * 128],
                      brT[0], lambda j0, n: brT[0][:, j0:j0 + n, ti * 128:(ti + 1) * 128], 4))
          if isB:
              pend_g.pop()()
          if isB:
              s.pop()

          chk("gla")
          if isB:
              s.push()
              wzu = load_w(w_in_d, 8, C_GZ, 512)
              wzv = load_w(w_in_d, 8, C_GZ + 512, 512)
              r_u = rot([128, 512], F32, "u", 2)
              r_vg = rot([128, 512], F32, "vg", 2)
              r_vln = rot([128, 512], BF16, "vln", 2)
              r_gm = rot([128, 512], F32, "gm", 2)
              r_st6 = rot([128, 6], F32, "gst6", 2)
              r_mv = rot([128, 2], F32, "gmv", 2)
              r_rs = rot([128, 1], F32, "grs", 2)
              gst = {}

              def gm_s0(ti):
                  u_t = r_u()
                  vg = r_vg()
                  ps = s.bank()
                  proj_tm(ps, ps[:], wzu, 0, 512, ti)
                  act(u_t, u_t[:], ps, ps[:], AF.Gelu)
                  ps = s.bank()
                  proj_tm(ps, ps[:], wzv, 0, 512, ti)
                  act(vg, vg[:], ps, ps[:], AF.Gelu)
                  st6 = r_st6()
                  mv = r_mv()
                  rs = r_rs()
                  s.op("dve", lambda e, st6=st6, vg=vg: e.bn_stats(st6[:], vg[:]), reads=[vg], writes=[st6])
                  s.op("dve", lambda e, st6=st6, mv=mv: e.bn_aggr(mv[:], st6[:]), reads=[st6], writes=[mv])
                  act(rs, rs[:], mv, mv[:, 1:2], AF.Sqrt, bias=1e-5)
                  s.op("dve", lambda e, rs=rs: e.reciprocal(rs[:], rs[:]), reads=[rs], writes=[rs])
                  ts("dve", vg, vg[:], vg, vg[:], mv[:, 0:1], rs[:, 0:1], ALU.subtract, ALU.mult, extra_r=[mv, rs])
                  tt("dve", vg, vg[:], vg, vg[:], glng, glng[:], ALU.mult)
                  vln = r_vln()
                  tt("dve", vln, vln[:], vg, vg[:], glnb, glnb[:], ALU.add)
                  gst[ti] = (u_t, vln)

              def gm_s1(ti):
                  u_t, vln = gst[ti]
                  ps = s.bank()
                  for h in range(4):
                      hs = slice(h * 128, (h + 1) * 128)
                      mm(ps, ps[:, hs], wsTm, wsTm[:, h, :], vln, vln[:, hs])
                  gm = r_gm()
                  for h in range(4):
                      hs = slice(h * 128, (h + 1) * 128)
                      stt("dve", gm, gm[:, hs], ps, ps[:, hs], bst[:, h:h + 1], u_t, u_t[:, hs], ALU.add, ALU.mult,
                          extra_r=[bst])
                  if g == 0 and ti == 0:
                      dbg_store("gmlp", gm, gm[:])
                  gst[ti] = gm

              def gm_s2(ti):
                  gm = gst[ti]
                  transpose_to(gm, lambda j, gm=gm: gm[:, j * 128:(j + 1) * 128],
                               brT[1], lambda j0, n, ti=ti: brT[1][:, j0:j0 + n, ti * 128:(ti + 1) * 128], 4)

              pipeline([gm_s0, gm_s1, gm_s2])
              s.pop()

          chk("gmlp")
          s.push()
          krT = s.sb([128, 4, 512], BF16, "krT")
          if not isB:
              wmk = load_w(w_in_d, 8, C_MK, 512)
              wmv = load_w(w_in_d, 8, C_MV, 512)
          else:
              KT_i = ccb_in[cb][g][:, 0:2048].rearrange("p (h t) -> p h t", h=4)
              V_i = ccb_in[cb][g][:, 2048:4096].rearrange("p (h t c) -> p h t c", h=4, t=4)
              s.dma("sp", lambda e, KT_i=KT_i: e.dma_start(out=krT[:], in_=KT_i),
                    reads=[ccb_in_t[cb][g]], writes=[krT])
              for ti in range(4):
                  s.dma("sp", lambda e, ti=ti, V_i=V_i: e.dma_start(
                      out=vaug[:, ti, :, 0:128], in_=V_i[:, :, ti, :]),
                      reads=[ccb_in_t[cb][g]], writes=[vaug])
          if isB:
              wmq = load_w(w_in_d, 8, C_MQ, 512)
              qrT = s.sb([128, 4, 512], BF16, "qrT")
              nbT = s.sb([128, 4, 512], BF16, "nbT")
              s.op("dve", lambda e, nbT=nbT: e.memset(nbT[:], 0.0), writes=[nbT])

          r_t1 = rot([128, 4, 16], F32, "t1", 4)
          r_t2 = rot([128, 4, 16], F32, "t2", 4)
          r_kr = rot([128, 512], F32, "kr", 2)
          if isB:
              r_qr = rot([128, 512], F32, "qr", 2)
              r_gsc = rot([128, 4, 32], F32, "gsc", 2)
              r_nb = rot([128, 4, 32], F32, "nb", 2)
              r_mx8 = rot([128, 8], F32, "mx8", 4)
              r_km2 = None
          else:
              r_km2 = rot([128, 4], F32, "km2", 2)

          def rope_tm(ps, dst, gti):
              cp("act", dst, dst[:], ps, ps[:])
              pv = ps[:].rearrange("p (h d) -> p h d", h=4)
              dv = dst[:].rearrange("p (h d) -> p h d", h=4)
              cs = cosT[:, gti, :].rearrange("p (h d) -> p h d", h=4)
              sn = sinT[:, gti, :].rearrange("p (h d) -> p h d", h=4)
              t1 = r_t1()
              t2 = r_t2()
              tt("dve", t1, t1[:], ps, pv[:, :, 0:16], cosT, cs, ALU.mult)
              tt("dve", t2, t2[:], ps, pv[:, :, 16:32], sinT, sn, ALU.mult)
              tt("dve", dst, dv[:, :, 0:16], t1, t1[:], t2, t2[:], ALU.subtract)
              t1 = r_t1()
              t2 = r_t2()
              tt("dve", t1, t1[:], ps, pv[:, :, 16:32], cosT, cs, ALU.mult)
              tt("dve", t2, t2[:], ps, pv[:, :, 0:16], sinT, sn, ALU.mult)
              tt("dve", dst, dv[:, :, 16:32], t1, t1[:], t2, t2[:], ALU.add)

          mst = {}

          def mb_s0(ti):
              gti = g * 4 + ti
              kr = None
              if not isB:
                  kr = r_kr()
                  ps = s.bank()
                  proj_tm(ps, ps[:], wmk, 0, 512, ti)
                  rope_tm(ps, kr, gti)
                  ps = s.bank()
                  proj_tm(ps, ps[:], wmv, 0, 512, ti)
                  cp("act", vaug, vaug[:, ti, :, 0:128], ps, ps[:].rearrange("p (h d) -> p h d", h=4))
              qr = None
              if isB:
                  qr = r_qr()
                  ps = s.bank()
                  proj_tm(ps, ps[:], wmq, 0, 512, ti)
                  rope_tm(ps, qr, gti)
              mst[ti] = (kr, qr)

          def mb_s1(ti):
              gti = g * 4 + ti
              tsl = slice(ti * 128, (ti + 1) * 128)
              kr, qr = mst[ti]
              if not isB:
                  pst = s.bank()
                  for h in range(4):
                      tr(pst, pst[:, h * 128:(h + 1) * 128], kr, kr[:, h * 128:(h + 1) * 128], ident)
                  pvw = pst[:].rearrange("p (a b) -> p a b", a=4)
                  cp("act", krT, krT[:, :, tsl], pst, pvw)
              if not isB:
                  blk = gti // 2
                  if gti % 2 == 0:
                      s.op("dve", lambda e, pvw=pvw, blk=blk: e.tensor_reduce(kmacc[:, :, blk], pvw, AX.X, ALU.add),
                           reads=[pst], writes=[kmacc])
                  else:
                      km2 = r_km2()
                      s.op("dve", lambda e, pvw=pvw, km2=km2: e.tensor_reduce(km2[:], pvw, AX.X, ALU.add),
                           reads=[pst], writes=[km2])
                      tt("dve", kmacc, kmacc[:, :, blk], kmacc, kmacc[:, :, blk], km2, km2[:], ALU.add)
              else:
                  pst = s.bank()
                  for h in range(4):
                      tr(pst, pst[:, h * 128:(h + 1) * 128], qr, qr[:, h * 128:(h + 1) * 128], ident)
                  cp("dve", qrT, qrT[:, :, tsl], pst, pst[:].rearrange("p (a b) -> p a b", a=4))
                  psg = s.bank()
                  for h in range(4):
                      mm(psg, psg[:, h * 32:(h + 1) * 32], qrT, qrT[:, h, tsl], kmT, kmT[:, h, :])
                  gsc = r_gsc()
                  nb = r_nb()
                  for h in range(4):
                      mx8 = r_mx8()
                      tt("dve", gsc, gsc[:, h, :], psg, psg[:, h * 32:(h + 1) * 32], valid, valid[:, gti, :], ALU.mult)
                      tt("dve", gsc, gsc[:, h, :], gsc, gsc[:, h, :], negfill, negfill[:, gti, :], ALU.add)
                      s.op("dve", lambda e, h=h, gsc=gsc, mx8=mx8: e.max(mx8[:], gsc[:, h, :]), reads=[gsc], writes=[mx8])
                      ts("dve", nb, nb[:, h, :], gsc, gsc[:, h, :], mx8[:, 2:3], None, ALU.is_ge, extra_r=[mx8])
                      tt("dve", nb, nb[:, h, :], nb, nb[:, h, :], valid, valid[:, gti, :], ALU.mult)
                  ts("dve", nb, nb[:], nb, nb[:], 1.0, -NEG, ALU.subtract, ALU.mult)
                  if g == 0 and ti == 0:
                      dbg_store("nb", nb, nb[:])
                  mst[ti] = nb

          def mb_s2(ti):
              if not isB:
                  return
              tsl = slice(ti * 128, (ti + 1) * 128)
              nb = mst[ti]
              psn = s.bank()
              for h in range(4):
                  tr(psn, psn[0:32, h * 128:(h + 1) * 128], nb, nb[:, h, :], ident)
              cp("act", nbT, nbT[0:32, :, tsl], psn, psn[0:32, :].rearrange("p (a b) -> p a b", a=4))

          def rec_between(step):
              for _ in range(2 if step < 2 else 1):
                  if rec_steps:
                      rec_steps.pop(0)()

          pipeline([mb_s0, mb_s1, mb_s2], between=(None if isB else rec_between))
          while rec_steps:
              rec_steps.pop(0)()
          if not isB:
              KT_o = ccb_in[cb][g][:, 0:2048].rearrange("p (h t) -> p h t", h=4)
              V_o = ccb_in[cb][g][:, 2048:4096].rearrange("p (h t c) -> p h t c", h=4, t=4)
              s.dma("sp", lambda e, KT_o=KT_o: e.dma_start(out=KT_o, in_=krT[:]),
                    reads=[krT], writes=[ccb_in_t[cb][g]])
              for ti in range(4):
                  s.dma("sp", lambda e, ti=ti, V_o=V_o: e.dma_start(
                      out=V_o[:, :, ti, :], in_=vaug[:, ti, :, 0:128]),
                      reads=[vaug], writes=[ccb_in_t[cb][g]])
              pend_cc.append(lambda g=g: s.cc(
                  lambda e: e.collective_compute("AllGather", ALU.bypass, replica_groups=[[0, 1, 2, 3], [4, 5, 6, 7]],
                                                 ins=[ccb_in[cb][g].opt()], outs=[ccb_out[cb][g].opt()]),
                  reads=[ccb_in_t[cb][g]], writes=[ccb_out_t[cb][g]]))
          else:
              SCALE = 128.0 ** -0.5
              p_tiles = [s.sb([128, 512], BF16, f"p_sb{i}") for i in range(5)]
              pacc = s.sb([128, 512], F32, "pacc")
              rl = s.sb([128, 512], F32, "rl")
              NSB = 6
              KTc = [s.sb([128, 2048], BF16, f"KTc{r_}") for r_ in range(4)]
              Vc = [s.sb([128, 16, 129], BF16, f"Vc{r_}") for r_ in range(4)]
              for r_ in range(4):
                  s.op("dve", lambda e, r_=r_: e.memset(Vc[r_][:, :, 128:129], 1.0), writes=[Vc[r_]])
              for h in range(4):
                  for r_ in range(4):
                      if r_ * 16 >= 2 * min(32, 26 + 2 * g):
                          continue
                      for gg in range(4):
                          src = ccb_out[cb][gg][r_ * 128:(r_ + 1) * 128, :]
                          s.dma("sp", lambda e, h=h, src=src, r_=r_, gg=gg: e.dma_start(
                              out=KTc[r_][:, gg * 512:(gg + 1) * 512], in_=src[:, h * 512:(h + 1) * 512]),
                              reads=[ccb_out_t[cb][gg]], writes=[KTc[r_]])
                          s.dma("sp", lambda e, h=h, src=src, r_=r_, gg=gg: e.dma_start(
                              out=Vc[r_][:, gg * 4:(gg + 1) * 4, 0:128],
                              in_=src[:, 2048 + h * 512:2048 + (h + 1) * 512].rearrange("p (t c) -> p t c", t=4)),
                              reads=[ccb_out_t[cb][gg]], writes=[Vc[r_]])
                  oT = s.psum_banks[7]
                  descs = [("g", kt) for kt in range(2 * min(32, 26 + 2 * g))] + [("o", lb, j) for lb in range(2) for j in range(2)]

                  def scores(idx, d):
                      pss = s.psum_banks[idx % NSB]
                      p_sb = p_tiles[idx % 5]
                      if d[0] == "g":
                          kt = d[1]
                          n = kt // 2
                          KTh = KTc[kt // 16]
                          mm(pss, pss[:], KTh, KTh[:, (kt % 16) * 128:(kt % 16 + 1) * 128], qrT, qrT[:, h, :],
                             start=True, stop=False)
                          mm(pss, pss[:], identb, identb[:, n:n + 1].to_broadcast([128, 128]), nbT, nbT[:, h, :],
                             start=False, stop=True)
                          act(p_sb, p_sb[:], pss, pss[:], AF.Exp, scale=SCALE)
                      else:
                          _, lb, j = d
                          tk = 2 * lb + j
                          mm(pss, pss[:, 0:256], krT, krT[:, h, tk * 128:(tk + 1) * 128], qrT,
                             qrT[:, h, lb * 256:(lb + 1) * 256], start=True, stop=False)
                          mm(pss, pss[:, 0:256], identb, identb[:], cmask, cmask[:, j, :], start=False, stop=True)
                          act(p_sb, p_sb[:, 0:256], pss, pss[:, 0:256], AF.Exp, scale=SCALE)
                      return p_sb

                  def pv(idx, d, p_sb):
                      if d[0] == "g":
                          kt = d[1]
                          mm(oT, oT[:], Vc[kt // 16], Vc[kt // 16][:, kt % 16, 0:128], p_sb, p_sb[:],
                             start=(kt == 0), stop=False)
                          if idx == 0:
                              cp("dve", pacc, pacc[:], p_sb, p_sb[:])
                          else:
                              tt("dve", pacc, pacc[:], pacc, pacc[:], p_sb, p_sb[:], ALU.add)
                      else:
                          _, lb, j = d
                          tk = 2 * lb + j
                          qs = slice(lb * 256, (lb + 1) * 256)
                          mm(oT, oT[:, qs], vaug, vaug[:, tk, h, 0:128], p_sb, p_sb[:, 0:256],
                             start=False, stop=(lb == 1 and j == 1))
                          tt("dve", pacc, pacc[:, qs], pacc, pacc[:, qs], p_sb, p_sb[:, 0:256], ALU.add)

                  LOOK = 2
                  pend = []
                  for idx, d in enumerate(descs):
                      p_cur = scores(idx, d)
                      pend.append((idx, d, p_cur))
                      if len(pend) > LOOK:
                          pv(*pend.pop(0))
                  while pend:
                      pv(*pend.pop(0))
                  psl = s.psum_banks[6]
                  mm(psl, psl[:], ones_f, ones_f[:], pacc, pacc[:])
                  s.op("dve", lambda e: e.reciprocal(rl[:], psl[:]), reads=[psl], writes=[rl])
                  tt("dve", brT[2], brT[2][:, h, :], oT, oT[:], rl, rl[:], ALU.mult)
          s.pop()
          if not isB:
              s.pop()
              continue

          chk("moba")
          s.push()
          macc = s.sb([128, 4, D], F32, "macc")
          r_sig = rot([128, 512], F32, "sig", 3)
          for br in range(3):
              wb0 = load_w(wbr_d[br], 4, 0, 512)
              wb1 = load_w(wbr_d[br], 4, 512, 512)
              wg0 = load_w(w_in_d, 8, C_GL + br * 1024, 512)
              wg1 = load_w(w_in_d, 8, C_GL + br * 1024 + 512, 512)
              for ti in range(4):
                  tsl = slice(ti * 128, (ti + 1) * 128)
                  for hf, (wb, wgt) in enumerate(((wb0, wg0), (wb1, wg1))):
                      cs_ = slice(hf * 512, (hf + 1) * 512)
                      psa = s.bank()
                      for kc in range(4):
                          mm(psa, psa[:], brT[br], brT[br][:, kc, tsl], wb, wb[:, kc, 0:512], start=(kc == 0), stop=(kc == 3))
                      psg = s.bank()
                      proj_tm(psg, psg[:], wgt, 0, 512, ti)
                      sig = r_sig()
                      act(sig, sig[:], psg, psg[:], AF.Sigmoid)
                      if br == 0:
                          tt("dve", macc, macc[:, ti, cs_], psa, psa[:], sig, sig[:], ALU.mult)
                      else:
                          tt("dve", sig, sig[:], psa, psa[:], sig, sig[:], ALU.mult)
                          tt("dve", macc, macc[:, ti, cs_], macc, macc[:, ti, cs_], sig, sig[:], ALU.add)
          mT = s.sb([128, 8, 512], BF16, "mT")
          for ti in range(4):
              transpose_to(macc, lambda j, ti=ti: macc[:, ti, j * 128:(j + 1) * 128],
                           mT, lambda j0, n, ti=ti: mT[:, j0:j0 + n, ti * 128:(ti + 1) * 128], 8)
          wo = [load_w(wout_d, 8, 0, 512), load_w(wout_d, 8, 512, 512)]
          for ti in range(4):
              for hf in range(2):
                  cs_ = slice(hf * 512, (hf + 1) * 512)
                  ps = s.bank()
                  proj_tm(ps, ps[:], wo[hf], 0, 512, ti, src=mT)
                  stt("dve", xgt[ti], xgt[ti][:, cs_], xgt[ti], xgt[ti][:, cs_], ALPHA, ps, ps[:], ALU.mult, ALU.add)
              layer_norm(xgt[ti], lnp[0], lnp[1])
          s.pop()
          if g == 0:
              dbg_store("x1", xgt[0], xgt[0][:])

          chk("merge")
          s.push()
          for ti in range(4):
              transpose_to(xgt[ti], lambda j, ti=ti: xgt[ti][:, j * 128:(j + 1) * 128],
                           xT, lambda j0, n, ti=ti: xT[:, j0:j0 + n, ti * 128:(ti + 1) * 128], 8)
          hT = s.sb([128, 32, 512], BF16, "hT")
          r_hsq = rot([128, 512], F32, "hsq", 3)
          for fcc in range(8):
              w1 = load_w(wff1_d, 8, fcc * 512, 512)
              for j in range(4):
                  fc = fcc * 4 + j
                  ps = s.bank()
                  proj_fm(ps, ps[:], w1, j * 128, 128)
                  hsq = r_hsq()
                  act(hsq, hsq[:], ps, ps[:], AF.Square)
                  stt("dve", hT, hT[:, fc, :], ps, ps[:], 0.0, hsq, hsq[:], ALU.is_gt, ALU.mult)
          w2c = [s.sb([128, 8, 512], BF16, f"w2c{q4}") for q4 in range(4)]
          for hf in range(2):
              cs_ = slice(hf * 512, (hf + 1) * 512)
              for q4 in range(4):
                  s.dma("pool", lambda e, hf=hf, q4=q4: e.dma_start(
                      out=w2c[q4][:],
                      in_=wff2_d.rearrange("(fc p) c -> p fc c", p=128)[:, q4 * 8:(q4 + 1) * 8, hf * 512:(hf + 1) * 512]),
                      writes=[w2c[q4]])
              facc = [s.psum_banks[4 + ti] for ti in range(4)]
              for q4 in range(4):
                  for ti in range(4):
                      for f8 in range(8):
                          fc = q4 * 8 + f8
                          mm(facc[ti], facc[ti][:], hT, hT[:, fc, ti * 128:(ti + 1) * 128], w2c[q4], w2c[q4][:, f8, :],
                             start=(fc == 0), stop=(fc == 31))
              for ti in range(4):
                  stt("dve", xgt[ti], xgt[ti][:, cs_], xgt[ti], xgt[ti][:, cs_], ALPHA, facc[ti], facc[ti][:],
                      ALU.mult, ALU.add)
          for ti in range(4):
              layer_norm(xgt[ti], lnp[2], lnp[3])
              r0_ = g * 512 + ti * 128
              s.dma("sp", lambda e, ti=ti, r0_=r0_: e.dma_start(out=xo_d[r0_:r0_ + 128, :], in_=xgt[ti][:]),
                    reads=[xgt[ti]], writes=([] if l == NL - 1 else [xsrc_t[(l % 2, g, ti)]]))
          s.pop()

      if not isB:
          ts("dve", kmacc, kmacc[:], kmacc, kmacc[:], 1.0 / 256, None, ALU.mult)
          s.dma("sp", lambda e: e.dma_start(out=ccf_in[cb][:, 0:32].rearrange("p (h b) -> p h b", h=4), in_=kmacc[:]),
                reads=[kmacc], writes=[ccf_in_t[cb]])
          s.dma("sp", lambda e: e.dma_start(out=ccf_in[cb][:, 32:544].rearrange("p (h d) -> p h d", h=4), in_=Sst[:]),
                reads=[Sst], writes=[ccf_in_t[cb]])
          s.dma("sp", lambda e: e.dma_start(out=ccf_in[cb][:, 544:548], in_=Bacc[:]), reads=[Bacc], writes=[ccf_in_t[cb]])
          GRP = [[0, 1, 2, 3], [4, 5, 6, 7]]
          while pend_cc:
              pend_cc.pop(0)()
          s.cc(lambda e: e.collective_compute("AllGather", ALU.bypass, replica_groups=GRP,
                                              ins=[ccf_in[cb].opt()], outs=[ccf_out[cb].opt()]),
               reads=[ccf_in_t[cb]], writes=[ccf_out_t[cb]])

    try:
        chk("setup")
        for l in range(NL):
            load_layer_params(l)
            run_phase(False, l)
            chk(f"A{l}")
            run_phase(True, l)
            chk(f"B{l}")
    except _Stop:
        pass
    s.wait_all("sp")
    s.emit()
    if dbg is not None:
        print("sb_peak", s.sb_peak, "counts", s.count, "dma", s.dma_n, flush=True)
    return nc


_PROG = {}


def _prog(nl):
    if nl not in _PROG:
        _PROG[nl] = build(nl)
    return _PROG[nl]


def _consts():
    c = {}
    c["ident"] = np.eye(128, dtype=np.float32)
    s_ = np.arange(128)[:, None]
    t_ = np.arange(128)[None, :]
    same = (s_ // 64) == (t_ // 64)
    c["tri"] = np.where((s_ <= t_) & same, -1.0 / 16.0, 0.0).astype(np.float32)
    c["gmask"] = np.where((s_ <= t_) & same, 1.0, 0.0).astype(np.float32)
    c["triu"] = np.where(s_ <= t_, 1.0, 0.0).astype(np.float32)
    tcol = np.arange(512)[None, :] % 128
    c["lomask"] = np.broadcast_to(np.where(tcol < 64, 1.0, 0.0), (128, 512)).astype(ml_dtypes.bfloat16)
    c["himask"] = np.broadcast_to(np.where(tcol >= 64, 1.0, 0.0), (128, 512)).astype(ml_dtypes.bfloat16)
    cm = np.zeros((128, 2, 256), np.float32)
    for j in range(2):
        ks = j * 128 + np.arange(128)[:, None]
        cm[:, j, :] = np.where(ks <= np.arange(256)[None, :], 0.0, NEG)
    c["cmask"] = cm.astype(ml_dtypes.bfloat16)
    c["identb"] = np.eye(128, dtype=np.float32).astype(ml_dtypes.bfloat16)
    half = 16
    inv = (1.0 / (np.float32(500000.0) ** (np.arange(half, dtype=np.float32) * np.float32(2.0 / 32)))).astype(np.float32)
    c["invf"] = np.broadcast_to(np.tile(inv, 4)[None, :], (128, 64)).astype(np.float32).copy()
    return c


def _core_consts(q):
    valid = np.zeros((128, 16, 32), np.float32)
    for ti in range(16):
        own = q * 8 + ti // 2
        valid[:, ti, :own] = 1.0
    negfill = ((valid - 1.0) * 1e30).astype(np.float32)
    cinc = np.zeros((128, 4), np.float32)
    cinc[:, :q] = 1.0
    bmask = np.zeros((128, 4, 4), np.float32)
    for p in range(4):
        for r in range(4):
            if p < r < q:
                bmask[:, p, r] = 1.0
    return {"valid": valid, "negfill": negfill, "cinc": cinc, "bmask": bmask}


def _bcl(v, n=128):
    v = np.asarray(v, np.float32)
    return np.ascontiguousarray(np.broadcast_to(v[:, None, :], (v.shape[0], n, v.shape[1])))


def shard_inputs(x, positions):
    x = np.ascontiguousarray(np.asarray(x, dtype=np.float32))
    positions = np.asarray(positions).astype(np.int32)
    xs = [np.ascontiguousarray(x[c // 4, (c % 4) * TOK:(c % 4 + 1) * TOK, :]) for c in range(NCORE)]
    posi = [np.ascontiguousarray(positions[c // 4, (c % 4) * TOK:(c % 4 + 1) * TOK].reshape(16, 128).T)
            for c in range(NCORE)]
    return xs, posi


def make_in_maps(x, positions, P, nl=DEPTH):
    f = lambda a: np.ascontiguousarray(np.asarray(a, dtype=np.float32)[:nl])
    cst = _consts()
    xs, posi = shard_inputs(x, positions)
    shared = dict(cst)
    shared.update({
        "w_in": f(P["w_in"]),
        "wgu": np.ascontiguousarray(np.concatenate([f(P["w_gate_up"]), f(P["b_gate"])[:, None, :]], axis=1)),
        "normw": _bcl(np.tile(f(P["gla_norm_w"]), (1, 4))),
        "glng": _bcl(f(P["gmlp_ln_g"])), "glnb": _bcl(f(P["gmlp_ln_b"])),
        "wsT": np.ascontiguousarray(f(P["gmlp_w_s"]).transpose(0, 3, 1, 2)),
        "bst": np.ascontiguousarray(f(P["gmlp_b_s"]).transpose(0, 2, 1)),
        "wbr": f(P["w_branch"]), "wout": f(P["w_out"]),
        "ln1g": _bcl(f(P["ln1_g"])), "ln1b": _bcl(f(P["ln1_b"])),
        "ln2g": _bcl(f(P["ln2_g"])), "ln2b": _bcl(f(P["ln2_b"])),
        "wff1": f(P["w_ff1"]), "wff2": f(P["w_ff2"]),
    })
    in_maps = []
    for c in range(NCORE):
        m = dict(shared, x=xs[c], posi=posi[c])
        m.update(_core_consts(c % 4))
        in_maps.append(m)
    return in_maps


def kernel(x, positions, w_in, w_gate_up, b_gate, gla_norm_w, gmlp_ln_g, gmlp_ln_b, gmlp_w_s, gmlp_b_s,
           w_branch, w_out, ln1_g, ln1_b, w_ff1, w_ff2, ln2_g, ln2_b):
    P = dict(w_in=w_in, w_gate_up=w_gate_up, b_gate=b_gate, gla_norm_w=gla_norm_w, gmlp_ln_g=gmlp_ln_g,
             gmlp_ln_b=gmlp_ln_b, gmlp_w_s=gmlp_w_s, gmlp_b_s=gmlp_b_s, w_branch=w_branch, w_out=w_out,
             ln1_g=ln1_g, ln1_b=ln1_b, w_ff1=w_ff1, w_ff2=w_ff2, ln2_g=ln2_g, ln2_b=ln2_b)
    in_maps = make_in_maps(x, positions, P, DEPTH)
    res = run_bass_kernel_spmd(_prog(DEPTH), in_maps, core_ids=list(range(NCORE))).results
    out = np.zeros((BATCH, SEQ, D), np.float32)
    for c in range(NCORE):
        out[c // 4, (c % 4) * TOK:(c % 4 + 1) * TOK, :] = np.asarray(res[c]["xout"], np.float32)
    return out
```
